# Optimizing a Trainium2 kernel written in Bass

```python
import jax
import jax.numpy as jnp
from jax import lax
import numpy as np

D_MODEL = 1024
BATCH = 16
SEQ = 2048
DEPTH = 2

GRID_W = 64
CTX_LEN = 256
N_BRANCH = 4
BRANCH_WIDTH = D_MODEL // 2
CONV_WIDTH = 4
CONV_PAD = (2, 1)
LRU_WIDTH = BRANCH_WIDTH
LRU_BLOCKS = 8
LRU_BLOCK_W = LRU_WIDTH // LRU_BLOCKS
LRU_C = 8.0
ATTN_HEAD_DIM = 64
ATTN_HEADS = BRANCH_WIDTH // ATTN_HEAD_DIM
ATTN_KV_HEADS = 2
ATTN_WINDOW = 128
ATTN_BLOCK = 128
ROPE_BASE = 10000.0
HGRN_HEAD_DIM = 128
HGRN_HEADS = BRANCH_WIDTH // HGRN_HEAD_DIM
HGRN_CHUNK = 64
SSD_WIDTH = BRANCH_WIDTH
SSD_HEAD_DIM = 64
SSD_HEADS = SSD_WIDTH // SSD_HEAD_DIM
SSD_GROUPS = 2
SSD_HPG = SSD_HEADS // SSD_GROUPS
SSD_STATE = 128
SSD_CHUNK = 64
MLP_HIDDEN = 4 * D_MODEL
NORM_EPS = 1e-6

IN_SPLITS = (
    LRU_WIDTH, LRU_WIDTH,
    ATTN_HEADS * ATTN_HEAD_DIM,
    ATTN_KV_HEADS * ATTN_HEAD_DIM, ATTN_KV_HEADS * ATTN_HEAD_DIM,
    BRANCH_WIDTH, BRANCH_WIDTH,
    BRANCH_WIDTH, BRANCH_WIDTH,
    BRANCH_WIDTH,
    SSD_WIDTH,
    SSD_WIDTH + 2 * SSD_GROUPS * SSD_STATE,
    SSD_HEADS, SSD_HEADS,
    N_BRANCH * D_MODEL,
)
IN_COLS = sum(IN_SPLITS)

kernel_name = 'hybrid_lru_swa_hgrn2_ssd_prefix_dit'


def rmsnorm(x, g):
    xf = x.astype(jnp.float32)
    y = xf * lax.rsqrt(jnp.mean(xf * xf, axis=-1, keepdims=True) + NORM_EPS)
    return (y * g.astype(jnp.float32)).astype(x.dtype)


def modulate(h, shift, scale):
    return h * (1 + scale) + shift


def split_columns(p):
    return jnp.split(p, np.cumsum(IN_SPLITS)[:-1].tolist(), axis=-1)


def dwconv(x, w, b):
    y = lax.conv_general_dilated(x, w[:, None, :], window_strides=(1,), padding=[CONV_PAD],
                                 dimension_numbers=('NWC', 'WIO', 'NWC'),
                                 feature_group_count=x.shape[-1])
    return y + b


def to_chunks(t, ch):
    return jnp.swapaxes(t.reshape(t.shape[0], t.shape[1] // ch, ch, *t.shape[2:]), 0, 1)


def from_chunks(t):
    t = jnp.swapaxes(t, 0, 1)
    return t.reshape(t.shape[0], t.shape[1] * t.shape[2], *t.shape[3:])


def axial_rope(x, rows, cols):
    half = x.shape[-1] // 2
    quarter = half // 2
    inv_freq = ROPE_BASE ** (-jnp.arange(quarter, dtype=jnp.float32) / quarter)

    def rotate(xa, pos):
        ang = pos.astype(jnp.float32)[:, None] * inv_freq
        cos = jnp.cos(ang)[None, :, None, :].astype(x.dtype)
        sin = jnp.sin(ang)[None, :, None, :].astype(x.dtype)
        x1, x2 = xa[..., :quarter], xa[..., quarter:]
        return jnp.concatenate([x1 * cos - x2 * sin, x2 * cos + x1 * sin], axis=-1)

    return jnp.concatenate([rotate(x[..., :half], rows), rotate(x[..., half:], cols)], axis=-1)


def linear_scan(a, b, h0, reverse):
    idx = -1 if reverse else 0
    b = b.at[:, idx].add(a[:, idx] * h0)

    def combine(l, r):
        return (l[0] * r[0], r[0] * l[1] + r[1])

    return lax.associative_scan(combine, (a, b), reverse=reverse, axis=1)[1]


def rglru_branch(u, y_gate, u_c, y_gate_c, conv_w, conv_b, rec_w, rec_b, inp_w, inp_b, lam,
                 with_ctx_out):
    x_l = dwconv(u, conv_w, conv_b)
    x_c = dwconv(u_c, conv_w, conv_b)

    def coeffs(xx, d):
        xb = xx.reshape(*xx.shape[:2], LRU_BLOCKS, LRU_BLOCK_W)
        r = jax.nn.sigmoid(jnp.einsum('bthi,hij->bthj', xb, rec_w[d]).reshape(xx.shape) + rec_b[d])
        i = jax.nn.sigmoid(jnp.einsum('bthi,hij->bthj', xb, inp_w[d]).reshape(xx.shape) + inp_b[d])
        log_a = (-LRU_C * r * jax.nn.softplus(-lam[d])).astype(jnp.float32)
        a = jnp.exp(log_a)
        b = jnp.sqrt(-jnp.expm1(2.0 * log_a)) * (i * xx).astype(jnp.float32)
        return a, b

    h0 = jnp.zeros((u_c.shape[0], LRU_WIDTH), jnp.float32)
    h_lat, h_ctx = 0.0, 0.0
    for d, rev in ((0, False), (1, True)):
        a_c, b_c = coeffs(x_c, d)
        hc = linear_scan(a_c, b_c, h0, rev)
        a_l, b_l = coeffs(x_l, d)
        hl = linear_scan(a_l, b_l, hc[:, 0] if rev else hc[:, -1], rev)
        h_lat = h_lat + hl
        h_ctx = h_ctx + hc
    y_lat = h_lat.astype(u.dtype) * jax.nn.gelu(y_gate)
    y_ctx = h_ctx.astype(u.dtype) * jax.nn.gelu(y_gate_c) if with_ctx_out else None
    return y_lat, y_ctx


def window_attention(q, k, v, k_ctx, v_ctx, sink):
    B_, L, H, hd = q.shape
    G = k.shape[2]
    R = H // G
    nb = L // ATTN_BLOCK
    N = k_ctx.shape[1]
    span = 3 * ATTN_BLOCK
    scale = hd ** -0.5
    qb = q.reshape(B_, nb, ATTN_BLOCK, G, R, hd)

    def band(t):
        tp = jnp.pad(t, ((0, 0), (ATTN_BLOCK, ATTN_BLOCK), (0, 0), (0, 0)))
        tp = tp.reshape(B_, nb + 2, ATTN_BLOCK, G, hd)
        return jnp.concatenate([tp[:, :-2], tp[:, 1:-1], tp[:, 2:]], axis=2)

    kb, vb = band(k), band(v)
    s_loc = jnp.einsum('bnqgrd,bnkgd->bngrqk', qb, kb).astype(jnp.float32) * scale
    q_pos = jnp.arange(nb)[:, None] * ATTN_BLOCK + jnp.arange(ATTN_BLOCK)[None, :]
    k_pos = (jnp.arange(nb)[:, None] - 1) * ATTN_BLOCK + jnp.arange(span)[None, :]
    valid = ((jnp.abs(q_pos[:, :, None] - k_pos[:, None, :]) <= ATTN_WINDOW)
             & (k_pos[:, None, :] >= 0) & (k_pos[:, None, :] < L))
    s_loc = jnp.where(valid[None, :, None, None], s_loc, -jnp.inf)
    s_ctx = jnp.einsum('bnqgrd,bcgd->bngrqc', qb, k_ctx).astype(jnp.float32) * scale
    s_sink = jnp.broadcast_to(sink.astype(jnp.float32).reshape(1, 1, G, R, 1, 1),
                              (B_, nb, G, R, ATTN_BLOCK, 1))
    p = jax.nn.softmax(jnp.concatenate([s_loc, s_ctx, s_sink], axis=-1), axis=-1).astype(v.dtype)
    o = (jnp.einsum('bngrqk,bnkgd->bnqgrd', p[..., :span], vb)
         + jnp.einsum('bngrqc,bcgd->bnqgrd', p[..., span:span + N], v_ctx))
    return o.reshape(B_, L, H * hd)


def ctx_attention(q, k, v, sink):
    B_, N, H, hd = q.shape
    G = k.shape[2]
    R = H // G
    qg = q.reshape(B_, N, G, R, hd)
    s = jnp.einsum('bqgrd,bkgd->bgrqk', qg, k).astype(jnp.float32) * hd ** -0.5
    s_sink = jnp.broadcast_to(sink.astype(jnp.float32).reshape(1, G, R, 1, 1), (B_, G, R, N, 1))
    p = jax.nn.softmax(jnp.concatenate([s, s_sink], axis=-1), axis=-1)[..., :N].astype(v.dtype)
    return jnp.einsum('bgrqk,bkgd->bqgrd', p, v).reshape(B_, N, H * hd)


def attention_branch(q, k, v, q_c, k_c, v_c, sink, rows, cols, with_ctx_out):
    B_, L = q.shape[:2]
    N = q_c.shape[1]
    qh = axial_rope(q.reshape(B_, L, ATTN_HEADS, ATTN_HEAD_DIM), rows, cols)
    kh = axial_rope(k.reshape(B_, L, ATTN_KV_HEADS, ATTN_HEAD_DIM), rows, cols)
    vh = v.reshape(B_, L, ATTN_KV_HEADS, ATTN_HEAD_DIM)
    kch = k_c.reshape(B_, N, ATTN_KV_HEADS, ATTN_HEAD_DIM)
    vch = v_c.reshape(B_, N, ATTN_KV_HEADS, ATTN_HEAD_DIM)
    y_lat = window_attention(qh, kh, vh, kch, vch, sink)
    y_ctx = (ctx_attention(q_c.reshape(B_, N, ATTN_HEADS, ATTN_HEAD_DIM), kch, vch, sink)
             if with_ctx_out else None)
    return y_lat, y_ctx


def gla_chunk_scan(q, k, v, log_f, s0, reverse):
    seqs = [t.astype(jnp.float32) for t in (q, k, v, log_f)]
    if reverse:
        seqs = [jnp.flip(t, axis=1) for t in seqs]
    tri = jnp.tril(jnp.ones((HGRN_CHUNK, HGRN_CHUNK), bool))[None, :, :, None, None]

    def step(state, blk):
        qc, kc, vc, lfc = blk
        cum = jnp.cumsum(lfc, axis=1)
        decay = jnp.exp(jnp.where(tri, cum[:, :, None] - cum[:, None, :], -jnp.inf))
        scores = jnp.einsum('bthk,btshk,bshk->bths', qc, decay, kc)
        out = (jnp.einsum('bths,bshv->bthv', scores, vc)
               + jnp.einsum('bthk,bhkv->bthv', qc * jnp.exp(cum), state))
        total = cum[:, -1:]
        state = (state * jnp.exp(total[:, 0])[..., None]
                 + jnp.einsum('bshk,bshv->bhkv', kc * jnp.exp(total - cum), vc))
        return state, out

    s_end, out = lax.scan(step, s0, tuple(to_chunks(t, HGRN_CHUNK) for t in seqs))
    out = from_chunks(out)
    if reverse:
        out = jnp.flip(out, axis=1)
    return out, s_end


def hgrn2_branch(q, v, f_fwd, f_bwd, g, q_c, v_c, f_fwd_c, f_bwd_c, g_c, lb, norm_g, with_ctx_out):
    def heads(t):
        return t.reshape(*t.shape[:2], HGRN_HEADS, HGRN_HEAD_DIM)

    lb = lb.astype(jnp.float32).reshape(HGRN_HEADS, HGRN_HEAD_DIM)
    log_lb, log_rest = jnp.log(lb), jnp.log1p(-lb)

    def forget(z):
        lf = jnp.logaddexp(log_lb, log_rest + jax.nn.log_sigmoid(heads(z).astype(jnp.float32)))
        return lf, -jnp.expm1(lf)

    ql, vl = heads(jax.nn.silu(q)), heads(v)
    qc, vc = heads(jax.nn.silu(q_c)), heads(v_c)
    s0 = jnp.zeros((q_c.shape[0], HGRN_HEADS, HGRN_HEAD_DIM, HGRN_HEAD_DIM), jnp.float32)
    o_lat, o_ctx = 0.0, 0.0
    for z_l, z_c, rev in ((f_fwd, f_fwd_c, False), (f_bwd, f_bwd_c, True)):
        lf_c, k_c = forget(z_c)
        oc, s_ctx_end = gla_chunk_scan(qc, k_c, vc, lf_c, s0, rev)
        lf_l, k_l = forget(z_l)
        ol, _ = gla_chunk_scan(ql, k_l, vl, lf_l, s_ctx_end, rev)
        o_lat = o_lat + ol
        o_ctx = o_ctx + oc
    gain = norm_g.reshape(HGRN_HEADS, HGRN_HEAD_DIM)

    def readout(o, gate):
        return (rmsnorm(o.astype(gate.dtype), gain) * jax.nn.silu(heads(gate))).reshape(gate.shape)

    return readout(o_lat, g), (readout(o_ctx, g_c) if with_ctx_out else None)


def ssd_chunk_scan(x, dt, bm, cm, a, s0, reverse):
    seqs = [t.astype(jnp.float32) for t in (x, dt, bm, cm)]
    if reverse:
        seqs = [jnp.flip(t, axis=1) for t in seqs]
    tri = jnp.tril(jnp.ones((SSD_CHUNK, SSD_CHUNK), bool))[None, :, :, None, None]

    def step(state, blk):
        xc, dtc, bc, cc = blk
        cum = jnp.cumsum(dtc * a, axis=1)
        decay = jnp.exp(jnp.where(tri, cum[:, :, None] - cum[:, None, :], -jnp.inf))
        mix = jnp.einsum('btgn,bsgn->btsg', cc, bc)[..., None] * decay * dtc[:, None]
        out = (jnp.einsum('btsgr,bsgrp->btgrp', mix, xc)
               + jnp.einsum('btgn,bgrnp->btgrp', cc, state) * jnp.exp(cum)[..., None])
        total = cum[:, -1:]
        w = jnp.exp(total - cum) * dtc
        state = (state * jnp.exp(total[:, 0])[..., None, None]
                 + jnp.einsum('bsgn,bsgr,bsgrp->bgrnp', bc, w, xc))
        return state, out

    s_end, out = lax.scan(step, s0, tuple(to_chunks(t, SSD_CHUNK) for t in seqs))
    out = from_chunks(out)
    if reverse:
        out = jnp.flip(out, axis=1)
    return out, s_end


def ssd_branch(z, xbc, dt_f, dt_b, z_c, xbc_c, dt_f_c, dt_b_c, conv_w, conv_b, dt_bias, a_log,
               skip, norm_g, with_ctx_out):
    def prep(xbc_raw):
        u = jax.nn.silu(dwconv(xbc_raw, conv_w, conv_b))
        xs, bm, cm = jnp.split(u, [SSD_WIDTH, SSD_WIDTH + SSD_GROUPS * SSD_STATE], axis=-1)
        B_, T = u.shape[:2]
        return (xs.reshape(B_, T, SSD_GROUPS, SSD_HPG, SSD_HEAD_DIM),
                bm.reshape(B_, T, SSD_GROUPS, SSD_STATE), cm.reshape(B_, T, SSD_GROUPS, SSD_STATE))

    x_l, b_l, c_l = prep(xbc)
    x_c, b_c, c_c = prep(xbc_c)
    s0 = jnp.zeros((x_c.shape[0], SSD_GROUPS, SSD_HPG, SSD_STATE, SSD_HEAD_DIM), jnp.float32)
    d_skip = skip.reshape(SSD_GROUPS, SSD_HPG)[..., None]
    y_lat = d_skip * x_l
    y_ctx = d_skip * x_c
    for d, (dl, dc, rev) in enumerate(((dt_f, dt_f_c, False), (dt_b, dt_b_c, True))):
        a = -jnp.exp(a_log[d].astype(jnp.float32)).reshape(SSD_GROUPS, SSD_HPG)
        dtl = jax.nn.softplus(dl + dt_bias[d]).reshape(*dl.shape[:2], SSD_GROUPS, SSD_HPG)
        dtc = jax.nn.softplus(dc + dt_bias[d]).reshape(*dc.shape[:2], SSD_GROUPS, SSD_HPG)
        yc, s_ctx_end = ssd_chunk_scan(x_c, dtc, b_c, c_c, a, s0, rev)
        yl, _ = ssd_chunk_scan(x_l, dtl, b_l, c_l, a, s_ctx_end, rev)
        y_lat = y_lat + yl.astype(x_l.dtype)
        y_ctx = y_ctx + yc.astype(x_c.dtype)
    gain = norm_g.reshape(SSD_GROUPS, SSD_WIDTH // SSD_GROUPS)

    def readout(y, gate):
        B_, T = gate.shape[:2]
        yg = (y.reshape(B_, T, SSD_WIDTH) * jax.nn.silu(gate)).reshape(B_, T, SSD_GROUPS, -1)
        return rmsnorm(yg, gain).reshape(B_, T, SSD_WIDTH)

    return readout(y_lat, z), (readout(y_ctx, z_c) if with_ctx_out else None)


def merge_branches(ys, gate_logits, w_branch, w_out):
    gates = jax.nn.sigmoid(gate_logits.reshape(*gate_logits.shape[:-1], N_BRANCH, D_MODEL))
    merged = gates[..., 0, :] * (ys[0] @ w_branch[0])
    for i in range(1, N_BRANCH):
        merged = merged + gates[..., i, :] * (ys[i] @ w_branch[i])
    return merged @ w_out


def hybrid_mixer(h_lat, h_ctx, rows, cols, w_in, lru_conv_w, lru_conv_b, lru_rec_w, lru_rec_b,
                 lru_inp_w, lru_inp_b, lru_lambda, attn_sink, hgrn_lb, hgrn_norm_g, ssd_conv_w,
                 ssd_conv_b, ssd_dt_bias, ssd_a_log, ssd_skip, ssd_norm_g, w_branch, w_out,
                 with_ctx_out):
    (l_lx, l_lg, l_q, l_k, l_v, l_hq, l_hi, l_hff, l_hfb, l_hg, l_z, l_xbc, l_dtf, l_dtb,
     l_mg) = split_columns(h_lat @ w_in)
    (c_lx, c_lg, c_q, c_k, c_v, c_hq, c_hi, c_hff, c_hfb, c_hg, c_z, c_xbc, c_dtf, c_dtb,
     c_mg) = split_columns(h_ctx @ w_in)
    ya = rglru_branch(l_lx, l_lg, c_lx, c_lg, lru_conv_w, lru_conv_b, lru_rec_w, lru_rec_b,
                      lru_inp_w, lru_inp_b, lru_lambda, with_ctx_out)
    yb = attention_branch(l_q, l_k, l_v, c_q, c_k, c_v, attn_sink, rows, cols, with_ctx_out)
    yc = hgrn2_branch(l_hq, l_hi, l_hff, l_hfb, l_hg, c_hq, c_hi, c_hff, c_hfb, c_hg,
                      hgrn_lb, hgrn_norm_g, with_ctx_out)
    yd = ssd_branch(l_z, l_xbc, l_dtf, l_dtb, c_z, c_xbc, c_dtf, c_dtb, ssd_conv_w, ssd_conv_b,
                    ssd_dt_bias, ssd_a_log, ssd_skip, ssd_norm_g, with_ctx_out)
    out_lat = merge_branches([ya[0], yb[0], yc[0], yd[0]], l_mg, w_branch, w_out)
    out_ctx = (merge_branches([ya[1], yb[1], yc[1], yd[1]], c_mg, w_branch, w_out)
               if with_ctx_out else None)
    return out_lat, out_ctx


def sqrelu_mlp(h, w_up, w_down):
    return jnp.square(jax.nn.relu(h @ w_up)) @ w_down


def setup_inputs(seed: int = 0) -> dict:
    key = jax.random.key(seed)
    ks = iter(jax.random.split(key, 40))
    f32 = jnp.float32

    def nrm(shape, s):
        return jax.random.normal(next(ks), shape, f32) * s

    def unif(shape, lo, hi):
        return jax.random.uniform(next(ks), shape, f32, lo, hi)

    def gain(shape):
        return 1.0 + nrm(shape, 0.05)

    L = DEPTH
    x = nrm((BATCH, SEQ, D_MODEL), 1.0)
    c = nrm((BATCH, D_MODEL), 1.0)
    ctx = nrm((BATCH, CTX_LEN, D_MODEL), 1.0)
    c_ctx = nrm((D_MODEL,), 1.0)
    w_ada = nrm((L, D_MODEL, 6 * D_MODEL), 0.5 * D_MODEL ** -0.5)
    b_ada = nrm((L, 6 * D_MODEL), 0.01)
    g_pre_mix = gain((L, D_MODEL))
    g_post_mix = gain((L, D_MODEL))
    g_pre_mlp = gain((L, D_MODEL))
    g_post_mlp = gain((L, D_MODEL))
    w_in = nrm((L, D_MODEL, IN_COLS), D_MODEL ** -0.5)
    lru_conv_w = nrm((L, CONV_WIDTH, LRU_WIDTH), CONV_WIDTH ** -0.5)
    lru_conv_b = nrm((L, LRU_WIDTH), 0.01)
    lru_rec_w = nrm((L, 2, LRU_BLOCKS, LRU_BLOCK_W, LRU_BLOCK_W), LRU_BLOCK_W ** -0.5)
    lru_rec_b = nrm((L, 2, LRU_WIDTH), 0.01)
    lru_inp_w = nrm((L, 2, LRU_BLOCKS, LRU_BLOCK_W, LRU_BLOCK_W), LRU_BLOCK_W ** -0.5)
    lru_inp_b = nrm((L, 2, LRU_WIDTH), 0.01)
    a_init = unif((L, 2, LRU_WIDTH), 0.9, 0.999) ** (1.0 / LRU_C)
    lru_lambda = jnp.log(a_init) - jnp.log1p(-a_init)
    attn_sink = nrm((L, ATTN_HEADS), 0.5)
    hgrn_lb_logits = nrm((L, BRANCH_WIDTH), 1.0)
    hgrn_norm_g = gain((L, BRANCH_WIDTH))
    ssd_conv_w = nrm((L, CONV_WIDTH, SSD_WIDTH + 2 * SSD_GROUPS * SSD_STATE), CONV_WIDTH ** -0.5)
    ssd_conv_b = nrm((L, SSD_WIDTH + 2 * SSD_GROUPS * SSD_STATE), 0.01)
    dt0 = jnp.exp(unif((L, 2, SSD_HEADS), float(np.log(1e-3)), float(np.log(1e-1))))
    ssd_dt_bias = dt0 + jnp.log(-jnp.expm1(-dt0))
    ssd_a_log = jnp.log(unif((L, 2, SSD_HEADS), 1.0, 16.0))
    ssd_skip = gain((L, SSD_HEADS))
    ssd_norm_g = gain((L, SSD_WIDTH))
    w_branch = nrm((L, N_BRANCH, BRANCH_WIDTH, D_MODEL), BRANCH_WIDTH ** -0.5)
    w_out = nrm((L, D_MODEL, D_MODEL), D_MODEL ** -0.5)
    w_mlp_up = nrm((L, D_MODEL, MLP_HIDDEN), D_MODEL ** -0.5)
    w_mlp_down = nrm((L, MLP_HIDDEN, D_MODEL), MLP_HIDDEN ** -0.5)
    return {
        'x': x, 'c': c, 'ctx': ctx, 'c_ctx': c_ctx, 'w_ada': w_ada, 'b_ada': b_ada,
        'g_pre_mix': g_pre_mix, 'g_post_mix': g_post_mix, 'g_pre_mlp': g_pre_mlp,
        'g_post_mlp': g_post_mlp, 'w_in': w_in, 'lru_conv_w': lru_conv_w, 'lru_conv_b': lru_conv_b,
        'lru_rec_w': lru_rec_w, 'lru_rec_b': lru_rec_b, 'lru_inp_w': lru_inp_w,
        'lru_inp_b': lru_inp_b, 'lru_lambda': lru_lambda, 'attn_sink': attn_sink,
        'hgrn_lb_logits': hgrn_lb_logits, 'hgrn_norm_g': hgrn_norm_g, 'ssd_conv_w': ssd_conv_w,
        'ssd_conv_b': ssd_conv_b, 'ssd_dt_bias': ssd_dt_bias, 'ssd_a_log': ssd_a_log,
        'ssd_skip': ssd_skip, 'ssd_norm_g': ssd_norm_g, 'w_branch': w_branch, 'w_out': w_out,
        'w_mlp_up': w_mlp_up, 'w_mlp_down': w_mlp_down,
    }


def reference(x, c, ctx, c_ctx, w_ada, b_ada, g_pre_mix, g_post_mix, g_pre_mlp, g_post_mlp, w_in,
              lru_conv_w, lru_conv_b, lru_rec_w, lru_rec_b, lru_inp_w, lru_inp_b, lru_lambda,
              attn_sink, hgrn_lb_logits, hgrn_norm_g, ssd_conv_w, ssd_conv_b, ssd_dt_bias,
              ssd_a_log, ssd_skip, ssd_norm_g, w_branch, w_out, w_mlp_up, w_mlp_down):
    n_lat = x.shape[1]
    ROWS = n_lat // GRID_W
    rows = jnp.repeat(jnp.arange(ROWS, dtype=jnp.int32), GRID_W)
    cols = jnp.tile(jnp.arange(GRID_W, dtype=jnp.int32), ROWS)
    lb_all = jnp.cumsum(jax.nn.softmax(hgrn_lb_logits.astype(jnp.float32), axis=0), axis=0)
    lb_all = lb_all - lb_all[0]
    cond_lat = jax.nn.silu(c)
    cond_ctx = jax.nn.silu(c_ctx)
    lat, cst = x, ctx
    for l in range(DEPTH):
        last = l == DEPTH - 1
        mod_lat = jnp.split((cond_lat @ w_ada[l] + b_ada[l])[:, None, :], 6, axis=-1)
        mod_ctx = jnp.split(cond_ctx @ w_ada[l] + b_ada[l], 6, axis=-1)
        h_lat = modulate(rmsnorm(lat, g_pre_mix[l]), mod_lat[0], mod_lat[1])
        h_ctx = modulate(rmsnorm(cst, g_pre_mix[l]), mod_ctx[0], mod_ctx[1])
        m_lat, m_ctx = hybrid_mixer(
            h_lat, h_ctx, rows, cols, w_in[l], lru_conv_w[l], lru_conv_b[l], lru_rec_w[l],
            lru_rec_b[l], lru_inp_w[l], lru_inp_b[l], lru_lambda[l], attn_sink[l], lb_all[l],
            hgrn_norm_g[l], ssd_conv_w[l], ssd_conv_b[l], ssd_dt_bias[l], ssd_a_log[l],
            ssd_skip[l], ssd_norm_g[l], w_branch[l], w_out[l], not last)
        lat = lat + mod_lat[2] * rmsnorm(m_lat, g_post_mix[l])
        h2 = modulate(rmsnorm(lat, g_pre_mlp[l]), mod_lat[3], mod_lat[4])
        lat = lat + mod_lat[5] * rmsnorm(sqrelu_mlp(h2, w_mlp_up[l], w_mlp_down[l]), g_post_mlp[l])
        if not last:
            cst = cst + mod_ctx[2] * rmsnorm(m_ctx, g_post_mix[l])
            h2c = modulate(rmsnorm(cst, g_pre_mlp[l]), mod_ctx[3], mod_ctx[4])
            cst = cst + mod_ctx[5] * rmsnorm(sqrelu_mlp(h2c, w_mlp_up[l], w_mlp_down[l]),
                                              g_post_mlp[l])
    return lat
```

```python
import numpy as np
import concourse.bass as bass
import concourse.mybir as mybir
from concourse.bass_utils import run_bass_kernel_spmd
from contextlib import ExitStack, contextmanager

F32 = mybir.dt.float32
BF16 = mybir.dt.bfloat16
AF = mybir.ActivationFunctionType
ALU = mybir.AluOpType

D = 1024
NCTX = 256
NLAT = 2048
T = NCTX + NLAT
NCH = T // 64
NT = T // 128
DEPTH = 2
NCORES = 8
SPC = 2
EPS = 1e-6
TBS = [(0, 256, 1), (256, 512, 0), (768, 512, 0), (1280, 512, 0), (1792, 512, 0)]
MLP_SUP = [[(0, 256, 1), (256, 512, 0)], [(768, 512, 0), (1280, 256, 0)], [(1536, 512, 0), (2048, 256, 0)]]

C_LX, C_LG, C_Q, C_K, C_V = 0, 512, 1024, 1536, 1664
C_HQ, C_HI, C_HFF, C_HFB, C_HG = 1792, 2304, 2816, 3328, 3840
C_Z, C_XS, C_B, C_C, C_DT, C_MG = 4352, 4864, 5376, 5632, 5888, 5904

VEC = {}
_o = 0
for _n, _w in [("bada", 48), ("gpre", 8), ("gpostmix", 8), ("gpremlp", 8), ("gpostmlp", 8), ("lcw", 16), ("lcb", 4),
               ("lrb", 8), ("lib", 8), ("llam", 8), ("hl0", 4), ("hl1", 4), ("hng", 4), ("scw", 16), ("scb", 4)]:
    VEC[_n] = (_o, _w)
    _o += _w
NV = _o
V64 = {}
_o = 0
for _n, _w in [("cwx", 32), ("cbx", 8), ("ng", 8)]:
    V64[_n] = (_o, _w)
    _o += _w
NV64 = _o
ROWV = {"sink": (0, 8), "dtb": (8, 16), "alog": (24, 16), "skip": (40, 8)}
NROW = 48

ENGS = ("tensor", "vector", "scalar", "gpsimd", "sync")
import os as _os
SSD_STAGE = float(_os.environ.get("SSD_STAGE", "9"))


class Buf:
    __slots__ = ("name", "w", "r", "t")

    def __init__(self, name, t=None):
        self.name = name
        self.w = []
        self.r = []
        self.t = t

    def __getitem__(self, k):
        return self.t[k]


class Sched:
    def __init__(self, nc, n_dma_sems=32, n_sw_sems=8):
        self.nc = nc
        self.cnt = {e: 0 for e in ENGS}
        self.clock = {e: {} for e in ENGS}
        self.dsem = []
        self.dnext = 0
        self.n_dma_sems = n_dma_sems
        self.ssem = []
        self.snext = 0
        self.n_sw_sems = n_sw_sems
        self.evclock = {}
        self.ctx = []
        self.semh = {}
        self.ninstr = 0

    def open(self):
        nc = self.nc
        for e in ENGS:
            cm = nc.semaphore("es_" + e)
            self.semh["e_" + e] = cm.__enter__()
            self.ctx.append(cm)
        for i in range(self.n_dma_sems):
            cm = nc.semaphore("ds_%d" % i)
            self.semh["d_%d" % i] = cm.__enter__()
            self.ctx.append(cm)
            self.dsem.append([0, "d_%d" % i])
        for i in range(self.n_sw_sems):
            cm = nc.semaphore("ss_%d" % i)
            self.semh["s_%d" % i] = cm.__enter__()
            self.ctx.append(cm)
            self.ssem.append([0, "s_%d" % i])

    def close(self):
        for cm in reversed(self.ctx):
            cm.__exit__(None, None, None)

    def _emit1(self, eng, waits, fn, key, inc, fuse=False):
        e = getattr(self.nc, eng)
        fused = None
        if fuse and fn is not None and waits and _os.environ.get("FUSE", "1") == "1":
            fused = waits[-1]
            waits = waits[:-1]
        for (k, v) in waits:
            e.wait_ge(self.semh[k], v)
            self.ninstr += 1
        if fn is not None:
            ins = fn(e)
            if fused is not None:
                ins._wait_ge(self.semh[fused[0]], fused[1])
            ins.then_inc(self.semh[key], inc)
            self.ninstr += 1

    @staticmethod
    def _deps(reads, writes, par=False):
        deps = []
        for b in reads:
            deps.extend(b.w)
        for b in writes:
            if par and not b.r:
                continue
            deps.extend(b.w)
            deps.extend(b.r)
        return deps

    def _waits(self, eng, deps, skip_self=False):
        clk = self.clock[eng]
        own = "e_" + eng
        need = {}
        for (k, v) in deps:
            if skip_self and k == own:
                continue
            if clk.get(k, 0) >= v:
                continue
            if need.get(k, 0) < v:
                need[k] = v
        for k, v in need.items():
            ec = self.evclock.get((k, v))
            if ec is not None:
                for kk, vv in ec.items():
                    if clk.get(kk, 0) < vv:
                        clk[kk] = vv
            if clk.get(k, 0) < v:
                clk[k] = v
        return list(need.items())

    def _mark(self, ev, reads, writes, par=False):
        for b in writes:
            if par and not b.r:
                b.w.append(ev)
            else:
                b.w = [ev]
                b.r = []
        for b in reads:
            if b not in writes:
                b.r.append(ev)

    def op(self, eng, fn, reads=(), writes=(), skip_self=False):
        waits = self._waits(eng, self._deps(reads, writes), skip_self)
        self.cnt[eng] += 1
        ev = ("e_" + eng, self.cnt[eng])
        self.evclock[ev] = dict(self.clock[eng])
        self._emit1(eng, waits, fn, ev[0], 1, fuse=(eng != "tensor"))
        self._mark(ev, reads, writes)
        return ev

    def dma(self, eng, fn, reads=(), writes=(), par=False):
        deps = self._deps(reads, writes, par)
        if eng == "gpsimd":
            slot = self.ssem[self.snext]
            self.snext = (self.snext + 1) % len(self.ssem)
        else:
            slot = self.dsem[self.dnext]
            self.dnext = (self.dnext + 1) % len(self.dsem)
        cur, key = slot
        if cur > 0:
            deps.append((key, cur))
        waits = self._waits(eng, deps)
        slot[0] = cur + 16
        ev = (key, cur + 16)
        self.evclock[ev] = dict(self.clock[eng])
        self._emit1(eng, waits, fn, key, 16, fuse=True)
        self._mark(ev, reads, writes, par)
        return ev

    def wait_events(self, eng, events):
        waits = self._waits(eng, list(events))
        self._emit1(eng, waits, None, None, 0)

    def barrier(self):
        evs = [("e_" + e, self.cnt[e]) for e in ENGS if self.cnt[e] > 0]
        evs += [(key, cur) for (cur, key) in self.dsem + self.ssem if cur > 0]
        for e in ENGS:
            self.wait_events(e, evs)


def build(dbg=False, nlayers=DEPTH, nseq=SPC, stop_after=None):
    nc = bass.Bass("TRN2", target_bir_lowering=False)

    def din(name, shape, dt=F32):
        return nc.dram_tensor(name, list(shape), dt, kind="ExternalInput").ap()

    xin = din("xin", [SPC, D, T])
    cond_d = din("cond", [128, 8, 3])
    w_ada = din("w_ada", [DEPTH, D, 6 * D])
    w_in = din("w_in", [DEPTH, D, 10000])
    wqkp = din("wqkp", [DEPTH, D, 640])
    w_branch = din("w_branch", [DEPTH, 4, 512, D])
    w_out = din("w_out", [DEPTH, D, D])
    w_up = din("w_mlp_up", [DEPTH, D, 4 * D])
    w_down = din("w_mlp_down", [DEPTH, 4 * D, D])
    vecs_d = din("vecs", [DEPTH, 128, NV])
    vecs64_d = din("vecs64", [DEPTH, 64, NV64])
    rowv_d = din("rowv", [DEPTH, NROW])
    lru_bd_d = din("lru_bd", [DEPTH, 128, 16, 128])
    ident_d = din("ident", [128, 128])
    rope_d = din("rope_cs", [2, 128, T])
    amask_d = din("amask", [128, 2, 512])
    tri_d = din("tri64", [64, 2, 64])
    out_d = nc.dram_tensor("out", [SPC, D, NLAT], F32, kind="ExternalOutput").ap()
    skind = "ExternalOutput" if dbg else "Internal"
    res_d = nc.dram_tensor("res", [SPC, D, T], F32, kind=skind).ap()
    ybr_d = nc.dram_tensor("ybr", [4, 128, 8 * T], BF16, kind=skind).ap()
    mrg_d = nc.dram_tensor("mrg", [128, 8 * T], BF16, kind=skind).ap()
    h_dbg = nc.dram_tensor("h_dbg", [128, 8 * T], BF16, kind=skind).ap() if dbg else None

    def dscr(name, shape, dt=BF16):
        return nc.dram_tensor(name, list(shape), dt, kind="Internal").ap()

    wb_ada = dscr("wb_ada", [DEPTH, D, 6 * D])
    wb_in = dscr("wb_in", [DEPTH, D, 10000])
    wb_qkp = dscr("wb_qkp", [DEPTH, D, 640])
    wb_branch = dscr("wb_branch", [DEPTH, 4, 512, D])
    wb_out = dscr("wb_out", [DEPTH, D, D])
    wb_up = dscr("wb_up", [DEPTH, D, 4 * D])
    wb_down = dscr("wb_down", [DEPTH, 4 * D, D])
    wb_bd = dscr("wb_bd", [DEPTH, 128, 16 * 128])

    S = Sched(nc)
    S.open()
    cvb = {}

    conv_done = []
    conv_queue = {}

    def convert_plan(l):
        q = []

        def cv(key, dst2d, src2d, rows_per, seg):
            nrows = src2d.shape[0]
            cvb[(key, l)] = []
            for r0 in range(0, nrows, rows_per):
                b = Buf("cv_%s_%d_%d" % (key, l, r0))
                cvb[(key, l)].append(b)
                d_ = dst2d[r0:r0 + rows_per]
                s_ = src2d[r0:r0 + rows_per]
                if seg:
                    d_ = d_.rearrange("r (a c) -> r a c", c=seg)
                    s_ = s_.rearrange("r (a c) -> r a c", c=seg)
                q.append((b, d_, s_))
        cv("ada", wb_ada[l], w_ada[l], 512, 2048)
        cv("in", wb_in[l], w_in[l], 256, 2000)
        cv("qkp", wb_qkp[l], wqkp[l], 1024, 0)
        cv("bd", wb_bd[l], lru_bd_d[l].rearrange("p a c -> p (a c)"), 128, 0)
        cv("branch", wb_branch[l].rearrange("i r c -> (i r) c"), w_branch[l].rearrange("i r c -> (i r) c"), 1024, 0)
        cv("out", wb_out[l], w_out[l], 1024, 0)
        cv("up", wb_up[l], w_up[l], 512, 2048)
        cv("down", wb_down[l], w_down[l], 2048, 0)
        conv_queue[l] = q

    def convert_issue(l, n=100):
        q = conv_queue[l]
        while q and n > 0:
            b, d_, s_ = q.pop(0)
            rd = [conv_done[-3]] if len(conv_done) >= 3 else []
            S.dma("gpsimd", lambda e, d_=d_, s_=s_: e.dma_start(out=d_, in_=s_), reads=rd, writes=[b])
            conv_done.append(b)
            n -= 1

    uid = [0]

    def nm(p):
        uid[0] += 1
        return "%s_%d" % (p, uid[0])

    def sb(es, name, shape, dt=F32):
        return Buf(name, es.enter_context(nc.sbuf_tensor(nm(name), list(shape), dt)))

    @contextmanager
    def scope():
        with ExitStack() as es:
            yield es
            S.barrier()

    top = ExitStack()
    psb = [Buf("ps%d" % i, top.enter_context(nc.psum_tensor("psum%d" % i, [128, 512], F32))) for i in range(8)]
    psi = [0]

    psa = [0]
    psbi = [0]

    def ps(pool=None):
        if pool == "a":
            b = psb[psa[0]]
            psa[0] = (psa[0] + 1) % 4
            return b
        if pool == "b":
            b = psb[4 + psbi[0]]
            psbi[0] = (psbi[0] + 1) % 4
            return b
        b = psb[psi[0]]
        psi[0] = (psi[0] + 1) % 8
        return b

    def V(fn, reads=(), writes=()):
        return S.op("vector", fn, reads, writes)

    def A(fn, reads=(), writes=()):
        return S.op("scalar", fn, reads, writes)

    def G(fn, reads=(), writes=()):
        return S.op("gpsimd", fn, reads, writes)

    def PE(fn, reads=(), writes=()):
        return S.op("tensor", fn, reads, writes, skip_self=True)

    def mm(pbuf, out_ap, lhsT, rhs, reads, start=True, stop=True):
        return PE(lambda e: e.matmul(out_ap, lhsT=lhsT, rhs=rhs, start=start, stop=stop), reads=reads, writes=[pbuf])

    def dma_in(out_ap, in_ap, wbuf, rbufs=(), par=False):
        return S.dma("sync", lambda e: e.dma_start(out=out_ap, in_=in_ap), reads=list(rbufs), writes=[wbuf], par=par)

    def pipe(items, load, compute, first=None):
        nxt = load(items[0]) if first is None else first
        for i_, it in enumerate(items):
            cur = nxt
            if i_ + 1 < len(items):
                nxt = load(items[i_ + 1])
            compute(it, cur)

    class WQ:
        def __init__(self, specs, bufs, loader):
            self.specs, self.bufs, self.loader = list(specs), bufs, loader
            self.i = 0
            self._issue(0)

        def _issue(self, k):
            if k < len(self.specs):
                b = self.bufs[k % len(self.bufs)]
                self.loader(b, self.specs[k])

        def get(self):
            b = self.bufs[self.i % len(self.bufs)]
            self.i += 1
            self._issue(self.i)
            return b

    def dma_cast(out_ap, in_ap, wbuf, rbufs=()):
        return S.dma("gpsimd", lambda e: e.dma_start(out=out_ap, in_=in_ap), reads=list(rbufs), writes=[wbuf])

    def dma_out(out_ap, in_ap, rbuf, wbufs=(), par=False):
        return S.dma("sync", lambda e: e.dma_start(out=out_ap, in_=in_ap), reads=[rbuf], writes=list(wbufs), par=par)

    dumps = {}

    def dump(name, buf, ap, shape, dt=F32):
        if not dbg or name in dumps:
            return
        dumps[name] = 1
        dd = nc.dram_tensor("dump_" + name, list(shape), dt, kind="ExternalOutput").ap()
        S.dma("sync", lambda e: e.dma_start(out=dd, in_=ap), reads=[buf], writes=[Buf("dd_" + name)])

    def act(out_ap, in_ap, func, reads, writes, bias=None, scale=1.0):
        kw = {}
        if bias is not None:
            kw["bias"] = bias
        return A(lambda e: e.activation(out=out_ap, in_=in_ap, func=func, scale=scale, **kw), reads, writes)

    def tt(eng, out_ap, a, b, op, reads, writes):
        return S.op(eng, lambda e: e.tensor_tensor(out=out_ap, in0=a, in1=b, op=op), reads, writes)

    def stt(out_ap, in0, scalar, in1, op0, op1, reads, writes):
        return V(lambda e: e.scalar_tensor_tensor(out=out_ap, in0=in0, scalar=scalar, in1=in1, op0=op0, op1=op1),
                 reads, writes)

    def ts(eng, out_ap, in0, s1, op0, reads, writes, s2=None, op1=None):
        if op1 is None:
            return S.op(eng, lambda e: e.tensor_scalar(out=out_ap, in0=in0, scalar1=s1, scalar2=None, op0=op0),
                        reads, writes)
        return S.op(eng, lambda e: e.tensor_scalar(out=out_ap, in0=in0, scalar1=s1, scalar2=s2, op0=op0, op1=op1),
                    reads, writes)

    def wrows(ap2d):
        return ap2d.rearrange("(kc p) n -> p kc n", p=128)

    res_g = [[Buf("res%d_%d" % (s, i)) for i in range(T // 256)] for s in range(SPC)]
    ybr_b = [Buf("ybr%d" % i) for i in range(4)]
    out_b = Buf("out")
    hdbg_b = Buf("hdbg")
    mrg_b = Buf("mrg")
    out_events = []

    def gran(s, t0, n):
        return res_g[s][t0 // 256:(t0 + n + 255) // 256]

    ones32 = sb(top, "ones32", [128, 128], F32)
    onesb = sb(top, "onesb", [128, 128], BF16)
    identb = sb(top, "identb", [128, 128], BF16)
    cst = sb(top, "cst", [128, 4], F32)
    condb = sb(top, "condb", [128, 8, 3], BF16)
    tri = sb(top, "tri", [64, 2, 64], F32)
    V(lambda e: e.memset(ones32[:], 1.0), writes=[ones32])
    V(lambda e: e.memset(onesb[:], 1.0), writes=[onesb])
    V(lambda e: e.memset(cst[:, 0:1], EPS), writes=[cst])
    V(lambda e: e.memset(cst[:, 1:2], 1.0), writes=[cst])
    V(lambda e: e.memset(cst[:, 2:3], 0.0), writes=[cst])
    dma_cast(identb[:], ident_d, identb)
    dma_in(tri[:], tri_d, tri)
    modT3 = [sb(top, "modT3", [128, 48, 3], F32) for _ in range(DEPTH)]
    with ExitStack() as es0:
        c32 = sb(es0, "c32", [128, 8, 3], F32)
        dma_in(c32[:], cond_d, c32)
        act(condb[:], c32[:], AF.Silu, [c32], [condb])
        S.barrier()

    def norm_block(es_tmp, src, src_ap, n, nparts=128, nsub=8, denom=float(D)):
        sq = sqpool[0]
        sqi[0] += 1
        rs = rspool[rsi[0] % 2]
        rsi[0] += 1
        act(sq[0:nparts, 0:nsub, 0:n], src_ap, AF.Square, [src], [sq])
        p = ps()
        for k in range(nsub):
            mm(p, p[0:nparts, 0:n], ones32[0:nparts, 0:nparts], sq[0:nparts, k, 0:n], [ones32, sq],
               start=(k == 0), stop=(k == nsub - 1))
        act(rs[0:nparts, 0:n], p[0:nparts, 0:n], AF.Sqrt, [p, cst], [rs], bias=cst[0:nparts, 0:1], scale=1.0 / denom)
        V(lambda e: e.reciprocal(rs[0:nparts, 0:n], rs[0:nparts, 0:n]), [rs], [rs])
        return rs

    sqpool = []
    rspool = []
    sqi = [0]
    rsi = [0]

    convert_plan(0)
    convert_plan(1)
    convert_issue(0)
    for s in range(nseq):
        for l in range(nlayers):
            last = (l == DEPTH - 1)
            src_res = xin[s] if l == 0 else res_d[s]
            src_bufs = (lambda t0, n: []) if l == 0 else (lambda t0, n, s=s: gran(s, t0, n))
            with scope() as L:
                hT = sb(L, "hT", [128, 8, T], BF16)
                vec = sb(L, "vec", [128, NV], F32)
                GS = sb(L, "GS", [128, 6, 8, 2], F32)
                sqpool[:] = [sb(L, "sq", [128, 8, 512], F32) for _ in range(1)]
                rspool[:] = [sb(L, "rs", [128, 512], F32) for _ in range(2)]
                dma_in(vec[:], vecs_d[l], vec)

                def vcol(name, j=0, n=1, p0=0, p1=128):
                    o, w = VEC[name]
                    return vec[p0:p1, o + j:o + j + n]

                csl = slice(s, 3, 2 - s)
                if s == 0:
                  with scope() as es:
                    wa = [sb(es, "wada", [128, 8, 1536], BF16) for _ in range(2)]
                    pm = ps()
                    pmv = pm[:, 0:144].rearrange("p (j w) -> p j w", w=3)

                    def ld_ada(piece):
                        wb = wa[piece % 2]
                        dma_in(wb[:], wrows(wb_ada[l])[:, :, piece * 1536:(piece + 1) * 1536], wb, cvb[("ada", l)])
                        return wb

                    def cp_ada(piece, wb):
                        for jj in range(12):
                            j = piece * 12 + jj
                            for dc in range(8):
                                mm(pm, pmv[:, j, :], wb[:, dc, jj * 128:(jj + 1) * 128], condb[:, dc, :],
                                   [wb, condb], start=(dc == 0), stop=(dc == 7))
                    pipe(list(range(4)), ld_ada, cp_ada)
                    o, w = VEC["bada"]
                    tt("vector", modT3[l][:], pmv, vec[:, o:o + 48].unsqueeze(2).to_broadcast([128, 48, 3]), ALU.add,
                       [pm, vec], [modT3[l]])
                if True:
                    modT_b = modT3[l]

                    class _MT:
                        def __getitem__(self, k):
                            return modT_b[:, :, csl][k]
                    modT = _MT()
                    modTb = modT_b

                    def gbc(name):
                        o, w = VEC[name]
                        return vec[:, o:o + 8].unsqueeze(2).to_broadcast([128, 8, 2])
                    stt(GS[:, 0], modT[:, 8:16, :], 1.0, gbc("gpre"), ALU.add, ALU.mult, [modTb, vec], [GS])
                    V(lambda e: e.tensor_copy(GS[:, 1], modT[:, 0:8, :]), [modTb], [GS])
                    tt("vector", GS[:, 2], modT[:, 16:24, :], gbc("gpostmix"), ALU.mult, [modTb, vec], [GS])
                    stt(GS[:, 3], modT[:, 32:40, :], 1.0, gbc("gpremlp"), ALU.add, ALU.mult, [modTb, vec], [GS])
                    V(lambda e: e.tensor_copy(GS[:, 4], modT[:, 24:32, :]), [modTb], [GS])
                    tt("vector", GS[:, 5], modT[:, 40:48, :], gbc("gpostmlp"), ALU.mult, [modTb, vec], [GS])

                def modulate_to(dst, xt, n, t0, w, gi, si, rstd, tmp):
                    for dc in range(8):
                        stt(tmp[:, 0:n], xt[:, dc, 0:n], GS[:, gi, dc, w:w + 1], rstd[:, 0:n], ALU.mult, ALU.mult,
                            [xt, GS, rstd], [tmp])
                        act(dst[:, dc, t0:t0 + n], tmp[:, 0:n], AF.Identity, [tmp, GS], [dst],
                            bias=GS[:, si, dc, w:w + 1])

                with scope() as es:
                    xts = [sb(es, "xt", [128, 8, 512], F32) for _ in range(2)]
                    tmps = [sb(es, "tmp", [128, 512], F32) for _ in range(2)]
                    def ld_x(it):
                        bi, (t0, n, w) = it
                        xt = xts[bi % 2]
                        dma_in(xt[:, :, 0:n], wrows(src_res)[:, :, t0:t0 + n], xt, src_bufs(t0, n))
                        return xt

                    def cp_x(it, xt):
                        bi, (t0, n, w) = it
                        rstd = norm_block(es, xt, xt[:, :, 0:n], n)
                        modulate_to(hT, xt, n, t0, w, 0, 1, rstd, tmps[bi % 2])
                    pipe(list(enumerate(TBS)), ld_x, cp_x)
                if dbg and stop_after == "h":
                    dma_out(h_dbg, hT[:].rearrange("p a t -> p (a t)"), hT, [hdbg_b])

                def load_win_cols(wt, dst_ap, c0, ncols, src=None):
                    if src is None:
                        dma_in(dst_ap, wrows(wb_in[l])[:, :, c0:c0 + ncols], wt, cvb[("in", l)], par=True)
                    else:
                        dma_in(dst_ap, wrows(wb_qkp[l])[:, :, c0:c0 + ncols], wt, cvb[("qkp", l)], par=True)

                def proj_fm(wt, w_ap_fn, t0, n, M=128):
                    p = ps()
                    for dc in range(8):
                        mm(p, p[0:M, 0:n], w_ap_fn(dc), hT[:, dc, t0:t0 + n], [wt, hT], start=(dc == 0), stop=(dc == 7))
                    return p

                def mixer_lru():
                    with scope() as es:
                        bd = sb(es, "bd", [128, 16, 128], BF16)
                        dma_in(bd[:].rearrange("p a c -> p (a c)"), wb_bd[l], bd, cvb[("bd", l)])
                        cs = sb(es, "cs", [128, 8], F32)
                        o, w = VEC["llam"]
                        act(cs[:], vec[:, o:o + 8], AF.Exp, [vec], [cs], scale=-1.0)
                        act(cs[:], cs[:], AF.Ln, [cs, cst], [cs], bias=cst[:, 1:2])
                        ts("vector", cs[:], cs[:], -8.0, ALU.mult, [cs], [cs])
                        ya = sb(es, "ya", [128, 4, T], BF16)
                        wls = [sb(es, "wl", [128, 8, 2, 128], BF16) for _ in range(2)]
                        u = sb(es, "u", [128, T], F32)
                        g = sb(es, "g", [128, T], F32)
                        xc = sb(es, "xc", [128, T], F32)
                        xcb = sb(es, "xcb", [128, T], BF16)
                        r_ = sb(es, "r", [128, T], F32)
                        i_ = sb(es, "i", [128, T], F32)
                        t_ = sb(es, "t", [128, T], F32)
                        h0 = sb(es, "h0", [128, T], F32)
                        def ld_l(wl, c):
                            load_win_cols(wl, wl[:, :, 0, :], C_LX + c * 128, 128)
                            load_win_cols(wl, wl[:, :, 1, :], C_LG + c * 128, 128)
                        wq_l = WQ(range(4), wls, ld_l)
                        for c in range(4):
                            wl = wq_l.get()
                            for (t0, n, w) in TBS:
                                p = proj_fm(wl, lambda dc: wl[:, dc, 0, :], t0, n)
                                act(u[:, t0:t0 + n], p[:, 0:n], AF.Copy, [p], [u])
                                p = proj_fm(wl, lambda dc: wl[:, dc, 1, :], t0, n)
                                act(g[:, t0:t0 + n], p[:, 0:n], AF.Gelu_apprx_tanh, [p], [g])
                            act(xc[:], u[:], AF.Identity, [u, vec], [xc], bias=vcol("lcb", c), scale=vcol("lcw", 2 * 4 + c))
                            for (s0, s1) in ((0, NCTX), (NCTX, T)):
                                for j in (0, 1, 3):
                                    off = j - 2
                                    a_ = max(s0, s0 - off)
                                    b_ = min(s1, s1 - off)
                                    stt(xc[:, a_:b_], u[:, a_ + off:b_ + off], vcol("lcw", j * 4 + c), xc[:, a_:b_],
                                        ALU.mult, ALU.add, [u, vec, xc], [xc])
                            act(xcb[:], xc[:], AF.Copy, [xc], [xcb])
                            for d in range(2):
                                for (t0, n, w) in TBS:
                                    p = ps()
                                    mm(p, p[:, 0:n], bd[:, (d * 2 + 0) * 4 + c, :], xcb[:, t0:t0 + n], [bd, xcb])
                                    act(r_[:, t0:t0 + n], p[:, 0:n], AF.Sigmoid, [p, vec], [r_], bias=vcol("lrb", d * 4 + c))
                                    p = ps()
                                    mm(p, p[:, 0:n], bd[:, (d * 2 + 1) * 4 + c, :], xcb[:, t0:t0 + n], [bd, xcb])
                                    act(i_[:, t0:t0 + n], p[:, 0:n], AF.Sigmoid, [p, vec], [i_], bias=vcol("lib", d * 4 + c))
                                act(r_[:], r_[:], AF.Exp, [r_, cs], [r_], scale=cs[:, d * 4 + c:d * 4 + c + 1])
                                tt("vector", t_[:], r_[:], r_[:], ALU.mult, [r_], [t_])
                                act(t_[:], t_[:], AF.Sqrt, [t_, cst], [t_], bias=cst[:, 1:2], scale=-1.0)
                                tt("gpsimd", i_[:], i_[:], xc[:], ALU.mult, [i_, xc], [i_])
                                tt("vector", i_[:], i_[:], t_[:], ALU.mult, [i_, t_], [i_])
                                if d == 0:
                                    V(lambda e: e.tensor_tensor_scan(out=h0[:], data0=r_[:], data1=i_[:], initial=0.0,
                                                                     op0=ALU.mult, op1=ALU.add), [r_, i_], [h0])
                                else:
                                    V(lambda e: e.tensor_tensor_scan(out=u[:, 0:NCTX][:, ::-1], data0=r_[:, 0:NCTX][:, ::-1],
                                                                     data1=i_[:, 0:NCTX][:, ::-1], initial=0.0,
                                                                     op0=ALU.mult, op1=ALU.add), [r_, i_], [u])
                                    V(lambda e: e.tensor_tensor_scan(out=u[:, NCTX:T][:, ::-1], data0=r_[:, NCTX:T][:, ::-1],
                                                                     data1=i_[:, NCTX:T][:, ::-1], initial=u[:, 0:1],
                                                                     op0=ALU.mult, op1=ALU.add), [r_, i_, u], [u])
                            tt("gpsimd", h0[:], h0[:], u[:], ALU.add, [h0, u], [h0])
                            tt("vector", ya[:, c, :], h0[:], g[:], ALU.mult, [h0, g], [ya])
                        dma_out(ybr_d[0][:, 0:4 * T], ya[:].rearrange("p a t -> p (a t)"), ya, [ybr_b[0]])

                def mixer_attn():
                    with scope() as es:
                        cos = sb(es, "cos", [128, T], F32)
                        sin = sb(es, "sin", [128, T], F32)
                        dma_in(cos[:], rope_d[0], cos)
                        dma_in(sin[:], rope_d[1], sin)
                        am = sb(es, "am", [128, 2, 512], BF16)
                        dma_cast(am[:], amask_d, am)
                        es8 = sb(es, "es8", [128, 8], F32)
                        o, w = ROWV["sink"]
                        dma_in(es8[:], rowv_d[l, o:o + 8].partition_broadcast(128), es8)
                        act(es8[:], es8[:], AF.Exp, [es8], [es8])
                        QT = sb(es, "QT", [128, NT, 512], BF16)
                        KT = sb(es, "KT", [128, T], BF16)
                        Vt = sb(es, "Vt", [128, NT, 128], BF16)
                        yb = sb(es, "yb", [128, 4, T], BF16)
                        wq = [sb(es, "wq", [128, 8, 2, 128], BF16) for _ in range(2)]
                        t1s = [sb(es, "t1", [128, 512], F32) for _ in range(2)]
                        t2s = [sb(es, "t2", [128, 512], F32) for _ in range(2)]
                        cnt = 0
                        wqa = sb(es, "wqa", [128, 8, 2, 512], BF16)
                        load_win_cols(wqa, wqa[:, :, 0, :], C_Q, 512)
                        load_win_cols(wqa, wqa[:, :, 1, :], 0, 512, src=wqkp[l])

                        def ld_q(wt, r):
                            if r < 4:
                                for wh in range(2):
                                    for g2 in range(2):
                                        G(lambda e, wh=wh, g2=g2: e.tensor_copy(
                                            wt[:, :, wh, g2 * 64:(g2 + 1) * 64],
                                            wqa[:, :, wh, (g2 * 4 + r) * 64:(g2 * 4 + r + 1) * 64]), [wqa], [wt])
                            elif r == 4:
                                load_win_cols(wt, wt[:, :, 0, :], C_K, 128)
                                load_win_cols(wt, wt[:, :, 1, :], 512, 128, src=wqkp[l])
                            else:
                                load_win_cols(wt, wt[:, :, 0, :], C_V, 128)
                        wq_a = WQ(range(6), wq, ld_q)
                        for r in range(5):
                            wt = wq_a.get()
                            for (t0, n, w) in TBS:
                                t1 = t1s[cnt % 2]
                                t2 = t2s[cnt % 2]
                                cnt += 1
                                p = proj_fm(wt, lambda dc: wt[:, dc, 0, :], t0, n)
                                tt("vector", t1[:, 0:n], p[:, 0:n], cos[:, t0:t0 + n], ALU.mult, [p, cos], [t1])
                                p = proj_fm(wt, lambda dc: wt[:, dc, 1, :], t0, n)
                                tt("vector", t2[:, 0:n], p[:, 0:n], sin[:, t0:t0 + n], ALU.mult, [p, sin], [t2])
                                if r < 4:
                                    dst = QT[:, t0 // 128:(t0 + n) // 128, r * 128:(r + 1) * 128]
                                    tt("gpsimd", dst, t1[:, 0:n].rearrange("p (b q) -> p b q", q=128),
                                       t2[:, 0:n].rearrange("p (b q) -> p b q", q=128), ALU.add, [t1, t2], [QT])
                                else:
                                    tt("gpsimd", KT[:, t0:t0 + n], t1[:, 0:n], t2[:, 0:n], ALU.add, [t1, t2], [KT])
                        wv = wq_a.get()
                        for i in range(NT):
                            p = ps()
                            for dc in range(8):
                                mm(p, p[:, 0:128], hT[:, dc, i * 128:(i + 1) * 128], wv[:, dc, 0, :], [hT, wv],
                                   start=(dc == 0), stop=(dc == 7))
                            act(Vt[:, i, :], p[:, 0:128], AF.Copy, [p], [Vt])
                        Es = [sb(es, "E", [128, 512], BF16) for _ in range(3)]
                        den = sb(es, "den", [128, 512], F32)
                        ecnt = 0
                        items = []
                        for qb in range(NT):
                            keys = [(0, None), (1, None)]
                            if qb >= 2:
                                if qb - 1 >= 2:
                                    keys.append((qb - 1, 0))
                                keys.append((qb, None))
                                if qb + 1 < NT:
                                    keys.append((qb + 1, 1))
                            for g_ in range(2):
                                for ki, (kt, mk) in enumerate(keys):
                                    items.append((qb, g_, ki, kt, mk, len(keys)))
                        ecn = [0]

                        def stage1(it):
                            qb, g_, ki, kt, mk, nk = it
                            hs = slice(g_ * 64, (g_ + 1) * 64)
                            pS = ps("b")
                            mm(pS, pS[:, :], KT[hs, kt * 128:(kt + 1) * 128], QT[hs, qb, :], [KT, QT])
                            E = Es[ecn[0] % 3]
                            ecn[0] += 1
                            act(E[:], pS[:], AF.Exp, [pS], [E], scale=0.125)
                            if mk is not None:
                                tt("vector", E[:], E[:], am[:, mk, :], ALU.mult, [E, am], [E])
                            return E
                        v3 = lambda ap: ap.rearrange("p (r q) -> p r q", q=128)
                        Enext = stage1(items[0])
                        pv = p1 = None
                        for ii, it in enumerate(items):
                            E = Enext
                            if ii + 1 < len(items):
                                Enext = stage1(items[ii + 1])
                            qb, g_, ki, kt, mk, nk = it
                            hs = slice(g_ * 64, (g_ + 1) * 64)
                            if ki == 0:
                                pv = ps("a")
                                p1 = ps("a")
                            mm(pv, pv[:, :], Vt[:, kt, :], E[:], [Vt, E], start=(ki == 0), stop=(ki == nk - 1))
                            mm(p1, p1[:, :], onesb[:], E[:], [onesb, E], start=(ki == 0), stop=(ki == nk - 1))
                            if ki == nk - 1:
                                tt("vector", v3(den[hs, :]), v3(p1[hs, :]),
                                   es8[hs, g_ * 4:(g_ + 1) * 4].unsqueeze(2).to_broadcast([64, 4, 128]), ALU.add,
                                   [p1, es8], [den])
                                V(lambda e, hs=hs: e.reciprocal(den[hs, :], den[hs, :]), [den], [den])
                                tt("vector", yb[hs, :, qb * 128:(qb + 1) * 128], v3(pv[hs, :]), v3(den[hs, :]), ALU.mult,
                                   [pv, den], [yb])
                        dma_out(ybr_d[1][:, 0:4 * T], yb[:].rearrange("p a t -> p (a t)"), yb, [ybr_b[1]])

                def mixer_hgrn():
                    with scope() as es:
                        lbv = sb(es, "lbv", [128, 4], F32)
                        omlb = sb(es, "omlb", [128, 4], F32)
                        if l == 0:
                            V(lambda e: e.memset(lbv[:], 0.0), writes=[lbv])
                        else:
                            o0, _ = VEC["hl0"]
                            o1, _ = VEC["hl1"]
                            tt("vector", lbv[:], vec[:, o1:o1 + 4], vec[:, o0:o0 + 4], ALU.subtract, [vec], [lbv])
                            act(lbv[:], lbv[:], AF.Sigmoid, [lbv], [lbv])
                        ts("vector", omlb[:], lbv[:], -1.0, ALU.mult, [lbv], [omlb], s2=1.0, op1=ALU.add)
                        rm = sb(es, "rm", [128, T + 64], BF16)
                        V(lambda e: e.memset(rm[:], 1.0), writes=[rm])
                        V(lambda e: e.memset(rm[:].rearrange("p (c j) -> p c j", j=64)[:, :, 0:1], 0.0), [rm], [rm])
                        ws = [sb(es, "wh", [128, 8, 128], BF16) for _ in range(2)]
                        wi = [0]

                        hspecs = []
                        for h_ in range(4):
                            hspecs += [C_HQ + h_ * 128, C_HI + h_ * 128, C_HFF + h_ * 128, C_HFB + h_ * 128, C_HG + h_ * 128]
                        wq_h = WQ(hspecs, ws, lambda wt, c0: load_win_cols(wt, wt[:], c0, 128))

                        def wnext(c0):
                            assert wq_h.specs[wq_h.i] == c0
                            return wq_h.get()
                        qs = sb(es, "qs", [128, T], F32)
                        vtok = sb(es, "vtok", [64, NCH, 128], BF16)
                        oacc = sb(es, "oacc", [128, T], F32)
                        Fb = sb(es, "Fb", [128, T], F32)
                        Fb2 = sb(es, "Fb2", [128, T], F32)
                        KK = sb(es, "KK", [128, T], F32)
                        CUM = sb(es, "CUM", [128, T], F32)
                        DM = sb(es, "DM", [128, T], F32)
                        Ab = sb(es, "Ab", [128, T], BF16)
                        Bm = sb(es, "Bm", [128, T], BF16)
                        Aq = sb(es, "Aq", [128, T], BF16)
                        Bk = sb(es, "Bk", [128, T], BF16)
                        Bkt = sb(es, "Bkt", [64, NCH, 128], BF16)
                        ext = sb(es, "ext", [128, NCH], F32)
                        Sms = [sb(es, "Sm", [64, 512], BF16) for _ in range(2)]
                        st = sb(es, "st", [128, 128], F32)
                        stbh = [sb(es, "stbh", [128, 9, 128], BF16) for _ in range(2)]
                        hdel = sb(es, "hdel", [128, 128 * 8], F32)
                        hmul = sb(es, "hmul", [128, 128 * 8], F32)
                        hsc = sb(es, "hsc", [128, 128 * 8], F32)
                        hk = [0]
                        if dbg:
                            print("SBUF remaining in HGRN scope:", nc.sbuf_bytes_remaining, flush=True)
                        ych = sb(es, "ych", [128, T], BF16)
                        c3 = lambda b: b[:, 0:T].rearrange("p (c j) -> p c j", j=64)
                        smi = 0
                        for h in range(4):
                            wt = wnext(C_HQ + h * 128)
                            for (t0, n, w) in TBS:
                                p = proj_fm(wt, lambda dc: wt[:, dc, :], t0, n)
                                act(qs[:, t0:t0 + n], p[:, 0:n], AF.Silu, [p], [qs])
                            wt = wnext(C_HI + h * 128)
                            for c0 in range(0, NCH, 4):
                                p = ps()
                                for j in range(4):
                                    ch = c0 + j
                                    for dc in range(8):
                                        mm(p, p[0:64, j * 128:(j + 1) * 128], hT[:, dc, ch * 64:(ch + 1) * 64], wt[:, dc, :],
                                           [hT, wt], start=(dc == 0), stop=(dc == 7))
                                act(vtok[:, c0:c0 + 4, :], p[0:64, :].rearrange("p (c v) -> p c v", v=128), AF.Copy, [p], [vtok])
                            for d in range(2):
                                wt = wnext((C_HFF if d == 0 else C_HFB) + h * 128)
                                for (t0, n, w) in TBS:
                                    p = proj_fm(wt, lambda dc: wt[:, dc, :], t0, n)
                                    act(Fb[:, t0:t0 + n], p[:, 0:n], AF.Sigmoid, [p], [Fb])
                                ts("vector", Fb[:], Fb[:], omlb[:, h:h + 1], ALU.mult, [Fb, omlb, lbv], [Fb],
                                   s2=lbv[:, h:h + 1], op1=ALU.add)
                                ts("gpsimd", KK[:], Fb[:], -1.0, ALU.mult, [Fb], [KK], s2=1.0, op1=ALU.add)
                                act(Fb[:], Fb[:], AF.Ln, [Fb], [Fb])
                                if d == 0:
                                    V(lambda e: e.tensor_tensor_scan(out=CUM[:], data0=rm[:, 0:T], data1=Fb[:], initial=0.0,
                                                                     op0=ALU.mult, op1=ALU.add), [rm, Fb], [CUM])
                                    mid, lastj = 31, 63
                                else:
                                    V(lambda e: e.tensor_tensor_scan(out=CUM[:, ::-1], data0=rm[:, 1:T + 1][:, ::-1],
                                                                     data1=Fb[:, ::-1], initial=0.0,
                                                                     op0=ALU.mult, op1=ALU.add), [rm, Fb], [CUM])
                                    mid, lastj = 32, 0
                                cum3 = c3(CUM)
                                dump("lf%d" % d, Fb, Fb[:], [128, T])
                                dump("kk%d" % d, KK, KK[:], [128, T])
                                dump("cum%d" % d, CUM, CUM[:], [128, T])
                                tt("vector", c3(DM), cum3, cum3[:, :, mid:mid + 1].to_broadcast([128, NCH, 64]), ALU.subtract,
                                   [CUM], [DM])
                                act(Fb[:], DM[:], AF.Exp, [DM], [Fb])
                                tt("vector", Ab[:], qs[:], Fb[:], ALU.mult, [qs, Fb], [Ab])
                                act(Fb2[:], DM[:], AF.Exp, [DM], [Fb2], scale=-1.0)
                                tt("vector", Bm[:], KK[:], Fb2[:], ALU.mult, [KK, Fb2], [Bm])
                                act(Fb[:], CUM[:], AF.Exp, [CUM], [Fb])
                                tt("vector", Aq[:], qs[:], Fb[:], ALU.mult, [qs, Fb], [Aq])
                                tt("vector", c3(DM), cum3[:, :, lastj:lastj + 1].to_broadcast([128, NCH, 64]), cum3,
                                   ALU.subtract, [CUM], [DM])
                                act(Fb2[:], DM[:], AF.Exp, [DM], [Fb2])
                                tt("vector", Bk[:], KK[:], Fb2[:], ALU.mult, [KK, Fb2], [Bk])
                                act(ext[:], cum3[:, :, lastj], AF.Exp, [CUM], [ext])
                                dump("ext%d" % d, ext, ext[:], [128, NCH])
                                dump("Ab%d" % d, Ab, Ab[:], [128, T], BF16)
                                dump("Bm%d" % d, Bm, Bm[:], [128, T], BF16)
                                dump("Aq%d" % d, Aq, Aq[:], [128, T], BF16)
                                dump("Bk%d" % d, Bk, Bk[:], [128, T], BF16)
                                for c0 in range(0, NCH, 4):
                                    p = ps()
                                    pb = p[:].bitcast(BF16)
                                    for j in range(4):
                                        ch = c0 + j
                                        PE(lambda e, j=j, ch=ch: e.transpose(pb[0:64, j * 128:(j + 1) * 128],
                                                                            Bk[:, ch * 64:(ch + 1) * 64], identb[:]),
                                           [Bk, identb], [p])
                                    act(Bkt[:, c0:c0 + 4, :], pb[0:64, 0:512].rearrange("p (c k) -> p c k", k=128), AF.Copy,
                                        [p], [Bkt])
                                V(lambda e: e.memset(st[:], 0.0), writes=[st])
                                V(lambda e, b_=stbh[hk[0] % 2]: e.memset(b_[:, 0, :], 0.0), writes=[stbh[hk[0] % 2]])
                                for Sm_ in Sms:
                                    V(lambda e, Sm_=Sm_: e.memset(Sm_[:], 0.0), writes=[Sm_])
                                triu = tri[:].bitcast(mybir.dt.uint32)
                                if d == 0:
                                    groups = [list(range(0, 4))] + [list(range(c, c + 8)) for c in range(4, NCH, 8)]
                                else:
                                    groups = [[3, 2, 1, 0]] + [list(range(c + 7, c - 1, -1)) for c in range(NCH - 8, 3, -8)]
                                for grp in groups:
                                    nj = len(grp)
                                    lo, hi = min(grp), max(grp)
                                    rev = grp[0] > grp[-1]
                                    stb = stbh[hk[0] % 2]
                                    stb_n = stbh[(hk[0] + 1) % 2]
                                    hk[0] += 1
                                    dl_v = hdel[:, 0:128 * nj].rearrange("p (c j) -> p c j", j=nj)
                                    ml_v = hmul[:, 0:128 * nj].rearrange("p (c j) -> p c j", j=nj)
                                    sc_v = hsc[:, 0:128 * nj].rearrange("p (c j) -> p c j", j=nj)
                                    for j0 in range(0, nj, 4):
                                        pD = ps("b")
                                        for jj in range(4):
                                            ch = grp[j0 + jj]
                                            mm(pD, pD[:, jj * 128:(jj + 1) * 128], Bkt[:, ch, :], vtok[:, ch, :], [Bkt, vtok])
                                        act(dl_v[:, :, j0:j0 + 4].rearrange("p c j -> p j c"),
                                            pD[:, 0:512].rearrange("p (j c) -> p j c", j=4), AF.Copy, [pD], [hdel])
                                    ext_g = ext[:, lo:hi + 1]
                                    if rev:
                                        ext_g = ext_g[:, ::-1]
                                    V(lambda e, ml_v=ml_v, ext_g=ext_g, nj=nj: e.tensor_copy(
                                        ml_v, ext_g.unsqueeze(1).to_broadcast([128, 128, nj])), [ext], [hmul])
                                    V(lambda e, ml_v=ml_v: e.memset(ml_v[:, :, 0:1], 0.0), [hmul], [hmul])
                                    stt(dl_v[:, :, 0], st[:], ext[:, grp[0]:grp[0] + 1], dl_v[:, :, 0], ALU.mult, ALU.add,
                                        [st, ext, hdel], [hdel])
                                    V(lambda e, nj=nj: e.tensor_tensor_scan(out=hsc[:, 0:128 * nj], data0=hmul[:, 0:128 * nj],
                                                                           data1=hdel[:, 0:128 * nj], initial=0.0,
                                                                           op0=ALU.mult, op1=ALU.add), [hmul, hdel], [hsc])
                                    act(stb[:, 1:nj + 1, :], sc_v.rearrange("p c j -> p j c"), AF.Copy, [hsc], [stb])
                                    V(lambda e, sc_v=sc_v, nj=nj: e.tensor_copy(st[:], sc_v[:, :, nj - 1]), [hsc], [st])
                                    act(stb_n[:, 0, :], sc_v[:, :, nj - 1], AF.Copy, [hsc], [stb_n])
                                    pS = ps("a")
                                    for j, ch in enumerate(grp):
                                        mm(pS, pS[0:64, j * 64:(j + 1) * 64], Bm[:, ch * 64:(ch + 1) * 64],
                                           Ab[:, ch * 64:(ch + 1) * 64], [Bm, Ab])
                                    Sm = Sms[smi % 2]
                                    smi += 1
                                    V(lambda e, Sm=Sm, pS=pS, nj=nj: e.copy_predicated(
                                        Sm[:, 0:nj * 64].rearrange("p (c j) -> p c j", j=64),
                                        triu[:, d:d + 1, :].to_broadcast([64, nj, 64]),
                                        pS[0:64, 0:nj * 64].rearrange("p (c j) -> p c j", j=64)), [pS, tri, Sm], [Sm])
                                    pO = ps("a")
                                    for j, ch in enumerate(grp):
                                        mm(pO, pO[:, j * 64:(j + 1) * 64], vtok[:, ch, :], Sm[:, j * 64:(j + 1) * 64],
                                           [vtok, Sm], start=True, stop=False)
                                        mm(pO, pO[:, j * 64:(j + 1) * 64], stb[:, j, :], Aq[:, ch * 64:(ch + 1) * 64],
                                           [stb, Aq], start=False, stop=True)
                                    oview = c3(oacc)[:, lo:hi + 1, :]
                                    if rev:
                                        oview = oview[:, ::-1, :]
                                    pov = pO[:, 0:nj * 64].rearrange("p (c j) -> p c j", j=64)
                                    if d == 0:
                                        act(oview, pov, AF.Copy, [pO], [oacc])
                                    else:
                                        tt("vector", oview, oview, pov, ALU.add, [oacc, pO], [oacc])
                            dump("oacc", oacc, oacc[:], [128, T])
                            dump("qs", qs, qs[:], [128, T])
                            dump("vtok", vtok, vtok[:], [64, NCH, 128], BF16)
                            dump("Bkt", Bkt, Bkt[:], [64, NCH, 128], BF16)
                            wt = wnext(C_HG + h * 128)
                            for (t0, n, w) in TBS:
                                p = proj_fm(wt, lambda dc: wt[:, dc, :], t0, n)
                                act(Fb[:, t0:t0 + n], p[:, 0:n], AF.Silu, [p], [Fb])
                                rstd = norm_block(es, oacc, oacc[:, t0:t0 + n].unsqueeze(1), n, nparts=128, nsub=1, denom=128.0)
                                stt(DM[:, t0:t0 + n], oacc[:, t0:t0 + n], vcol("hng", h), rstd[:, 0:n], ALU.mult, ALU.mult,
                                    [oacc, vec, rstd], [DM])
                                tt("vector", ych[:, t0:t0 + n], DM[:, t0:t0 + n], Fb[:, t0:t0 + n], ALU.mult, [DM, Fb], [ych])
                            dma_out(ybr_d[2][:, h * T:(h + 1) * T], ych[:], ych, [ybr_b[2]], par=True)

                def mixer_ssd():
                    with scope() as es:
                        v64 = sb(es, "v64", [64, NV64], F32)
                        dma_in(v64[:], vecs64_d[l], v64)
                        rowb = sb(es, "rowb", [128, NROW], F32)
                        dma_in(rowb[:], rowv_d[l].partition_broadcast(128), rowb)
                        aneg = sb(es, "aneg", [64, 16], F32)
                        o, w = ROWV["alog"]
                        act(aneg[:], rowb[0:64, o:o + 16], AF.Exp, [rowb], [aneg])
                        ts("vector", aneg[:], aneg[:], -1.0, ALU.mult, [aneg], [aneg])
                        odt, _ = ROWV["dtb"]
                        osk, _ = ROWV["skip"]
                        dtt = sb(es, "dtt", [64, 2, NCH, 8], F32)
                        dta = sb(es, "dta", [64, 2, NCH, 8], F32)
                        cumt = sb(es, "cumt", [64, 2, NCH, 8], F32)
                        wdt = sb(es, "wdt", [128, 8, 128], BF16)
                        load_win_cols(wdt, wdt[:], C_DT + 16 - 128, 128)
                        for c0 in range(0, NCH, 12):
                            p = ps()
                            for j in range(12):
                                ch = c0 + j
                                for dc in range(8):
                                    mm(p, p[0:64, j * 16:(j + 1) * 16], hT[:, dc, ch * 64:(ch + 1) * 64], wdt[:, dc, 112:128],
                                       [hT, wdt], start=(dc == 0), stop=(dc == 7))
                            for d in range(2):
                                tt("vector", dtt[:, d, c0:c0 + 12, :],
                                   p[0:64, 0:192].rearrange("p (c d h) -> p d c h", d=2, h=8)[:, d],
                                   rowb[0:64, odt + d * 8:odt + d * 8 + 8].unsqueeze(1).to_broadcast([64, 12, 8]), ALU.add,
                                   [p, rowb], [dtt])
                        ts("vector", dtt[:], dtt[:], 30.0, ALU.min, [dtt], [dtt])
                        act(dtt[:], dtt[:], AF.Exp, [dtt], [dtt])
                        act(dtt[:], dtt[:], AF.Ln, [dtt, cst], [dtt], bias=cst[0:64, 1:2])
                        for d in range(2):
                            tt("vector", dta[:, d], dtt[:, d], aneg[:, d * 8:(d + 1) * 8].unsqueeze(1).to_broadcast([64, NCH, 8]),
                               ALU.mult, [dtt, aneg], [dta])
                            p = ps()
                            mm(p, p[0:64, 0:NCH * 8], tri[:, d, :], dta[:, d].rearrange("p c h -> p (c h)"), [tri, dta])
                            act(cumt[:, d].rearrange("p c h -> p (c h)"), p[0:64, 0:NCH * 8], AF.Copy, [p], [cumt])
                        ws = [sb(es, "wsd", [128, 8, 128], BF16) for _ in range(2)]
                        wi = [0]

                        sspecs = []
                        for g2 in range(2):
                            sspecs += [(C_XS + (g2 * 4 + r2) * 64, 128) for r2 in (0, 2)]
                            sspecs += [(C_B + g2 * 128, 128), (C_C + g2 * 128, 128)]
                            sspecs += [(C_Z + (g2 * 4 + r2) * 64, 128) for r2 in (0, 2)]
                        wq_s = WQ(sspecs, ws, lambda wt, sp: load_win_cols(wt, wt[:, :, 0:sp[1]], sp[0], sp[1]))

                        def wnext(c0, ncols):
                            assert wq_s.specs[wq_s.i] == (c0, ncols)
                            return wq_s.get()
                        yacc = sb(es, "yacc", [64, 4, T], F32)
                        xtok = sb(es, "xtok", [64, NCH, 256], BF16)
                        BT = sb(es, "BT", [128, T], BF16)
                        CTb = sb(es, "CTb", [128, T], BF16)
                        Btok = sb(es, "Btok", [64, NCH, 128], BF16)
                        CBs = sb(es, "CBs", [64, NCH, 64], BF16)
                        ocw, _ = V64["cwx"]
                        ocb, _ = V64["cbx"]
                        ong, _ = V64["ng"]
                        for g_ in range(2):
                            with scope() as e2:
                                u = sb(e2, "u", [128, T], F32)
                                xc = sb(e2, "xc", [128, T], F32)
                                xsb = sb(e2, "xsb", [64, T], BF16)

                                def conv_silu(np_, cw_fn, cb_ap, dst_ap, dst_buf, rd):
                                    act(xc[0:np_, :], u[0:np_, :], AF.Identity, [u] + rd, [xc], bias=cb_ap, scale=cw_fn(2))
                                    for (s0, s1) in ((0, NCTX), (NCTX, T)):
                                        for j in (0, 1, 3):
                                            off = j - 2
                                            a_ = max(s0, s0 - off)
                                            b_ = min(s1, s1 - off)
                                            stt(xc[0:np_, a_:b_], u[0:np_, a_ + off:b_ + off], cw_fn(j), xc[0:np_, a_:b_],
                                                ALU.mult, ALU.add, [u, xc] + rd, [xc])
                                    act(dst_ap, xc[0:np_, :], AF.Silu, [xc], [dst_buf])
                                for r in range(4):
                                    h = g_ * 4 + r
                                    if r % 2 == 0:
                                        wtx = wnext(C_XS + h * 64, 128)
                                    wt = wtx
                                    cof = (r % 2) * 64
                                    for (t0, n, w) in TBS:
                                        p = proj_fm(wt, lambda dc: wt[:, dc, cof:cof + 64], t0, n, M=64)
                                        act(u[0:64, t0:t0 + n], p[0:64, 0:n], AF.Copy, [p], [u])
                                    conv_silu(64, lambda j: v64[:, ocw + j * 8 + h:ocw + j * 8 + h + 1],
                                              v64[:, ocb + h:ocb + h + 1], xc[0:64, :], xc, [v64])
                                    G(lambda e: e.tensor_copy(xsb[:], xc[0:64, :]), [xc], [xsb])
                                    ts("vector", yacc[:, r, :], xc[0:64, :], rowb[0:64, osk + h:osk + h + 1], ALU.mult,
                                       [xc, rowb], [yacc])
                                    for c0 in range(0, NCH, 12):
                                        p = ps()
                                        pb = p[:].bitcast(BF16)
                                        for j in range(12):
                                            ch = c0 + j
                                            PE(lambda e, j=j, ch=ch, pb=pb: e.transpose(pb[0:64, j * 64:(j + 1) * 64],
                                                                                       xsb[:, ch * 64:(ch + 1) * 64], identb[0:64, 0:64]),
                                               [xsb, identb], [p])
                                        act(xtok[:, c0:c0 + 12, r * 64:(r + 1) * 64],
                                            pb[0:64, 0:768].rearrange("p (c k) -> p c k", k=64), AF.Copy, [p], [xtok])
                                for which, dstb in ((0, BT), (1, CTb)):
                                    wt = wnext((C_B if which == 0 else C_C) + g_ * 128, 128)
                                    for (t0, n, w) in TBS:
                                        p = proj_fm(wt, lambda dc: wt[:, dc, :], t0, n)
                                        act(u[:, t0:t0 + n], p[:, 0:n], AF.Copy, [p], [u])
                                    cidx = which * 2 + g_
                                    conv_silu(128, lambda j: vcol("scw", j * 4 + cidx), vcol("scb", cidx), dstb[:], dstb, [vec])
                                for c0 in range(0, NCH, 4):
                                    p = ps()
                                    pb = p[:].bitcast(BF16)
                                    for j in range(4):
                                        ch = c0 + j
                                        PE(lambda e, j=j, ch=ch, pb=pb: e.transpose(pb[0:64, j * 128:(j + 1) * 128],
                                                                                   BT[:, ch * 64:(ch + 1) * 64], identb[:]),
                                           [BT, identb], [p])
                                    act(Btok[:, c0:c0 + 4, :], pb[0:64, 0:512].rearrange("p (c k) -> p c k", k=128), AF.Copy,
                                        [p], [Btok])
                                for c0 in range(0, NCH, 8):
                                    nj = min(8, NCH - c0)
                                    p = ps()
                                    for j in range(nj):
                                        ch = c0 + j
                                        mm(p, p[0:64, j * 64:(j + 1) * 64], BT[:, ch * 64:(ch + 1) * 64],
                                           CTb[:, ch * 64:(ch + 1) * 64], [BT, CTb])
                                    act(CBs[:, c0:c0 + nj, :], p[0:64, 0:nj * 64].rearrange("p (c t) -> p c t", t=64), AF.Copy,
                                        [p], [CBs])
                            with scope() as e2:
                                BS = []
                                TS = []
                                for d_ in range(2):
                                    BS.append(dict(
                                        st4=sb(e2, "st4", [128, 4, 64], F32), stb4=sb(e2, "stb4", [128, 4, 64], BF16)))
                                    TS.append([dict(
                                        prep=sb(e2, "prep", [64, 4, 64], F32), Dd=sb(e2, "Dd", [64, 4, 64], F32),
                                        mdt=sb(e2, "mdt", [64, 4, 64], F32), Mb=sb(e2, "Mb", [64, 4, 64], BF16),
                                        Ec=sb(e2, "Ec", [128, 4, 64], F32), Cs=sb(e2, "Cs", [128, 4, 64], BF16),
                                        wv=sb(e2, "wv", [64, 4], F32), xw=sb(e2, "xw", [64, 4, 64], BF16),
                                        pcs=sb(e2, "pcs", [128, 256], F32)) for _ in range(2)])
                                    V(lambda e, b_=BS[d_]["st4"]: e.memset(b_[:], 0.0), writes=[BS[d_]["st4"]])
                                    V(lambda e, b_=BS[d_]["stb4"]: e.memset(b_[:], 0.0), writes=[BS[d_]["stb4"]])
                                hsl = slice(g_ * 4, g_ * 4 + 4)
                                orders = [list(range(NCH)), [3, 2, 1, 0] + list(range(NCH - 1, 3, -1))]
                                pdk = {}

                                def stepA(d, ch, par_):
                                    T_ = TS[d][par_]
                                    prep, Dd, mdt = T_["prep"], T_["Dd"], T_["mdt"]
                                    Mb, Ec, Cs, wv, xw = T_["Mb"], T_["Ec"], T_["Cs"], T_["wv"], T_["xw"]
                                    lastj = 63 if d == 0 else 0
                                    trib = tri[:, d:d + 1, :].to_broadcast([64, 4, 64])
                                    tsl = slice(ch * 64, (ch + 1) * 64)
                                    tt("vector", prep[:], dta[:, d, ch, hsl].unsqueeze(2).to_broadcast([64, 4, 64]), trib,
                                       ALU.mult, [dta, tri], [prep])
                                    pc = psb[d]
                                    mm(pc, pc[:, 0:256], ones32[0:64, :], prep[:].rearrange("p r t -> p (r t)"), [ones32, prep])
                                    pc_ = T_["pcs"]
                                    V(lambda e, pc_=pc_, pc=pc: e.tensor_copy(pc_[:], pc[:, 0:256]), [pc], [pc_])
                                    pc3 = pc_[:].rearrange("p (r t) -> p r t", t=64)
                                    pc = pc_
                                    tt("vector", Dd[:], pc3[0:64], cumt[:, d, ch, hsl].unsqueeze(2).to_broadcast([64, 4, 64]),
                                       ALU.subtract, [pc, cumt], [Dd])
                                    tt("vector", wv[:], pc3[0:64, :, lastj], cumt[:, d, ch, hsl], ALU.subtract, [pc, cumt], [wv])
                                    act(Ec[:], pc3, AF.Exp, [pc], [Ec])
                                    ts("vector", Dd[:], Dd[:], 0.0, ALU.min, [Dd], [Dd])
                                    act(Dd[:], Dd[:], AF.Exp, [Dd], [Dd])
                                    act(wv[:], wv[:], AF.Exp, [wv], [wv])
                                    tt("gpsimd", mdt[:], dtt[:, d, ch, hsl].unsqueeze(2).to_broadcast([64, 4, 64]), trib,
                                       ALU.mult, [dtt, tri], [mdt])
                                    tt("gpsimd", Cs[:], Ec[:], CTb[:, tsl].unsqueeze(1).to_broadcast([128, 4, 64]), ALU.mult,
                                       [Ec, CTb], [Cs])
                                    tt("vector", Dd[:], Dd[:], mdt[:], ALU.mult, [Dd, mdt], [Dd])
                                    tt("vector", Mb[:], Dd[:], CBs[:, ch:ch + 1, :].to_broadcast([64, 4, 64]), ALU.mult,
                                       [Dd, CBs], [Mb])
                                    tt("vector", wv[:], wv[:], dtt[:, d, ch, hsl], ALU.mult, [wv, dtt], [wv])
                                    tt("gpsimd", xw[:], xtok[:, ch, :].rearrange("p (r k) -> p r k", k=64),
                                       wv[:].unsqueeze(2).to_broadcast([64, 4, 64]), ALU.mult, [xtok, wv], [xw])
                                    pd = psb[2 + d * 2 + par_]
                                    mm(pd, pd[:, 0:256], Btok[:, ch, :], xw[:].rearrange("p r k -> p (r k)"), [Btok, xw])
                                    pdk[(d, par_)] = pd

                                def stepB(d, ch, par_):
                                    B_ = BS[d]
                                    T_ = TS[d][par_]
                                    st4, stb4 = B_["st4"], B_["stb4"]
                                    Mb, Ec, Cs = T_["Mb"], T_["Ec"], T_["Cs"]
                                    lastj = 63 if d == 0 else 0
                                    tsl = slice(ch * 64, (ch + 1) * 64)
                                    pd = pdk[(d, par_)]
                                    po = psb[6 + d]
                                    for r in range(4):
                                        mm(po, po[0:64, r * 64:(r + 1) * 64], xtok[:, ch, r * 64:(r + 1) * 64], Mb[:, r, :],
                                           [xtok, Mb], start=True, stop=False)
                                        mm(po, po[0:64, r * 64:(r + 1) * 64], stb4[:, r, :], Cs[:, r, :], [stb4, Cs],
                                           start=False, stop=True)
                                    tt("vector", st4[:], st4[:], Ec[:, :, lastj:lastj + 1].to_broadcast([128, 4, 64]), ALU.mult,
                                       [st4, Ec], [st4])
                                    tt("vector", st4[:], st4[:], pd[:, 0:256].rearrange("p (r k) -> p r k", k=64), ALU.add,
                                       [st4, pd], [st4])
                                    act(stb4[:], st4[:], AF.Copy, [st4], [stb4])
                                    tt("vector", yacc[:, :, tsl], yacc[:, :, tsl],
                                       po[0:64, 0:256].rearrange("p (r t) -> p r t", t=64), ALU.add, [yacc, po], [yacc])
                                if _os.environ.get("SWP", "1") == "1":
                                    for d_ in range(2):
                                        stepA(d_, orders[d_][0], 0)
                                    for k_ in range(NCH):
                                        if k_ + 1 < NCH:
                                            for d_ in range(2):
                                                stepA(d_, orders[d_][k_ + 1], (k_ + 1) % 2)
                                        for d_ in range(2):
                                            stepB(d_, orders[d_][k_], k_ % 2)
                                else:
                                    for k_ in range(NCH):
                                        for d_ in range(2):
                                            stepA(d_, orders[d_][k_], k_ % 2)
                                            stepB(d_, orders[d_][k_], k_ % 2)
                            with scope() as e2:
                                zs = sb(e2, "zs", [64, 512], F32)
                                ydb = [sb(e2, "ydb", [64, 4, 512], BF16) for _ in range(2)]
                                for r in range(4):
                                    h = g_ * 4 + r
                                    if r % 2 == 0:
                                        wtz = wnext(C_Z + h * 64, 128)
                                    wt = wtz
                                    cof = (r % 2) * 64
                                    for (t0, n, w) in TBS:
                                        p = proj_fm(wt, lambda dc: wt[:, dc, cof:cof + 64], t0, n, M=64)
                                        act(zs[:, 0:n], p[0:64, 0:n], AF.Silu, [p], [zs])
                                        tt("vector", yacc[:, r, t0:t0 + n], yacc[:, r, t0:t0 + n], zs[:, 0:n], ALU.mult,
                                           [yacc, zs], [yacc])
                                for bi, (t0, n, w) in enumerate(TBS):
                                    rstd = norm_block(e2, yacc, yacc[:, :, t0:t0 + n], n, nparts=64, nsub=4, denom=256.0)
                                    yb_ = ydb[bi % 2]
                                    for r in range(4):
                                        h = g_ * 4 + r
                                        stt(yb_[:, r, 0:n], yacc[:, r, t0:t0 + n], v64[:, ong + h:ong + h + 1], rstd[0:64, 0:n],
                                            ALU.mult, ALU.mult, [yacc, v64, rstd], [yb_])
                                    dma_out(ybr_d[3][0:64, g_ * 4 * T:(g_ + 1) * 4 * T].rearrange("p (r t) -> p r t", t=T)[:, :, t0:t0 + n],
                                            yb_[:, :, 0:n], yb_, [ybr_b[3]], par=True)

                def run_mixers():
                    mixer_lru()
                    if stop_after == "lru":
                        return
                    mixer_attn()
                    if stop_after == "attn":
                        return
                    mixer_hgrn()
                    if stop_after == "hgrn":
                        return
                    mixer_ssd()

                if stop_after != "h":
                    run_mixers()
                if s == 0 and l == 0 and nlayers > 1:
                    convert_issue(1)

                if stop_after is None:
                    PARTS = [TBS[0:2], TBS[2:4], TBS[4:5]]
                    with scope() as es:
                        ya = sb(es, "ya", [128, 4, 1024], BF16)
                        yb = sb(es, "yb", [128, 4, 1024], BF16)
                        yc = sb(es, "yc", [128, 4, 1024], BF16)
                        yd = sb(es, "yd", [64, 8, 1024], BF16)
                        wgs = [sb(es, "wg", [128, 8, 4, 128], BF16) for _ in range(2)]
                        wbs = [sb(es, "wb", [128, 3, 4, 128], BF16) for _ in range(2)]
                        wds = [sb(es, "wd", [64, 8, 128], BF16) for _ in range(2)]
                        sgs = [sb(es, "sg", [128, 512], F32) for _ in range(2)]
                        accs = [sb(es, "acc", [128, 512], F32) for _ in range(2)]
                        tmps = [sb(es, "tmpm", [128, 512], F32) for _ in range(2)]
                        mbl = [sb(es, "mbl", [128, 512], BF16) for _ in range(2)]
                        kcnt = [0]
                        wkc = [0]
                        for part in PARTS:
                            pt0 = part[0][0]
                            pn = sum(b_[1] for b_ in part)
                            for bi_, ysb in enumerate((ya, yb, yc)):
                                dma_in(ysb[:, :, 0:pn], ybr_d[bi_][:, 0:4 * T].rearrange("p (a t) -> p a t", t=T)[:, :, pt0:pt0 + pn],
                                       ysb, [ybr_b[bi_]])
                            dma_in(yd[:, :, 0:pn], ybr_d[3][0:64, :].rearrange("p (a t) -> p a t", t=T)[:, :, pt0:pt0 + pn],
                                   yd, [ybr_b[3]])
                            def ld_m(oc):
                                k_ = wkc[0]
                                wkc[0] += 1
                                wg = wgs[k_ % 2]
                                wb = wbs[k_ % 2]
                                wd = wds[k_ % 2]
                                ocs = slice(oc * 128, (oc + 1) * 128)
                                for i in range(4):
                                    load_win_cols(wg, wg[:, :, i, :], C_MG + i * D + oc * 128, 128)
                                dma_in(wb[:, 0], wb_branch[l, 0][:, ocs].rearrange("(kc p) c -> p kc c", p=128), wb, cvb[("branch", l)], par=True)
                                for g_ in range(2):
                                    dma_in(wb[g_ * 64:(g_ + 1) * 64, 1],
                                           wb_branch[l, 1][:, ocs].rearrange("(g r d) c -> g d r c", g=2, r=4)[g_], wb,
                                           cvb[("branch", l)], par=True)
                                dma_in(wb[:, 2], wb_branch[l, 2][:, ocs].rearrange("(kc p) c -> p kc c", p=128), wb, cvb[("branch", l)], par=True)
                                dma_in(wd[:], wb_branch[l, 3][:, ocs].rearrange("(h p) c -> p h c", p=64), wd, cvb[("branch", l)], par=True)
                                return (wg, wb, wd)

                            def cp_m(oc, wts, part=part, pt0=pt0):
                                wg, wb, wd = wts
                                kk_ = [0]
                                for (t0, n, w) in part:
                                    acc = accs[kcnt[0] % 2]
                                    mb_ = mbl[kcnt[0] % 2]
                                    kcnt[0] += 1
                                    lt = t0 - pt0
                                    for i in range(4):
                                        sg = sgs[i % 2]
                                        pg = proj_fm(wg, lambda dc: wg[:, dc, i, :], t0, n)
                                        act(sg[:, 0:n], pg[:, 0:n], AF.Sigmoid, [pg], [sg])
                                        pp = ps()
                                        if i < 3:
                                            ysrc = (ya, yb, yc)[i]
                                            for kc in range(4):
                                                mm(pp, pp[:, 0:n], wb[:, i, kc, :], ysrc[:, kc, lt:lt + n], [wb, ysrc],
                                                   start=(kc == 0), stop=(kc == 3))
                                        else:
                                            for h in range(8):
                                                mm(pp, pp[:, 0:n], wd[:, h, :], yd[:, h, lt:lt + n], [wd, yd],
                                                   start=(h == 0), stop=(h == 7))
                                        if i == 0:
                                            tt("vector", acc[:, 0:n], sg[:, 0:n], pp[:, 0:n], ALU.mult, [sg, pp], [acc])
                                        else:
                                            tmp = tmps[i % 2]
                                            tt("vector", tmp[:, 0:n], sg[:, 0:n], pp[:, 0:n], ALU.mult, [sg, pp], [tmp])
                                            if i < 3:
                                                tt("gpsimd", acc[:, 0:n], acc[:, 0:n], tmp[:, 0:n], ALU.add, [acc, tmp], [acc])
                                            else:
                                                tt("gpsimd", mb_[:, 0:n], acc[:, 0:n], tmp[:, 0:n], ALU.add, [acc, tmp], [mb_])
                                    dma_out(mrg_d[:, oc * T + t0:oc * T + t0 + n], mb_[:, 0:n], mb_, [mrg_b], par=True)
                            pipe(list(range(8)), ld_m, cp_m)
                    with scope() as es:
                        wo = sb(es, "wo", [128, 8, D], BF16)
                        dma_in(wo[:], wrows(wb_out[l]), wo, cvb[("out", l)])
                        ms = [sb(es, "m", [128, 8, 512], F32) for _ in range(2)]
                        mgs = [sb(es, "mg", [128, 8, 512], BF16) for _ in range(2)]
                        xts = [sb(es, "xt", [128, 8, 512], F32) for _ in range(1)]
                        tmps = [sb(es, "tmp", [128, 512], F32) for _ in range(2)]
                        def ld_b(it):
                            bi, (t0, n, w) = it
                            mg = mgs[bi % 2]
                            dma_in(mg[:, :, 0:n], mrg_d.rearrange("p (a t) -> p a t", t=T)[:, :, t0:t0 + n], mg, [mrg_b])
                            return mg

                        def cp_b(it, mg):
                            bi, (t0, n, w) = it
                            m = ms[bi % 2]
                            xt = xts[0]
                            tmp = tmps[bi % 2]
                            for oc2 in range(8):
                                p = ps()
                                for oc in range(8):
                                    mm(p, p[:, 0:n], wo[:, oc, oc2 * 128:(oc2 + 1) * 128], mg[:, oc, 0:n],
                                       [wo, mg], start=(oc == 0), stop=(oc == 7))
                                act(m[:, oc2, 0:n], p[:, 0:n], AF.Copy, [p], [m])
                            rstd = norm_block(es, m, m[:, :, 0:n], n)
                            dma_in(xt[:, :, 0:n], wrows(src_res)[:, :, t0:t0 + n], xt, src_bufs(t0, n))
                            for dc in range(8):
                                stt(tmp[:, 0:n], m[:, dc, 0:n], GS[:, 2, dc, w:w + 1], rstd[:, 0:n], ALU.mult, ALU.mult,
                                    [m, GS, rstd], [tmp])
                                tt("gpsimd", xt[:, dc, 0:n], xt[:, dc, 0:n], tmp[:, 0:n], ALU.add, [xt, tmp], [xt])
                            dma_out(wrows(res_d[s])[:, :, t0:t0 + n], xt[:, :, 0:n], xt, gran(s, t0, n))
                            rstd2 = norm_block(es, xt, xt[:, :, 0:n], n)
                            modulate_to(hT, xt, n, t0, w, 3, 4, rstd2, tmp)
                        pipe(list(enumerate(TBS)), ld_b, cp_b)
                    with scope() as es:
                        aT = sb(es, "aT", [128, 32, 768], BF16)
                        mo = sb(es, "mo", [128, 8, 768], F32)
                        wus = [sb(es, "wu", [128, 8, 128], BF16) for _ in range(2)]
                        wdn = [sb(es, "wdn", [128, 32, 128], BF16) for _ in range(2)]
                        rl = [sb(es, "rl", [128, 512], F32) for _ in range(2)]
                        xts = [sb(es, "xt", [128, 8, 512], F32) for _ in range(1)]
                        tmps = [sb(es, "tmp", [128, 512], F32) for _ in range(2)]
                        kq = [0]
                        pre_u = [None]
                        for si_, sup in enumerate(MLP_SUP):
                            base = sup[0][0]
                            def ld_u(ht):
                                wu = wus[ht % 2]
                                dma_in(wu[:], wrows(wb_up[l])[:, :, ht * 128:(ht + 1) * 128], wu, cvb[("up", l)])
                                return wu

                            def cp_u(ht, wu, sup=sup, base=base):
                                for (t0, n, w) in sup:
                                    p = proj_fm(wu, lambda dc: wu[:, dc, :], t0, n)
                                    r_ = rl[kq[0] % 2]
                                    kq[0] += 1
                                    act(r_[:, 0:n], p[:, 0:n], AF.Relu, [p], [r_])
                                    tt("vector", aT[:, ht, t0 - base:t0 - base + n], r_[:, 0:n], r_[:, 0:n], ALU.mult, [r_], [aT])
                            pre_d = None
                            pipe(list(range(32)), ld_u, cp_u, first=pre_u[0])
                            pre_u[0] = None

                            def ld_d(oc):
                                wd_ = wdn[oc % 2]
                                dma_in(wd_[:], wb_down[l][:, oc * 128:(oc + 1) * 128].rearrange("(ht p) c -> p ht c", p=128), wd_, cvb[("down", l)])
                                return wd_

                            def cp_d(oc, wd_, sup=sup, base=base):
                                for (t0, n, w) in sup:
                                    p = ps()
                                    for ht in range(32):
                                        mm(p, p[:, 0:n], wd_[:, ht, :], aT[:, ht, t0 - base:t0 - base + n], [wd_, aT],
                                           start=(ht == 0), stop=(ht == 31))
                                    act(mo[:, oc, t0 - base:t0 - base + n], p[:, 0:n], AF.Copy, [p], [mo])
                            pipe(list(range(8)), ld_d, cp_d)
                            if si_ + 1 < len(MLP_SUP):
                                pre_u[0] = ld_u(0)
                            for bi, (t0, n, w) in enumerate(sup):
                                xt = xts[0]
                                tmp = tmps[bi % 2]
                                mos = mo[:, :, t0 - base:t0 - base + n]
                                rstd = norm_block(es, mo, mos, n)
                                dma_in(xt[:, :, 0:n], wrows(res_d[s])[:, :, t0:t0 + n], xt, gran(s, t0, n))
                                for dc in range(8):
                                    stt(tmp[:, 0:n], mo[:, dc, t0 - base:t0 - base + n], GS[:, 5, dc, w:w + 1], rstd[:, 0:n],
                                        ALU.mult, ALU.mult, [mo, GS, rstd], [tmp])
                                    tt("gpsimd", xt[:, dc, 0:n], xt[:, dc, 0:n], tmp[:, 0:n], ALU.add, [xt, tmp], [xt])
                                if not last:
                                    dma_out(wrows(res_d[s])[:, :, t0:t0 + n], xt[:, :, 0:n], xt, gran(s, t0, n))
                                elif t0 >= NCTX:
                                    ev = dma_out(wrows(out_d[s])[:, :, t0 - NCTX:t0 - NCTX + n], xt[:, :, 0:n], xt, [out_b], par=True)
                                    out_events.append(ev)

    S.barrier()
    S.wait_events("sync", out_events)
    top.close()
    S.close()
    return nc, S


def _host_inputs(inputs):
    f = np.float32
    x = np.asarray(inputs["x"], f)
    ctx = np.asarray(inputs["ctx"], f)
    c = np.asarray(inputs["c"], f)
    c_ctx = np.asarray(inputs["c_ctx"], f)
    B = x.shape[0]

    def pj(v, p=128):
        v = np.asarray(v, f)
        return np.ascontiguousarray(v.reshape(-1, p).T)

    shared = {}
    for k_ in ("w_ada", "w_in", "w_branch", "w_out", "w_mlp_up", "w_mlp_down"):
        shared[k_] = np.ascontiguousarray(np.asarray(inputs[k_], f))
    w_in = shared["w_in"]
    qk = np.concatenate([w_in[:, :, C_Q:C_Q + 512], w_in[:, :, C_K:C_K + 128]], axis=2)
    perm = np.arange(640).reshape(10, 2, 2, 16)[:, :, ::-1, :].reshape(640)
    shared["wqkp"] = np.ascontiguousarray(qk[:, :, perm])
    vecs = np.zeros((DEPTH, 128, NV), f)
    vecs64 = np.zeros((DEPTH, 64, NV64), f)
    rowv = np.zeros((DEPTH, NROW), f)
    lru_bd = np.zeros((DEPTH, 128, 16, 128), f)
    for l in range(DEPTH):
        def put(name, arr):
            o, w = VEC[name]
            assert arr.shape == (128, w), (name, arr.shape)
            vecs[l][:, o:o + w] = arr
        put("bada", pj(inputs["b_ada"][l]))
        put("gpre", pj(inputs["g_pre_mix"][l]))
        put("gpostmix", pj(inputs["g_post_mix"][l]))
        put("gpremlp", pj(inputs["g_pre_mlp"][l]))
        put("gpostmlp", pj(inputs["g_post_mlp"][l]))
        put("lcw", np.concatenate([pj(inputs["lru_conv_w"][l][j]) for j in range(4)], axis=1))
        put("lcb", pj(inputs["lru_conv_b"][l]))
        put("lrb", np.concatenate([pj(inputs["lru_rec_b"][l][d]) for d in range(2)], axis=1))
        put("lib", np.concatenate([pj(inputs["lru_inp_b"][l][d]) for d in range(2)], axis=1))
        put("llam", np.concatenate([pj(inputs["lru_lambda"][l][d]) for d in range(2)], axis=1))
        put("hl0", pj(inputs["hgrn_lb_logits"][0]))
        put("hl1", pj(inputs["hgrn_lb_logits"][l]))
        put("hng", pj(inputs["hgrn_norm_g"][l]))
        scw = np.asarray(inputs["ssd_conv_w"][l], f)
        scb = np.asarray(inputs["ssd_conv_b"][l], f)
        put("scw", np.concatenate([pj(scw[j, 512:1024]) for j in range(4)], axis=1))
        put("scb", pj(scb[512:1024]))

        def put64(name, arr):
            o, w = V64[name]
            assert arr.shape == (64, w), (name, arr.shape)
            vecs64[l][:, o:o + w] = arr
        put64("cwx", np.concatenate([pj(scw[j, 0:512], 64) for j in range(4)], axis=1))
        put64("cbx", pj(scb[0:512], 64))
        put64("ng", pj(inputs["ssd_norm_g"][l], 64))
        rowv[l, 0:8] = inputs["attn_sink"][l]
        rowv[l, 8:24] = np.asarray(inputs["ssd_dt_bias"][l], f).reshape(16)
        rowv[l, 24:40] = np.asarray(inputs["ssd_a_log"][l], f).reshape(16)
        rowv[l, 40:48] = inputs["ssd_skip"][l]
        for d in range(2):
            for gate, nm_ in enumerate(("lru_rec_w", "lru_inp_w")):
                wm = np.asarray(inputs[nm_][l][d], f)
                for c_ in range(4):
                    for hb in range(2):
                        lru_bd[l, hb * 64:(hb + 1) * 64, (d * 2 + gate) * 4 + c_, hb * 64:(hb + 1) * 64] = wm[2 * c_ + hb]
    shared.update(vecs=vecs, vecs64=vecs64, rowv=rowv, lru_bd=lru_bd)
    shared["ident"] = np.eye(128, dtype=f)
    quarter = 16
    inv_freq = (10000.0 ** (-np.arange(quarter, dtype=np.float64) / quarter))
    t = np.arange(NLAT)
    rows_, cols_ = t // 64, t % 64
    cos = np.ones((64, T), np.float64)
    sin = np.zeros((64, T), np.float64)
    for half, pos in ((0, rows_), (1, cols_)):
        ang = pos[None, :] * inv_freq[:, None]
        cc, ss = np.cos(ang.astype(np.float32)), np.sin(ang.astype(np.float32))
        b0 = half * 32
        cos[b0:b0 + 16, NCTX:] = cc
        cos[b0 + 16:b0 + 32, NCTX:] = cc
        sin[b0:b0 + 16, NCTX:] = -ss
        sin[b0 + 16:b0 + 32, NCTX:] = ss
    rope = np.stack([np.concatenate([cos, cos], 0), np.concatenate([sin, sin], 0)]).astype(f)
    shared["rope_cs"] = np.ascontiguousarray(rope)
    j = np.arange(128)[:, None]
    i = np.arange(128)[None, :]
    am = np.stack([np.tile((j >= i).astype(f), (1, 4)), np.tile((j <= i).astype(f), (1, 4))], axis=1)
    shared["amask"] = np.ascontiguousarray(am)
    a = np.arange(64)[:, None]
    b = np.arange(64)[None, :]
    shared["tri64"] = np.ascontiguousarray(np.stack([(a <= b).astype(f), (a >= b).astype(f)], axis=1))
    in_maps = []
    for core in range(NCORES):
        bs = [core * SPC + k_ for k_ in range(SPC)]
        xin = np.stack([np.concatenate([ctx[b_].T, x[b_].T], axis=1) for b_ in bs]).astype(f)
        cond = np.stack([pj(c[bs[0]]), pj(c[bs[1]]), pj(c_ctx)], axis=2)
        m = dict(shared)
        m["xin"] = np.ascontiguousarray(xin)
        m["cond"] = np.ascontiguousarray(cond)
        in_maps.append(m)
    return in_maps


_CACHE = {}


def kernel(**inputs):
    if "nc" not in _CACHE:
        _CACHE["nc"] = build()[0]
    nc = _CACHE["nc"]
    in_maps = _host_inputs(inputs)
    res = run_bass_kernel_spmd(nc, in_maps, core_ids=list(range(NCORES)))
    outs = []
    for core in range(NCORES):
        o = np.asarray(res.results[core]["out"])
        for k_ in range(SPC):
            outs.append(o[k_].T)
    return np.ascontiguousarray(np.stack(outs).astype(np.float32))
```

```python
import numpy as np
import concourse.bass as bass
import concourse.mybir as mybir
from concourse.bass_utils import run_bass_kernel_spmd
from contextlib import ExitStack, contextmanager

F32 = mybir.dt.float32
BF16 = mybir.dt.bfloat16
AF = mybir.ActivationFunctionType
ALU = mybir.AluOpType

D = 1024
NCTX = 256
NLAT = 2048
T = NCTX + NLAT
NCH = T // 64
NT = T // 128
DEPTH = 2
NCORES = 8
SPC = 2
EPS = 1e-6
TBS = [(0, 256, 1), (256, 512, 0), (768, 512, 0), (1280, 512, 0), (1792, 512, 0)]
MLP_SUP = [[(0, 256, 1), (256, 512, 0)], [(768, 512, 0), (1280, 256, 0)], [(1536, 512, 0), (2048, 256, 0)]]

C_LX, C_LG, C_Q, C_K, C_V = 0, 512, 1024, 1536, 1664
C_HQ, C_HI, C_HFF, C_HFB, C_HG = 1792, 2304, 2816, 3328, 3840
C_Z, C_XS, C_B, C_C, C_DT, C_MG = 4352, 4864, 5376, 5632, 5888, 5904

VEC = {}
_o = 0
for _n, _w in [("bada", 48), ("gpre", 8), ("gpostmix", 8), ("gpremlp", 8), ("gpostmlp", 8), ("lcw", 16), ("lcb", 4),
               ("lrb", 8), ("lib", 8), ("llam", 8), ("hl0", 4), ("hl1", 4), ("hng", 4), ("scw", 16), ("scb", 4)]:
    VEC[_n] = (_o, _w)
    _o += _w
NV = _o
V64 = {}
_o = 0
for _n, _w in [("cwx", 32), ("cbx", 8), ("ng", 8)]:
    V64[_n] = (_o, _w)
    _o += _w
NV64 = _o
ROWV = {"sink": (0, 8), "dtb": (8, 16), "alog": (24, 16), "skip": (40, 8)}
NROW = 48

ENGS = ("tensor", "vector", "scalar", "gpsimd", "sync")
import os as _os
SSD_STAGE = float(_os.environ.get("SSD_STAGE", "9"))


class Buf:
    __slots__ = ("name", "w", "r", "t")

    def __init__(self, name, t=None):
        self.name = name
        self.w = []
        self.r = []
        self.t = t

    def __getitem__(self, k):
        return self.t[k]


class Sched:
    def __init__(self, nc, n_dma_sems=32, n_sw_sems=8):
        self.nc = nc
        self.cnt = {e: 0 for e in ENGS}
        self.clock = {e: {} for e in ENGS}
        self.dsem = []
        self.dnext = 0
        self.n_dma_sems = n_dma_sems
        self.ssem = []
        self.snext = 0
        self.n_sw_sems = n_sw_sems
        self.evclock = {}
        self.ctx = []
        self.semh = {}
        self.ninstr = 0

    def open(self):
        nc = self.nc
        for e in ENGS:
            cm = nc.semaphore("es_" + e)
            self.semh["e_" + e] = cm.__enter__()
            self.ctx.append(cm)
        for i in range(self.n_dma_sems):
            cm = nc.semaphore("ds_%d" % i)
            self.semh["d_%d" % i] = cm.__enter__()
            self.ctx.append(cm)
            self.dsem.append([0, "d_%d" % i])
        for i in range(self.n_sw_sems):
            cm = nc.semaphore("ss_%d" % i)
            self.semh["s_%d" % i] = cm.__enter__()
            self.ctx.append(cm)
            self.ssem.append([0, "s_%d" % i])

    def close(self):
        for cm in reversed(self.ctx):
            cm.__exit__(None, None, None)

    def _emit1(self, eng, waits, fn, key, inc, fuse=False):
        e = getattr(self.nc, eng)
        fused = None
        if fuse and fn is not None and waits and _os.environ.get("FUSE", "1") == "1":
            fused = waits[-1]
            waits = waits[:-1]
        for (k, v) in waits:
            e.wait_ge(self.semh[k], v)
            self.ninstr += 1
        if fn is not None:
            ins = fn(e)
            if fused is not None:
                ins._wait_ge(self.semh[fused[0]], fused[1])
            ins.then_inc(self.semh[key], inc)
            self.ninstr += 1

    @staticmethod
    def _deps(reads, writes, par=False):
        deps = []
        for b in reads:
            deps.extend(b.w)
        for b in writes:
            if par and not b.r:
                continue
            deps.extend(b.w)
            deps.extend(b.r)
        return deps

    def _waits(self, eng, deps, skip_self=False):
        clk = self.clock[eng]
        own = "e_" + eng
        need = {}
        for (k, v) in deps:
            if skip_self and k == own:
                continue
            if clk.get(k, 0) >= v:
                continue
            if need.get(k, 0) < v:
                need[k] = v
        for k, v in need.items():
            ec = self.evclock.get((k, v))
            if ec is not None:
                for kk, vv in ec.items():
                    if clk.get(kk, 0) < vv:
                        clk[kk] = vv
            if clk.get(k, 0) < v:
                clk[k] = v
        return list(need.items())

    def _mark(self, ev, reads, writes, par=False):
        for b in writes:
            if par and not b.r:
                b.w.append(ev)
            else:
                b.w = [ev]
                b.r = []
        for b in reads:
            if b not in writes:
                b.r.append(ev)

    def op(self, eng, fn, reads=(), writes=(), skip_self=False):
        waits = self._waits(eng, self._deps(reads, writes), skip_self)
        self.cnt[eng] += 1
        ev = ("e_" + eng, self.cnt[eng])
        self.evclock[ev] = dict(self.clock[eng])
        self._emit1(eng, waits, fn, ev[0], 1, fuse=(eng != "tensor"))
        self._mark(ev, reads, writes)
        return ev

    def dma(self, eng, fn, reads=(), writes=(), par=False):
        deps = self._deps(reads, writes, par)
        if eng == "gpsimd":
            slot = self.ssem[self.snext]
            self.snext = (self.snext + 1) % len(self.ssem)
        else:
            slot = self.dsem[self.dnext]
            self.dnext = (self.dnext + 1) % len(self.dsem)
        cur, key = slot
        if cur > 0:
            deps.append((key, cur))
        waits = self._waits(eng, deps)
        slot[0] = cur + 16
        ev = (key, cur + 16)
        self.evclock[ev] = dict(self.clock[eng])
        self._emit1(eng, waits, fn, key, 16, fuse=True)
        self._mark(ev, reads, writes, par)
        return ev

    def wait_events(self, eng, events):
        waits = self._waits(eng, list(events))
        self._emit1(eng, waits, None, None, 0)

    def barrier(self):
        evs = [("e_" + e, self.cnt[e]) for e in ENGS if self.cnt[e] > 0]
        evs += [(key, cur) for (cur, key) in self.dsem + self.ssem if cur > 0]
        for e in ENGS:
            self.wait_events(e, evs)


def build(dbg=False, nlayers=DEPTH, nseq=SPC, stop_after=None):
    nc = bass.Bass("TRN2", target_bir_lowering=False)

    def din(name, shape, dt=F32):
        return nc.dram_tensor(name, list(shape), dt, kind="ExternalInput").ap()

    xin = din("xin", [SPC, D, T])
    cond_d = din("cond", [128, 8, 3])
    w_ada = din("w_ada", [DEPTH, D, 6 * D])
    w_in = din("w_in", [DEPTH, D, 10000])
    wqkp = din("wqkp", [DEPTH, D, 640])
    w_branch = din("w_branch", [DEPTH, 4, 512, D])
    w_out = din("w_out", [DEPTH, D, D])
    w_up = din("w_mlp_up", [DEPTH, D, 4 * D])
    w_down = din("w_mlp_down", [DEPTH, 4 * D, D])
    vecs_d = din("vecs", [DEPTH, 128, NV])
    vecs64_d = din("vecs64", [DEPTH, 64, NV64])
    rowv_d = din("rowv", [DEPTH, NROW])
    lru_bd_d = din("lru_bd", [DEPTH, 128, 16, 128])
    ident_d = din("ident", [128, 128])
    rope_d = din("rope_cs", [2, 128, T])
    amask_d = din("amask", [128, 2, 512])
    tri_d = din("tri64", [64, 2, 64])
    out_d = nc.dram_tensor("out", [SPC, D, NLAT], F32, kind="ExternalOutput").ap()
    skind = "ExternalOutput" if dbg else "Internal"
    res_d = nc.dram_tensor("res", [SPC, D, T], F32, kind=skind).ap()
    ybr_d = nc.dram_tensor("ybr", [4, 128, 8 * T], BF16, kind=skind).ap()
    mrg_d = nc.dram_tensor("mrg", [128, 8 * T], BF16, kind=skind).ap()
    h_dbg = nc.dram_tensor("h_dbg", [128, 8 * T], BF16, kind=skind).ap() if dbg else None

    def dscr(name, shape, dt=BF16):
        return nc.dram_tensor(name, list(shape), dt, kind="Internal").ap()

    wb_ada = dscr("wb_ada", [DEPTH, D, 6 * D])
    wb_in = dscr("wb_in", [DEPTH, D, 10000])
    wb_qkp = dscr("wb_qkp", [DEPTH, D, 640])
    wb_branch = dscr("wb_branch", [DEPTH, 4, 512, D])
    wb_out = dscr("wb_out", [DEPTH, D, D])
    wb_up = dscr("wb_up", [DEPTH, D, 4 * D])
    wb_down = dscr("wb_down", [DEPTH, 4 * D, D])
    wb_bd = dscr("wb_bd", [DEPTH, 128, 16 * 128])

    S = Sched(nc)
    S.open()
    cvb = {}

    conv_done = []
    conv_queue = {}

    def convert_plan(l):
        q = []

        def cv(key, dst2d, src2d, rows_per, seg):
            nrows = src2d.shape[0]
            cvb[(key, l)] = []
            for r0 in range(0, nrows, rows_per):
                b = Buf("cv_%s_%d_%d" % (key, l, r0))
                cvb[(key, l)].append(b)
                d_ = dst2d[r0:r0 + rows_per]
                s_ = src2d[r0:r0 + rows_per]
                if seg:
                    d_ = d_.rearrange("r (a c) -> r a c", c=seg)
                    s_ = s_.rearrange("r (a c) -> r a c", c=seg)
                q.append((b, d_, s_))
        cv("ada", wb_ada[l], w_ada[l], 512, 2048)
        cv("in", wb_in[l], w_in[l], 256, 2000)
        cv("qkp", wb_qkp[l], wqkp[l], 1024, 0)
        cv("bd", wb_bd[l], lru_bd_d[l].rearrange("p a c -> p (a c)"), 128, 0)
        cv("branch", wb_branch[l].rearrange("i r c -> (i r) c"), w_branch[l].rearrange("i r c -> (i r) c"), 1024, 0)
        cv("out", wb_out[l], w_out[l], 1024, 0)
        cv("up", wb_up[l], w_up[l], 512, 2048)
        cv("down", wb_down[l], w_down[l], 2048, 0)
        conv_queue[l] = q

    def convert_issue(l, n=100):
        q = conv_queue[l]
        while q and n > 0:
            b, d_, s_ = q.pop(0)
            rd = [conv_done[-3]] if len(conv_done) >= 3 else []
            S.dma("gpsimd", lambda e, d_=d_, s_=s_: e.dma_start(out=d_, in_=s_), reads=rd, writes=[b])
            conv_done.append(b)
            n -= 1

    uid = [0]

    def nm(p):
        uid[0] += 1
        return "%s_%d" % (p, uid[0])

    def sb(es, name, shape, dt=F32):
        return Buf(name, es.enter_context(nc.sbuf_tensor(nm(name), list(shape), dt)))

    @contextmanager
    def scope():
        with ExitStack() as es:
            yield es
            S.barrier()

    top = ExitStack()
    psb = [Buf("ps%d" % i, top.enter_context(nc.psum_tensor("psum%d" % i, [128, 512], F32))) for i in range(8)]
    psi = [0]

    psa = [0]
    psbi = [0]

    def ps(pool=None):
        if pool == "a":
            b = psb[psa[0]]
            psa[0] = (psa[0] + 1) % 4
            return b
        if pool == "b":
            b = psb[4 + psbi[0]]
            psbi[0] = (psbi[0] + 1) % 4
            return b
        b = psb[psi[0]]
        psi[0] = (psi[0] + 1) % 8
        return b

    def V(fn, reads=(), writes=()):
        return S.op("vector", fn, reads, writes)

    def A(fn, reads=(), writes=()):
        return S.op("scalar", fn, reads, writes)

    def G(fn, reads=(), writes=()):
        return S.op("gpsimd", fn, reads, writes)

    def PE(fn, reads=(), writes=()):
        return S.op("tensor", fn, reads, writes, skip_self=True)

    def mm(pbuf, out_ap, lhsT, rhs, reads, start=True, stop=True):
        return PE(lambda e: e.matmul(out_ap, lhsT=lhsT, rhs=rhs, start=start, stop=stop), reads=reads, writes=[pbuf])

    def dma_in(out_ap, in_ap, wbuf, rbufs=(), par=False):
        return S.dma("sync", lambda e: e.dma_start(out=out_ap, in_=in_ap), reads=list(rbufs), writes=[wbuf], par=par)

    def pipe(items, load, compute, first=None):
        nxt = load(items[0]) if first is None else first
        for i_, it in enumerate(items):
            cur = nxt
            if i_ + 1 < len(items):
                nxt = load(items[i_ + 1])
            compute(it, cur)

    class WQ:
        def __init__(self, specs, bufs, loader):
            self.specs, self.bufs, self.loader = list(specs), bufs, loader
            self.i = 0
            self._issue(0)

        def _issue(self, k):
            if k < len(self.specs):
                b = self.bufs[k % len(self.bufs)]
                self.loader(b, self.specs[k])

        def get(self):
            b = self.bufs[self.i % len(self.bufs)]
            self.i += 1
            self._issue(self.i)
            return b

    def dma_cast(out_ap, in_ap, wbuf, rbufs=()):
        return S.dma("gpsimd", lambda e: e.dma_start(out=out_ap, in_=in_ap), reads=list(rbufs), writes=[wbuf])

    def dma_out(out_ap, in_ap, rbuf, wbufs=(), par=False):
        return S.dma("sync", lambda e: e.dma_start(out=out_ap, in_=in_ap), reads=[rbuf], writes=list(wbufs), par=par)

    dumps = {}

    def dump(name, buf, ap, shape, dt=F32):
        if not dbg or name in dumps:
            return
        dumps[name] = 1
        dd = nc.dram_tensor("dump_" + name, list(shape), dt, kind="ExternalOutput").ap()
        S.dma("sync", lambda e: e.dma_start(out=dd, in_=ap), reads=[buf], writes=[Buf("dd_" + name)])

    def act(out_ap, in_ap, func, reads, writes, bias=None, scale=1.0):
        kw = {}
        if bias is not None:
            kw["bias"] = bias
        return A(lambda e: e.activation(out=out_ap, in_=in_ap, func=func, scale=scale, **kw), reads, writes)

    def tt(eng, out_ap, a, b, op, reads, writes):
        return S.op(eng, lambda e: e.tensor_tensor(out=out_ap, in0=a, in1=b, op=op), reads, writes)

    def stt(out_ap, in0, scalar, in1, op0, op1, reads, writes):
        return V(lambda e: e.scalar_tensor_tensor(out=out_ap, in0=in0, scalar=scalar, in1=in1, op0=op0, op1=op1),
                 reads, writes)

    def ts(eng, out_ap, in0, s1, op0, reads, writes, s2=None, op1=None):
        if op1 is None:
            return S.op(eng, lambda e: e.tensor_scalar(out=out_ap, in0=in0, scalar1=s1, scalar2=None, op0=op0),
                        reads, writes)
        return S.op(eng, lambda e: e.tensor_scalar(out=out_ap, in0=in0, scalar1=s1, scalar2=s2, op0=op0, op1=op1),
                    reads, writes)

    def wrows(ap2d):
        return ap2d.rearrange("(kc p) n -> p kc n", p=128)

    res_g = [[Buf("res%d_%d" % (s, i)) for i in range(T // 256)] for s in range(SPC)]
    ybr_b = [Buf("ybr%d" % i) for i in range(4)]
    out_b = Buf("out")
    hdbg_b = Buf("hdbg")
    mrg_b = Buf("mrg")
    out_events = []

    def gran(s, t0, n):
        return res_g[s][t0 // 256:(t0 + n + 255) // 256]

    ones32 = sb(top, "ones32", [128, 128], F32)
    onesb = sb(top, "onesb", [128, 128], BF16)
    identb = sb(top, "identb", [128, 128], BF16)
    cst = sb(top, "cst", [128, 4], F32)
    condb = sb(top, "condb", [128, 8, 3], BF16)
    tri = sb(top, "tri", [64, 2, 64], F32)
    V(lambda e: e.memset(ones32[:], 1.0), writes=[ones32])
    V(lambda e: e.memset(onesb[:], 1.0), writes=[onesb])
    V(lambda e: e.memset(cst[:, 0:1], EPS), writes=[cst])
    V(lambda e: e.memset(cst[:, 1:2], 1.0), writes=[cst])
    V(lambda e: e.memset(cst[:, 2:3], 0.0), writes=[cst])
    dma_cast(identb[:], ident_d, identb)
    dma_in(tri[:], tri_d, tri)
    modT3 = [sb(top, "modT3", [128, 48, 3], F32) for _ in range(DEPTH)]
    with ExitStack() as es0:
        c32 = sb(es0, "c32", [128, 8, 3], F32)
        dma_in(c32[:], cond_d, c32)
        act(condb[:], c32[:], AF.Silu, [c32], [condb])
        S.barrier()

    def norm_block(es_tmp, src, src_ap, n, nparts=128, nsub=8, denom=float(D)):
        sq = sqpool[0]
        sqi[0] += 1
        rs = rspool[rsi[0] % 2]
        rsi[0] += 1
        act(sq[0:nparts, 0:nsub, 0:n], src_ap, AF.Square, [src], [sq])
        p = ps()
        for k in range(nsub):
            mm(p, p[0:nparts, 0:n], ones32[0:nparts, 0:nparts], sq[0:nparts, k, 0:n], [ones32, sq],
               start=(k == 0), stop=(k == nsub - 1))
        act(rs[0:nparts, 0:n], p[0:nparts, 0:n], AF.Sqrt, [p, cst], [rs], bias=cst[0:nparts, 0:1], scale=1.0 / denom)
        V(lambda e: e.reciprocal(rs[0:nparts, 0:n], rs[0:nparts, 0:n]), [rs], [rs])
        return rs

    sqpool = []
    rspool = []
    sqi = [0]
    rsi = [0]

    convert_plan(0)
    convert_plan(1)
    convert_issue(0)
    for s in range(nseq):
        for l in range(nlayers):
            last = (l == DEPTH - 1)
            src_res = xin[s] if l == 0 else res_d[s]
            src_bufs = (lambda t0, n: []) if l == 0 else (lambda t0, n, s=s: gran(s, t0, n))
            with scope() as L:
                hT = sb(L, "hT", [128, 8, T], BF16)
                vec = sb(L, "vec", [128, NV], F32)
                GS = sb(L, "GS", [128, 6, 8, 2], F32)
                sqpool[:] = [sb(L, "sq", [128, 8, 512], F32) for _ in range(1)]
                rspool[:] = [sb(L, "rs", [128, 512], F32) for _ in range(2)]
                dma_in(vec[:], vecs_d[l], vec)

                def vcol(name, j=0, n=1, p0=0, p1=128):
                    o, w = VEC[name]
                    return vec[p0:p1, o + j:o + j + n]

                csl = slice(s, 3, 2 - s)
                if s == 0:
                  with scope() as es:
                    wa = [sb(es, "wada", [128, 8, 1536], BF16) for _ in range(2)]
                    pm = ps()
                    pmv = pm[:, 0:144].rearrange("p (j w) -> p j w", w=3)

                    def ld_ada(piece):
                        wb = wa[piece % 2]
                        dma_in(wb[:], wrows(wb_ada[l])[:, :, piece * 1536:(piece + 1) * 1536], wb, cvb[("ada", l)])
                        return wb

                    def cp_ada(piece, wb):
                        for jj in range(12):
                            j = piece * 12 + jj
                            for dc in range(8):
                                mm(pm, pmv[:, j, :], wb[:, dc, jj * 128:(jj + 1) * 128], condb[:, dc, :],
                                   [wb, condb], start=(dc == 0), stop=(dc == 7))
                    pipe(list(range(4)), ld_ada, cp_ada)
                    o, w = VEC["bada"]
                    tt("vector", modT3[l][:], pmv, vec[:, o:o + 48].unsqueeze(2).to_broadcast([128, 48, 3]), ALU.add,
                       [pm, vec], [modT3[l]])
                if True:
                    modT_b = modT3[l]

                    class _MT:
                        def __getitem__(self, k):
                            return modT_b[:, :, csl][k]
                    modT = _MT()
                    modTb = modT_b

                    def gbc(name):
                        o, w = VEC[name]
                        return vec[:, o:o + 8].unsqueeze(2).to_broadcast([128, 8, 2])
                    stt(GS[:, 0], modT[:, 8:16, :], 1.0, gbc("gpre"), ALU.add, ALU.mult, [modTb, vec], [GS])
                    V(lambda e: e.tensor_copy(GS[:, 1], modT[:, 0:8, :]), [modTb], [GS])
                    tt("vector", GS[:, 2], modT[:, 16:24, :], gbc("gpostmix"), ALU.mult, [modTb, vec], [GS])
                    stt(GS[:, 3], modT[:, 32:40, :], 1.0, gbc("gpremlp"), ALU.add, ALU.mult, [modTb, vec], [GS])
                    V(lambda e: e.tensor_copy(GS[:, 4], modT[:, 24:32, :]), [modTb], [GS])
                    tt("vector", GS[:, 5], modT[:, 40:48, :], gbc("gpostmlp"), ALU.mult, [modTb, vec], [GS])

                def modulate_to(dst, xt, n, t0, w, gi, si, rstd, tmp):
                    for dc in range(8):
                        stt(tmp[:, 0:n], xt[:, dc, 0:n], GS[:, gi, dc, w:w + 1], rstd[:, 0:n], ALU.mult, ALU.mult,
                            [xt, GS, rstd], [tmp])
                        act(dst[:, dc, t0:t0 + n], tmp[:, 0:n], AF.Identity, [tmp, GS], [dst],
                            bias=GS[:, si, dc, w:w + 1])

                with scope() as es:
                    xts = [sb(es, "xt", [128, 8, 512], F32) for _ in range(2)]
                    tmps = [sb(es, "tmp", [128, 512], F32) for _ in range(2)]
                    def ld_x(it):
                        bi, (t0, n, w) = it
                        xt = xts[bi % 2]
                        dma_in(xt[:, :, 0:n], wrows(src_res)[:, :, t0:t0 + n], xt, src_bufs(t0, n))
                        return xt

                    def cp_x(it, xt):
                        bi, (t0, n, w) = it
                        rstd = norm_block(es, xt, xt[:, :, 0:n], n)
                        modulate_to(hT, xt, n, t0, w, 0, 1, rstd, tmps[bi % 2])
                    pipe(list(enumerate(TBS)), ld_x, cp_x)
                if dbg and stop_after == "h":
                    dma_out(h_dbg, hT[:].rearrange("p a t -> p (a t)"), hT, [hdbg_b])

                def load_win_cols(wt, dst_ap, c0, ncols, src=None):
                    if src is None:
                        dma_in(dst_ap, wrows(wb_in[l])[:, :, c0:c0 + ncols], wt, cvb[("in", l)], par=True)
                    else:
                        dma_in(dst_ap, wrows(wb_qkp[l])[:, :, c0:c0 + ncols], wt, cvb[("qkp", l)], par=True)

                def proj_fm(wt, w_ap_fn, t0, n, M=128):
                    p = ps()
                    for dc in range(8):
                        mm(p, p[0:M, 0:n], w_ap_fn(dc), hT[:, dc, t0:t0 + n], [wt, hT], start=(dc == 0), stop=(dc == 7))
                    return p

                def mixer_lru():
                    with scope() as es:
                        bd = sb(es, "bd", [128, 16, 128], BF16)
                        dma_in(bd[:].rearrange("p a c -> p (a c)"), wb_bd[l], bd, cvb[("bd", l)])
                        cs = sb(es, "cs", [128, 8], F32)
                        o, w = VEC["llam"]
                        act(cs[:], vec[:, o:o + 8], AF.Exp, [vec], [cs], scale=-1.0)
                        act(cs[:], cs[:], AF.Ln, [cs, cst], [cs], bias=cst[:, 1:2])
                        ts("vector", cs[:], cs[:], -8.0, ALU.mult, [cs], [cs])
                        ya = sb(es, "ya", [128, 4, T], BF16)
                        wls = [sb(es, "wl", [128, 8, 2, 128], BF16) for _ in range(2)]
                        u = sb(es, "u", [128, T], F32)
                        g = sb(es, "g", [128, T], F32)
                        xc = sb(es, "xc", [128, T], F32)
                        xcb = sb(es, "xcb", [128, T], BF16)
                        r_ = sb(es, "r", [128, T], F32)
                        i_ = sb(es, "i", [128, T], F32)
                        t_ = sb(es, "t", [128, T], F32)
                        h0 = sb(es, "h0", [128, T], F32)
                        def ld_l(wl, c):
                            load_win_cols(wl, wl[:, :, 0, :], C_LX + c * 128, 128)
                            load_win_cols(wl, wl[:, :, 1, :], C_LG + c * 128, 128)
                        wq_l = WQ(range(4), wls, ld_l)
                        for c in range(4):
                            wl = wq_l.get()
                            for (t0, n, w) in TBS:
                                p = proj_fm(wl, lambda dc: wl[:, dc, 0, :], t0, n)
                                act(u[:, t0:t0 + n], p[:, 0:n], AF.Copy, [p], [u])
                                p = proj_fm(wl, lambda dc: wl[:, dc, 1, :], t0, n)
                                act(g[:, t0:t0 + n], p[:, 0:n], AF.Gelu_apprx_tanh, [p], [g])
                            act(xc[:], u[:], AF.Identity, [u, vec], [xc], bias=vcol("lcb", c), scale=vcol("lcw", 2 * 4 + c))
                            for (s0, s1) in ((0, NCTX), (NCTX, T)):
                                for j in (0, 1, 3):
                                    off = j - 2
                                    a_ = max(s0, s0 - off)
                                    b_ = min(s1, s1 - off)
                                    stt(xc[:, a_:b_], u[:, a_ + off:b_ + off], vcol("lcw", j * 4 + c), xc[:, a_:b_],
                                        ALU.mult, ALU.add, [u, vec, xc], [xc])
                            act(xcb[:], xc[:], AF.Copy, [xc], [xcb])
                            for d in range(2):
                                for (t0, n, w) in TBS:
                                    p = ps()
                                    mm(p, p[:, 0:n], bd[:, (d * 2 + 0) * 4 + c, :], xcb[:, t0:t0 + n], [bd, xcb])
                                    act(r_[:, t0:t0 + n], p[:, 0:n], AF.Sigmoid, [p, vec], [r_], bias=vcol("lrb", d * 4 + c))
                                    p = ps()
                                    mm(p, p[:, 0:n], bd[:, (d * 2 + 1) * 4 + c, :], xcb[:, t0:t0 + n], [bd, xcb])
                                    act(i_[:, t0:t0 + n], p[:, 0:n], AF.Sigmoid, [p, vec], [i_], bias=vcol("lib", d * 4 + c))
                                act(r_[:], r_[:], AF.Exp, [r_, cs], [r_], scale=cs[:, d * 4 + c:d * 4 + c + 1])
                                tt("vector", t_[:], r_[:], r_[:], ALU.mult, [r_], [t_])
                                act(t_[:], t_[:], AF.Sqrt, [t_, cst], [t_], bias=cst[:, 1:2], scale=-1.0)
                                tt("gpsimd", i_[:], i_[:], xc[:], ALU.mult, [i_, xc], [i_])
                                tt("vector", i_[:], i_[:], t_[:], ALU.mult, [i_, t_], [i_])
                                if d == 0:
                                    V(lambda e: e.tensor_tensor_scan(out=h0[:], data0=r_[:], data1=i_[:], initial=0.0,
                                                                     op0=ALU.mult, op1=ALU.add), [r_, i_], [h0])
                                else:
                                    V(lambda e: e.tensor_tensor_scan(out=u[:, 0:NCTX][:, ::-1], data0=r_[:, 0:NCTX][:, ::-1],
                                                                     data1=i_[:, 0:NCTX][:, ::-1], initial=0.0,
                                                                     op0=ALU.mult, op1=ALU.add), [r_, i_], [u])
                                    V(lambda e: e.tensor_tensor_scan(out=u[:, NCTX:T][:, ::-1], data0=r_[:, NCTX:T][:, ::-1],
                                                                     data1=i_[:, NCTX:T][:, ::-1], initial=u[:, 0:1],
                                                                     op0=ALU.mult, op1=ALU.add), [r_, i_, u], [u])
                            tt("gpsimd", h0[:], h0[:], u[:], ALU.add, [h0, u], [h0])
                            tt("vector", ya[:, c, :], h0[:], g[:], ALU.mult, [h0, g], [ya])
                        dma_out(ybr_d[0][:, 0:4 * T], ya[:].rearrange("p a t -> p (a t)"), ya, [ybr_b[0]])

                def mixer_attn():
                    with scope() as es:
                        cos = sb(es, "cos", [128, T], F32)
                        sin = sb(es, "sin", [128, T], F32)
                        dma_in(cos[:], rope_d[0], cos)
                        dma_in(sin[:], rope_d[1], sin)
                        am = sb(es, "am", [128, 2, 512], BF16)
                        dma_cast(am[:], amask_d, am)
                        es8 = sb(es, "es8", [128, 8], F32)
                        o, w = ROWV["sink"]
                        dma_in(es8[:], rowv_d[l, o:o + 8].partition_broadcast(128), es8)
                        act(es8[:], es8[:], AF.Exp, [es8], [es8])
                        QT = sb(es, "QT", [128, NT, 512], BF16)
                        KT = sb(es, "KT", [128, T], BF16)
                        Vt = sb(es, "Vt", [128, NT, 128], BF16)
                        yb = sb(es, "yb", [128, 4, T], BF16)
                        wq = [sb(es, "wq", [128, 8, 2, 128], BF16) for _ in range(2)]
                        t1s = [sb(es, "t1", [128, 512], F32) for _ in range(2)]
                        t2s = [sb(es, "t2", [128, 512], F32) for _ in range(2)]
                        cnt = 0
                        wqa = sb(es, "wqa", [128, 8, 2, 512], BF16)
                        load_win_cols(wqa, wqa[:, :, 0, :], C_Q, 512)
                        load_win_cols(wqa, wqa[:, :, 1, :], 0, 512, src=wqkp[l])

                        def ld_q(wt, r):
                            if r < 4:
                                for wh in range(2):
                                    for g2 in range(2):
                                        G(lambda e, wh=wh, g2=g2: e.tensor_copy(
                                            wt[:, :, wh, g2 * 64:(g2 + 1) * 64],
                                            wqa[:, :, wh, (g2 * 4 + r) * 64:(g2 * 4 + r + 1) * 64]), [wqa], [wt])
                            elif r == 4:
                                load_win_cols(wt, wt[:, :, 0, :], C_K, 128)
                                load_win_cols(wt, wt[:, :, 1, :], 512, 128, src=wqkp[l])
                            else:
                                load_win_cols(wt, wt[:, :, 0, :], C_V, 128)
                        wq_a = WQ(range(6), wq, ld_q)
                        for r in range(5):
                            wt = wq_a.get()
                            for (t0, n, w) in TBS:
                                t1 = t1s[cnt % 2]
                                t2 = t2s[cnt % 2]
                                cnt += 1
                                p = proj_fm(wt, lambda dc: wt[:, dc, 0, :], t0, n)
                                tt("vector", t1[:, 0:n], p[:, 0:n], cos[:, t0:t0 + n], ALU.mult, [p, cos], [t1])
                                p = proj_fm(wt, lambda dc: wt[:, dc, 1, :], t0, n)
                                tt("vector", t2[:, 0:n], p[:, 0:n], sin[:, t0:t0 + n], ALU.mult, [p, sin], [t2])
                                if r < 4:
                                    dst = QT[:, t0 // 128:(t0 + n) // 128, r * 128:(r + 1) * 128]
                                    tt("gpsimd", dst, t1[:, 0:n].rearrange("p (b q) -> p b q", q=128),
                                       t2[:, 0:n].rearrange("p (b q) -> p b q", q=128), ALU.add, [t1, t2], [QT])
                                else:
                                    tt("gpsimd", KT[:, t0:t0 + n], t1[:, 0:n], t2[:, 0:n], ALU.add, [t1, t2], [KT])
                        wv = wq_a.get()
                        for i in range(NT):
                            p = ps()
                            for dc in range(8):
                                mm(p, p[:, 0:128], hT[:, dc, i * 128:(i + 1) * 128], wv[:, dc, 0, :], [hT, wv],
                                   start=(dc == 0), stop=(dc == 7))
                            act(Vt[:, i, :], p[:, 0:128], AF.Copy, [p], [Vt])
                        Es = [sb(es, "E", [128, 512], BF16) for _ in range(3)]
                        den = sb(es, "den", [128, 512], F32)
                        ecnt = 0
                        items = []
                        for qb in range(NT):
                            keys = [(0, None), (1, None)]
                            if qb >= 2:
                                if qb - 1 >= 2:
                                    keys.append((qb - 1, 0))
                                keys.append((qb, None))
                                if qb + 1 < NT:
                                    keys.append((qb + 1, 1))
                            for g_ in range(2):
                                for ki, (kt, mk) in enumerate(keys):
                                    items.append((qb, g_, ki, kt, mk, len(keys)))
                        ecn = [0]

                        def stage1(it):
                            qb, g_, ki, kt, mk, nk = it
                            hs = slice(g_ * 64, (g_ + 1) * 64)
                            pS = ps("b")
                            mm(pS, pS[:, :], KT[hs, kt * 128:(kt + 1) * 128], QT[hs, qb, :], [KT, QT])
                            E = Es[ecn[0] % 3]
                            ecn[0] += 1
                            act(E[:], pS[:], AF.Exp, [pS], [E], scale=0.125)
                            if mk is not None:
                                tt("vector", E[:], E[:], am[:, mk, :], ALU.mult, [E, am], [E])
                            return E
                        v3 = lambda ap: ap.rearrange("p (r q) -> p r q", q=128)
                        Enext = stage1(items[0])
                        pv = p1 = None
                        for ii, it in enumerate(items):
                            E = Enext
                            if ii + 1 < len(items):
                                Enext = stage1(items[ii + 1])
                            qb, g_, ki, kt, mk, nk = it
                            hs = slice(g_ * 64, (g_ + 1) * 64)
                            if ki == 0:
                                pv = ps("a")
                                p1 = ps("a")
                            mm(pv, pv[:, :], Vt[:, kt, :], E[:], [Vt, E], start=(ki == 0), stop=(ki == nk - 1))
                            mm(p1, p1[:, :], onesb[:], E[:], [onesb, E], start=(ki == 0), stop=(ki == nk - 1))
                            if ki == nk - 1:
                                tt("vector", v3(den[hs, :]), v3(p1[hs, :]),
                                   es8[hs, g_ * 4:(g_ + 1) * 4].unsqueeze(2).to_broadcast([64, 4, 128]), ALU.add,
                                   [p1, es8], [den])
                                V(lambda e, hs=hs: e.reciprocal(den[hs, :], den[hs, :]), [den], [den])
                                tt("vector", yb[hs, :, qb * 128:(qb + 1) * 128], v3(pv[hs, :]), v3(den[hs, :]), ALU.mult,
                                   [pv, den], [yb])
                        dma_out(ybr_d[1][:, 0:4 * T], yb[:].rearrange("p a t -> p (a t)"), yb, [ybr_b[1]])

                def mixer_hgrn():
                    with scope() as es:
                        lbv = sb(es, "lbv", [128, 4], F32)
                        omlb = sb(es, "omlb", [128, 4], F32)
                        if l == 0:
                            V(lambda e: e.memset(lbv[:], 0.0), writes=[lbv])
                        else:
                            o0, _ = VEC["hl0"]
                            o1, _ = VEC["hl1"]
                            tt("vector", lbv[:], vec[:, o1:o1 + 4], vec[:, o0:o0 + 4], ALU.subtract, [vec], [lbv])
                            act(lbv[:], lbv[:], AF.Sigmoid, [lbv], [lbv])
                        ts("vector", omlb[:], lbv[:], -1.0, ALU.mult, [lbv], [omlb], s2=1.0, op1=ALU.add)
                        rm = sb(es, "rm", [128, T + 64], BF16)
                        V(lambda e: e.memset(rm[:], 1.0), writes=[rm])
                        V(lambda e: e.memset(rm[:].rearrange("p (c j) -> p c j", j=64)[:, :, 0:1], 0.0), [rm], [rm])
                        ws = [sb(es, "wh", [128, 8, 128], BF16) for _ in range(2)]
                        wi = [0]

                        hspecs = []
                        for h_ in range(4):
                            hspecs += [C_HQ + h_ * 128, C_HI + h_ * 128, C_HFF + h_ * 128, C_HFB + h_ * 128, C_HG + h_ * 128]
                        wq_h = WQ(hspecs, ws, lambda wt, c0: load_win_cols(wt, wt[:], c0, 128))

                        def wnext(c0):
                            assert wq_h.specs[wq_h.i] == c0
                            return wq_h.get()
                        qs = sb(es, "qs", [128, T], F32)
                        vtok = sb(es, "vtok", [64, NCH, 128], BF16)
                        oacc = sb(es, "oacc", [128, T], F32)
                        Fb = sb(es, "Fb", [128, T], F32)
                        Fb2 = sb(es, "Fb2", [128, T], F32)
                        trim = sb(es, "trim", [64, 2, 512], F32)
                        for d2 in range(2):
                            V(lambda e, d2=d2: e.tensor_copy(trim[:, d2, :].rearrange("p (c j) -> p c j", j=64),
                                                             tri[:, d2:d2 + 1, :].to_broadcast([64, 8, 64])), [tri], [trim])
                        trimu = trim[:].bitcast(mybir.dt.uint32)
                        KK = sb(es, "KK", [128, T], F32)
                        CUM = sb(es, "CUM", [128, T], F32)
                        DM = sb(es, "DM", [128, T], F32)
                        Ab = sb(es, "Ab", [128, T], BF16)
                        Bm = sb(es, "Bm", [128, T], BF16)
                        Aq = sb(es, "Aq", [128, T], BF16)
                        Bk = sb(es, "Bk", [128, T], BF16)
                        Bkt = sb(es, "Bkt", [64, NCH, 128], BF16)
                        ext = sb(es, "ext", [128, NCH], F32)
                        Sms = [sb(es, "Sm", [64, 512], BF16) for _ in range(2)]
                        st = sb(es, "st", [128, 128], F32)
                        stbh = [sb(es, "stbh", [128, 9, 128], BF16) for _ in range(2)]
                        hdel = sb(es, "hdel", [128, 128 * 8], F32)
                        hmul = sb(es, "hmul", [128, 128 * 8], F32)
                        hsc = sb(es, "hsc", [128, 128 * 8], F32)
                        hk = [0]
                        if dbg:
                            print("SBUF remaining in HGRN scope:", nc.sbuf_bytes_remaining, flush=True)
                        ych = sb(es, "ych", [128, T], BF16)
                        c3 = lambda b: b[:, 0:T].rearrange("p (c j) -> p c j", j=64)
                        smi = 0
                        for h in range(4):
                            wt = wnext(C_HQ + h * 128)
                            for (t0, n, w) in TBS:
                                p = proj_fm(wt, lambda dc: wt[:, dc, :], t0, n)
                                act(qs[:, t0:t0 + n], p[:, 0:n], AF.Silu, [p], [qs])
                            wt = wnext(C_HI + h * 128)
                            for c0 in range(0, NCH, 4):
                                p = ps()
                                for j in range(4):
                                    ch = c0 + j
                                    for dc in range(8):
                                        mm(p, p[0:64, j * 128:(j + 1) * 128], hT[:, dc, ch * 64:(ch + 1) * 64], wt[:, dc, :],
                                           [hT, wt], start=(dc == 0), stop=(dc == 7))
                                act(vtok[:, c0:c0 + 4, :], p[0:64, :].rearrange("p (c v) -> p c v", v=128), AF.Copy, [p], [vtok])
                            for d in range(2):
                                wt = wnext((C_HFF if d == 0 else C_HFB) + h * 128)
                                for (t0, n, w) in TBS:
                                    p = proj_fm(wt, lambda dc: wt[:, dc, :], t0, n)
                                    act(Fb[:, t0:t0 + n], p[:, 0:n], AF.Sigmoid, [p], [Fb])
                                ts("vector", Fb[:], Fb[:], omlb[:, h:h + 1], ALU.mult, [Fb, omlb, lbv], [Fb],
                                   s2=lbv[:, h:h + 1], op1=ALU.add)
                                ts("gpsimd", KK[:], Fb[:], -1.0, ALU.mult, [Fb], [KK], s2=1.0, op1=ALU.add)
                                act(Fb[:], Fb[:], AF.Ln, [Fb], [Fb])
                                if d == 0:
                                    V(lambda e: e.tensor_tensor_scan(out=CUM[:], data0=rm[:, 0:T], data1=Fb[:], initial=0.0,
                                                                     op0=ALU.mult, op1=ALU.add), [rm, Fb], [CUM])
                                    mid, lastj = 31, 63
                                else:
                                    V(lambda e: e.tensor_tensor_scan(out=CUM[:, ::-1], data0=rm[:, 1:T + 1][:, ::-1],
                                                                     data1=Fb[:, ::-1], initial=0.0,
                                                                     op0=ALU.mult, op1=ALU.add), [rm, Fb], [CUM])
                                    mid, lastj = 32, 0
                                cum3 = c3(CUM)
                                dump("lf%d" % d, Fb, Fb[:], [128, T])
                                dump("kk%d" % d, KK, KK[:], [128, T])
                                dump("cum%d" % d, CUM, CUM[:], [128, T])
                                tt("vector", c3(DM), cum3, cum3[:, :, mid:mid + 1].to_broadcast([128, NCH, 64]), ALU.subtract,
                                   [CUM], [DM])
                                act(Fb[:], DM[:], AF.Exp, [DM], [Fb])
                                tt("vector", Ab[:], qs[:], Fb[:], ALU.mult, [qs, Fb], [Ab])
                                act(Fb2[:], DM[:], AF.Exp, [DM], [Fb2], scale=-1.0)
                                tt("vector", Bm[:], KK[:], Fb2[:], ALU.mult, [KK, Fb2], [Bm])
                                act(Fb[:], CUM[:], AF.Exp, [CUM], [Fb])
                                tt("vector", Aq[:], qs[:], Fb[:], ALU.mult, [qs, Fb], [Aq])
                                tt("vector", c3(DM), cum3[:, :, lastj:lastj + 1].to_broadcast([128, NCH, 64]), cum3,
                                   ALU.subtract, [CUM], [DM])
                                act(Fb2[:], DM[:], AF.Exp, [DM], [Fb2])
                                tt("vector", Bk[:], KK[:], Fb2[:], ALU.mult, [KK, Fb2], [Bk])
                                act(ext[:], cum3[:, :, lastj], AF.Exp, [CUM], [ext])
                                dump("ext%d" % d, ext, ext[:], [128, NCH])
                                dump("Ab%d" % d, Ab, Ab[:], [128, T], BF16)
                                dump("Bm%d" % d, Bm, Bm[:], [128, T], BF16)
                                dump("Aq%d" % d, Aq, Aq[:], [128, T], BF16)
                                dump("Bk%d" % d, Bk, Bk[:], [128, T], BF16)
                                for c0 in range(0, NCH, 4):
                                    p = ps()
                                    pb = p[:].bitcast(BF16)
                                    for j in range(4):
                                        ch = c0 + j
                                        PE(lambda e, j=j, ch=ch: e.transpose(pb[0:64, j * 128:(j + 1) * 128],
                                                                            Bk[:, ch * 64:(ch + 1) * 64], identb[:]),
                                           [Bk, identb], [p])
                                    act(Bkt[:, c0:c0 + 4, :], pb[0:64, 0:512].rearrange("p (c k) -> p c k", k=128), AF.Copy,
                                        [p], [Bkt])
                                V(lambda e: e.memset(st[:], 0.0), writes=[st])
                                V(lambda e, b_=stbh[hk[0] % 2]: e.memset(b_[:, 0, :], 0.0), writes=[stbh[hk[0] % 2]])
                                for Sm_ in Sms:
                                    V(lambda e, Sm_=Sm_: e.memset(Sm_[:], 0.0), writes=[Sm_])
                                triu = tri[:].bitcast(mybir.dt.uint32)
                                if d == 0:
                                    groups = [list(range(0, 4))] + [list(range(c, c + 8)) for c in range(4, NCH, 8)]
                                else:
                                    groups = [[3, 2, 1, 0]] + [list(range(c + 7, c - 1, -1)) for c in range(NCH - 8, 3, -8)]
                                for grp in groups:
                                    nj = len(grp)
                                    lo, hi = min(grp), max(grp)
                                    rev = grp[0] > grp[-1]
                                    stb = stbh[hk[0] % 2]
                                    stb_n = stbh[(hk[0] + 1) % 2]
                                    hk[0] += 1
                                    dl_v = hdel[:, 0:128 * nj].rearrange("p (c j) -> p c j", j=nj)
                                    ml_v = hmul[:, 0:128 * nj].rearrange("p (c j) -> p c j", j=nj)
                                    sc_v = hsc[:, 0:128 * nj].rearrange("p (c j) -> p c j", j=nj)
                                    for j0 in range(0, nj, 4):
                                        pD = ps("b")
                                        for jj in range(4):
                                            ch = grp[j0 + jj]
                                            mm(pD, pD[:, jj * 128:(jj + 1) * 128], Bkt[:, ch, :], vtok[:, ch, :], [Bkt, vtok])
                                        act(dl_v[:, :, j0:j0 + 4].rearrange("p c j -> p j c"),
                                            pD[:, 0:512].rearrange("p (j c) -> p j c", j=4), AF.Copy, [pD], [hdel])
                                    ext_g = ext[:, lo:hi + 1]
                                    if rev:
                                        ext_g = ext_g[:, ::-1]
                                    V(lambda e, ml_v=ml_v, ext_g=ext_g, nj=nj: e.tensor_copy(
                                        ml_v, ext_g.unsqueeze(1).to_broadcast([128, 128, nj])), [ext], [hmul])
                                    V(lambda e, ml_v=ml_v: e.memset(ml_v[:, :, 0:1], 0.0), [hmul], [hmul])
                                    stt(dl_v[:, :, 0], st[:], ext[:, grp[0]:grp[0] + 1], dl_v[:, :, 0], ALU.mult, ALU.add,
                                        [st, ext, hdel], [hdel])
                                    V(lambda e, nj=nj: e.tensor_tensor_scan(out=hsc[:, 0:128 * nj], data0=hmul[:, 0:128 * nj],
                                                                           data1=hdel[:, 0:128 * nj], initial=0.0,
                                                                           op0=ALU.mult, op1=ALU.add), [hmul, hdel], [hsc])
                                    act(stb[:, 1:nj + 1, :], sc_v.rearrange("p c j -> p j c"), AF.Copy, [hsc], [stb])
                                    V(lambda e, sc_v=sc_v, nj=nj: e.tensor_copy(st[:], sc_v[:, :, nj - 1]), [hsc], [st])
                                    act(stb_n[:, 0, :], sc_v[:, :, nj - 1], AF.Copy, [hsc], [stb_n])
                                    pS = ps("a")
                                    for j, ch in enumerate(grp):
                                        mm(pS, pS[0:64, j * 64:(j + 1) * 64], Bm[:, ch * 64:(ch + 1) * 64],
                                           Ab[:, ch * 64:(ch + 1) * 64], [Bm, Ab])
                                    Sm = Sms[smi % 2]
                                    smi += 1
                                    V(lambda e, Sm=Sm, pS=pS, nj=nj: e.copy_predicated(
                                        Sm[:, 0:nj * 64], trimu[:, d, 0:nj * 64], pS[0:64, 0:nj * 64]), [pS, trim, Sm], [Sm])
                                    pO = ps("a")
                                    for j, ch in enumerate(grp):
                                        mm(pO, pO[:, j * 64:(j + 1) * 64], vtok[:, ch, :], Sm[:, j * 64:(j + 1) * 64],
                                           [vtok, Sm], start=True, stop=False)
                                        mm(pO, pO[:, j * 64:(j + 1) * 64], stb[:, j, :], Aq[:, ch * 64:(ch + 1) * 64],
                                           [stb, Aq], start=False, stop=True)
                                    oview = c3(oacc)[:, lo:hi + 1, :]
                                    if rev:
                                        oview = oview[:, ::-1, :]
                                    pov = pO[:, 0:nj * 64].rearrange("p (c j) -> p c j", j=64)
                                    if d == 0:
                                        act(oview, pov, AF.Copy, [pO], [oacc])
                                    else:
                                        tt("vector", oview, oview, pov, ALU.add, [oacc, pO], [oacc])
                            dump("oacc", oacc, oacc[:], [128, T])
                            dump("qs", qs, qs[:], [128, T])
                            dump("vtok", vtok, vtok[:], [64, NCH, 128], BF16)
                            dump("Bkt", Bkt, Bkt[:], [64, NCH, 128], BF16)
                            wt = wnext(C_HG + h * 128)
                            for (t0, n, w) in TBS:
                                p = proj_fm(wt, lambda dc: wt[:, dc, :], t0, n)
                                act(Fb[:, t0:t0 + n], p[:, 0:n], AF.Silu, [p], [Fb])
                                rstd = norm_block(es, oacc, oacc[:, t0:t0 + n].unsqueeze(1), n, nparts=128, nsub=1, denom=128.0)
                                stt(DM[:, t0:t0 + n], oacc[:, t0:t0 + n], vcol("hng", h), rstd[:, 0:n], ALU.mult, ALU.mult,
                                    [oacc, vec, rstd], [DM])
                                tt("vector", ych[:, t0:t0 + n], DM[:, t0:t0 + n], Fb[:, t0:t0 + n], ALU.mult, [DM, Fb], [ych])
                            dma_out(ybr_d[2][:, h * T:(h + 1) * T], ych[:], ych, [ybr_b[2]], par=True)

                def mixer_ssd():
                    with scope() as es:
                        v64 = sb(es, "v64", [64, NV64], F32)
                        dma_in(v64[:], vecs64_d[l], v64)
                        rowb = sb(es, "rowb", [128, NROW], F32)
                        dma_in(rowb[:], rowv_d[l].partition_broadcast(128), rowb)
                        aneg = sb(es, "aneg", [64, 16], F32)
                        o, w = ROWV["alog"]
                        act(aneg[:], rowb[0:64, o:o + 16], AF.Exp, [rowb], [aneg])
                        ts("vector", aneg[:], aneg[:], -1.0, ALU.mult, [aneg], [aneg])
                        odt, _ = ROWV["dtb"]
                        osk, _ = ROWV["skip"]
                        dtt = sb(es, "dtt", [64, 2, NCH, 8], F32)
                        dta = sb(es, "dta", [64, 2, NCH, 8], F32)
                        cumt = sb(es, "cumt", [64, 2, NCH, 8], F32)
                        wdt = sb(es, "wdt", [128, 8, 128], BF16)
                        load_win_cols(wdt, wdt[:], C_DT + 16 - 128, 128)
                        for c0 in range(0, NCH, 12):
                            p = ps()
                            for j in range(12):
                                ch = c0 + j
                                for dc in range(8):
                                    mm(p, p[0:64, j * 16:(j + 1) * 16], hT[:, dc, ch * 64:(ch + 1) * 64], wdt[:, dc, 112:128],
                                       [hT, wdt], start=(dc == 0), stop=(dc == 7))
                            for d in range(2):
                                tt("vector", dtt[:, d, c0:c0 + 12, :],
                                   p[0:64, 0:192].rearrange("p (c d h) -> p d c h", d=2, h=8)[:, d],
                                   rowb[0:64, odt + d * 8:odt + d * 8 + 8].unsqueeze(1).to_broadcast([64, 12, 8]), ALU.add,
                                   [p, rowb], [dtt])
                        ts("vector", dtt[:], dtt[:], 30.0, ALU.min, [dtt], [dtt])
                        act(dtt[:], dtt[:], AF.Exp, [dtt], [dtt])
                        act(dtt[:], dtt[:], AF.Ln, [dtt, cst], [dtt], bias=cst[0:64, 1:2])
                        for d in range(2):
                            tt("vector", dta[:, d], dtt[:, d], aneg[:, d * 8:(d + 1) * 8].unsqueeze(1).to_broadcast([64, NCH, 8]),
                               ALU.mult, [dtt, aneg], [dta])
                            p = ps()
                            mm(p, p[0:64, 0:NCH * 8], tri[:, d, :], dta[:, d].rearrange("p c h -> p (c h)"), [tri, dta])
                            act(cumt[:, d].rearrange("p c h -> p (c h)"), p[0:64, 0:NCH * 8], AF.Copy, [p], [cumt])
                        ws = [sb(es, "wsd", [128, 8, 128], BF16) for _ in range(2)]
                        wi = [0]

                        sspecs = []
                        for g2 in range(2):
                            sspecs += [(C_XS + (g2 * 4 + r2) * 64, 128) for r2 in (0, 2)]
                            sspecs += [(C_B + g2 * 128, 128), (C_C + g2 * 128, 128)]
                            sspecs += [(C_Z + (g2 * 4 + r2) * 64, 128) for r2 in (0, 2)]
                        wq_s = WQ(sspecs, ws, lambda wt, sp: load_win_cols(wt, wt[:, :, 0:sp[1]], sp[0], sp[1]))

                        def wnext(c0, ncols):
                            assert wq_s.specs[wq_s.i] == (c0, ncols)
                            return wq_s.get()
                        yacc = sb(es, "yacc", [64, 4, T], F32)
                        xtok = sb(es, "xtok", [64, NCH, 256], BF16)
                        BT = sb(es, "BT", [128, T], BF16)
                        CTb = sb(es, "CTb", [128, T], BF16)
                        Btok = sb(es, "Btok", [64, NCH, 128], BF16)
                        CBs = sb(es, "CBs", [64, NCH, 64], BF16)
                        ocw, _ = V64["cwx"]
                        ocb, _ = V64["cbx"]
                        ong, _ = V64["ng"]
                        for g_ in range(2):
                            with scope() as e2:
                                u = sb(e2, "u", [128, T], F32)
                                xc = sb(e2, "xc", [128, T], F32)
                                xsb = sb(e2, "xsb", [64, T], BF16)

                                def conv_silu(np_, cw_fn, cb_ap, dst_ap, dst_buf, rd):
                                    act(xc[0:np_, :], u[0:np_, :], AF.Identity, [u] + rd, [xc], bias=cb_ap, scale=cw_fn(2))
                                    for (s0, s1) in ((0, NCTX), (NCTX, T)):
                                        for j in (0, 1, 3):
                                            off = j - 2
                                            a_ = max(s0, s0 - off)
                                            b_ = min(s1, s1 - off)
                                            stt(xc[0:np_, a_:b_], u[0:np_, a_ + off:b_ + off], cw_fn(j), xc[0:np_, a_:b_],
                                                ALU.mult, ALU.add, [u, xc] + rd, [xc])
                                    act(dst_ap, xc[0:np_, :], AF.Silu, [xc], [dst_buf])
                                for r in range(4):
                                    h = g_ * 4 + r
                                    if r % 2 == 0:
                                        wtx = wnext(C_XS + h * 64, 128)
                                    wt = wtx
                                    cof = (r % 2) * 64
                                    for (t0, n, w) in TBS:
                                        p = proj_fm(wt, lambda dc: wt[:, dc, cof:cof + 64], t0, n, M=64)
                                        act(u[0:64, t0:t0 + n], p[0:64, 0:n], AF.Copy, [p], [u])
                                    conv_silu(64, lambda j: v64[:, ocw + j * 8 + h:ocw + j * 8 + h + 1],
                                              v64[:, ocb + h:ocb + h + 1], xc[0:64, :], xc, [v64])
                                    G(lambda e: e.tensor_copy(xsb[:], xc[0:64, :]), [xc], [xsb])
                                    ts("vector", yacc[:, r, :], xc[0:64, :], rowb[0:64, osk + h:osk + h + 1], ALU.mult,
                                       [xc, rowb], [yacc])
                                    for c0 in range(0, NCH, 12):
                                        p = ps()
                                        pb = p[:].bitcast(BF16)
                                        for j in range(12):
                                            ch = c0 + j
                                            PE(lambda e, j=j, ch=ch, pb=pb: e.transpose(pb[0:64, j * 64:(j + 1) * 64],
                                                                                       xsb[:, ch * 64:(ch + 1) * 64], identb[0:64, 0:64]),
                                               [xsb, identb], [p])
                                        act(xtok[:, c0:c0 + 12, r * 64:(r + 1) * 64],
                                            pb[0:64, 0:768].rearrange("p (c k) -> p c k", k=64), AF.Copy, [p], [xtok])
                                for which, dstb in ((0, BT), (1, CTb)):
                                    wt = wnext((C_B if which == 0 else C_C) + g_ * 128, 128)
                                    for (t0, n, w) in TBS:
                                        p = proj_fm(wt, lambda dc: wt[:, dc, :], t0, n)
                                        act(u[:, t0:t0 + n], p[:, 0:n], AF.Copy, [p], [u])
                                    cidx = which * 2 + g_
                                    conv_silu(128, lambda j: vcol("scw", j * 4 + cidx), vcol("scb", cidx), dstb[:], dstb, [vec])
                                for c0 in range(0, NCH, 4):
                                    p = ps()
                                    pb = p[:].bitcast(BF16)
                                    for j in range(4):
                                        ch = c0 + j
                                        PE(lambda e, j=j, ch=ch, pb=pb: e.transpose(pb[0:64, j * 128:(j + 1) * 128],
                                                                                   BT[:, ch * 64:(ch + 1) * 64], identb[:]),
                                           [BT, identb], [p])
                                    act(Btok[:, c0:c0 + 4, :], pb[0:64, 0:512].rearrange("p (c k) -> p c k", k=128), AF.Copy,
                                        [p], [Btok])
                                for c0 in range(0, NCH, 8):
                                    nj = min(8, NCH - c0)
                                    p = ps()
                                    for j in range(nj):
                                        ch = c0 + j
                                        mm(p, p[0:64, j * 64:(j + 1) * 64], BT[:, ch * 64:(ch + 1) * 64],
                                           CTb[:, ch * 64:(ch + 1) * 64], [BT, CTb])
                                    act(CBs[:, c0:c0 + nj, :], p[0:64, 0:nj * 64].rearrange("p (c t) -> p c t", t=64), AF.Copy,
                                        [p], [CBs])
                            with scope() as e2:
                                BS = []
                                TS = []
                                for d_ in range(2):
                                    BS.append(dict(
                                        st4=sb(e2, "st4", [128, 4, 64], F32), stb4=sb(e2, "stb4", [128, 4, 64], BF16)))
                                    TS.append([dict(
                                        prep=sb(e2, "prep", [64, 4, 64], F32), Dd=sb(e2, "Dd", [64, 4, 64], F32),
                                        mdt=sb(e2, "mdt", [64, 4, 64], F32), Mb=sb(e2, "Mb", [64, 4, 64], BF16),
                                        Ec=sb(e2, "Ec", [128, 4, 64], F32), Cs=sb(e2, "Cs", [128, 4, 64], BF16),
                                        wv=sb(e2, "wv", [64, 4], F32), xw=sb(e2, "xw", [64, 4, 64], BF16),
                                        pcs=sb(e2, "pcs", [128, 256], F32)) for _ in range(2)])
                                    V(lambda e, b_=BS[d_]["st4"]: e.memset(b_[:], 0.0), writes=[BS[d_]["st4"]])
                                    V(lambda e, b_=BS[d_]["stb4"]: e.memset(b_[:], 0.0), writes=[BS[d_]["stb4"]])
                                hsl = slice(g_ * 4, g_ * 4 + 4)
                                orders = [list(range(NCH)), [3, 2, 1, 0] + list(range(NCH - 1, 3, -1))]
                                pdk = {}

                                def stepA(d, ch, par_):
                                    T_ = TS[d][par_]
                                    prep, Dd, mdt = T_["prep"], T_["Dd"], T_["mdt"]
                                    Mb, Ec, Cs, wv, xw = T_["Mb"], T_["Ec"], T_["Cs"], T_["wv"], T_["xw"]
                                    lastj = 63 if d == 0 else 0
                                    trib = tri[:, d:d + 1, :].to_broadcast([64, 4, 64])
                                    tsl = slice(ch * 64, (ch + 1) * 64)
                                    tt("vector", prep[:], dta[:, d, ch, hsl].unsqueeze(2).to_broadcast([64, 4, 64]), trib,
                                       ALU.mult, [dta, tri], [prep])
                                    pc = psb[d]
                                    mm(pc, pc[:, 0:256], ones32[0:64, :], prep[:].rearrange("p r t -> p (r t)"), [ones32, prep])
                                    pc_ = T_["pcs"]
                                    V(lambda e, pc_=pc_, pc=pc: e.tensor_copy(pc_[:], pc[:, 0:256]), [pc], [pc_])
                                    pc3 = pc_[:].rearrange("p (r t) -> p r t", t=64)
                                    pc = pc_
                                    tt("vector", Dd[:], pc3[0:64], cumt[:, d, ch, hsl].unsqueeze(2).to_broadcast([64, 4, 64]),
                                       ALU.subtract, [pc, cumt], [Dd])
                                    tt("vector", wv[:], pc3[0:64, :, lastj], cumt[:, d, ch, hsl], ALU.subtract, [pc, cumt], [wv])
                                    act(Ec[:], pc3, AF.Exp, [pc], [Ec])
                                    ts("vector", Dd[:], Dd[:], 0.0, ALU.min, [Dd], [Dd])
                                    act(Dd[:], Dd[:], AF.Exp, [Dd], [Dd])
                                    act(wv[:], wv[:], AF.Exp, [wv], [wv])
                                    tt("gpsimd", mdt[:], dtt[:, d, ch, hsl].unsqueeze(2).to_broadcast([64, 4, 64]), trib,
                                       ALU.mult, [dtt, tri], [mdt])
                                    tt("gpsimd", Cs[:], Ec[:], CTb[:, tsl].unsqueeze(1).to_broadcast([128, 4, 64]), ALU.mult,
                                       [Ec, CTb], [Cs])
                                    tt("vector", Dd[:], Dd[:], mdt[:], ALU.mult, [Dd, mdt], [Dd])
                                    tt("vector", Mb[:], Dd[:], CBs[:, ch:ch + 1, :].to_broadcast([64, 4, 64]), ALU.mult,
                                       [Dd, CBs], [Mb])
                                    tt("vector", wv[:], wv[:], dtt[:, d, ch, hsl], ALU.mult, [wv, dtt], [wv])
                                    tt("gpsimd", xw[:], xtok[:, ch, :].rearrange("p (r k) -> p r k", k=64),
                                       wv[:].unsqueeze(2).to_broadcast([64, 4, 64]), ALU.mult, [xtok, wv], [xw])
                                    pd = psb[2 + d * 2 + par_]
                                    mm(pd, pd[:, 0:256], Btok[:, ch, :], xw[:].rearrange("p r k -> p (r k)"), [Btok, xw])
                                    pdk[(d, par_)] = pd

                                def stepB(d, ch, par_):
                                    B_ = BS[d]
                                    T_ = TS[d][par_]
                                    st4, stb4 = B_["st4"], B_["stb4"]
                                    Mb, Ec, Cs = T_["Mb"], T_["Ec"], T_["Cs"]
                                    lastj = 63 if d == 0 else 0
                                    tsl = slice(ch * 64, (ch + 1) * 64)
                                    pd = pdk[(d, par_)]
                                    po = psb[6 + d]
                                    for r in range(4):
                                        mm(po, po[0:64, r * 64:(r + 1) * 64], xtok[:, ch, r * 64:(r + 1) * 64], Mb[:, r, :],
                                           [xtok, Mb], start=True, stop=False)
                                        mm(po, po[0:64, r * 64:(r + 1) * 64], stb4[:, r, :], Cs[:, r, :], [stb4, Cs],
                                           start=False, stop=True)
                                    tt("vector", st4[:], st4[:], Ec[:, :, lastj:lastj + 1].to_broadcast([128, 4, 64]), ALU.mult,
                                       [st4, Ec], [st4])
                                    tt("vector", st4[:], st4[:], pd[:, 0:256].rearrange("p (r k) -> p r k", k=64), ALU.add,
                                       [st4, pd], [st4])
                                    act(stb4[:], st4[:], AF.Copy, [st4], [stb4])
                                    tt("vector", yacc[:, :, tsl], yacc[:, :, tsl],
                                       po[0:64, 0:256].rearrange("p (r t) -> p r t", t=64), ALU.add, [yacc, po], [yacc])
                                if _os.environ.get("SWP", "1") == "1":
                                    for d_ in range(2):
                                        stepA(d_, orders[d_][0], 0)
                                    for k_ in range(NCH):
                                        if k_ + 1 < NCH:
                                            for d_ in range(2):
                                                stepA(d_, orders[d_][k_ + 1], (k_ + 1) % 2)
                                        for d_ in range(2):
                                            stepB(d_, orders[d_][k_], k_ % 2)
                                else:
                                    for k_ in range(NCH):
                                        for d_ in range(2):
                                            stepA(d_, orders[d_][k_], k_ % 2)
                                            stepB(d_, orders[d_][k_], k_ % 2)
                            with scope() as e2:
                                zs = sb(e2, "zs", [64, 512], F32)
                                ydb = [sb(e2, "ydb", [64, 4, 512], BF16) for _ in range(2)]
                                for r in range(4):
                                    h = g_ * 4 + r
                                    if r % 2 == 0:
                                        wtz = wnext(C_Z + h * 64, 128)
                                    wt = wtz
                                    cof = (r % 2) * 64
                                    for (t0, n, w) in TBS:
                                        p = proj_fm(wt, lambda dc: wt[:, dc, cof:cof + 64], t0, n, M=64)
                                        act(zs[:, 0:n], p[0:64, 0:n], AF.Silu, [p], [zs])
                                        tt("vector", yacc[:, r, t0:t0 + n], yacc[:, r, t0:t0 + n], zs[:, 0:n], ALU.mult,
                                           [yacc, zs], [yacc])
                                for bi, (t0, n, w) in enumerate(TBS):
                                    rstd = norm_block(e2, yacc, yacc[:, :, t0:t0 + n], n, nparts=64, nsub=4, denom=256.0)
                                    yb_ = ydb[bi % 2]
                                    for r in range(4):
                                        h = g_ * 4 + r
                                        stt(yb_[:, r, 0:n], yacc[:, r, t0:t0 + n], v64[:, ong + h:ong + h + 1], rstd[0:64, 0:n],
                                            ALU.mult, ALU.mult, [yacc, v64, rstd], [yb_])
                                    dma_out(ybr_d[3][0:64, g_ * 4 * T:(g_ + 1) * 4 * T].rearrange("p (r t) -> p r t", t=T)[:, :, t0:t0 + n],
                                            yb_[:, :, 0:n], yb_, [ybr_b[3]], par=True)

                def run_mixers():
                    mixer_lru()
                    if stop_after == "lru":
                        return
                    mixer_attn()
                    if stop_after == "attn":
                        return
                    mixer_hgrn()
                    if stop_after == "hgrn":
                        return
                    mixer_ssd()

                if stop_after != "h":
                    run_mixers()
                if s == 0 and l == 0 and nlayers > 1:
                    convert_issue(1)

                if stop_after is None:
                    PARTS = [TBS[0:2], TBS[2:4], TBS[4:5]]
                    with scope() as es:
                        ya = sb(es, "ya", [128, 4, 1024], BF16)
                        yb = sb(es, "yb", [128, 4, 1024], BF16)
                        yc = sb(es, "yc", [128, 4, 1024], BF16)
                        yd = sb(es, "yd", [64, 8, 1024], BF16)
                        wgs = [sb(es, "wg", [128, 8, 4, 128], BF16) for _ in range(2)]
                        wbs = [sb(es, "wb", [128, 3, 4, 128], BF16) for _ in range(2)]
                        wds = [sb(es, "wd", [64, 8, 128], BF16) for _ in range(2)]
                        sgs = [sb(es, "sg", [128, 512], F32) for _ in range(2)]
                        accs = [sb(es, "acc", [128, 512], F32) for _ in range(2)]
                        tmps = [sb(es, "tmpm", [128, 512], F32) for _ in range(2)]
                        mbl = [sb(es, "mbl", [128, 512], BF16) for _ in range(2)]
                        kcnt = [0]
                        wkc = [0]
                        for part in PARTS:
                            pt0 = part[0][0]
                            pn = sum(b_[1] for b_ in part)
                            for bi_, ysb in enumerate((ya, yb, yc)):
                                dma_in(ysb[:, :, 0:pn], ybr_d[bi_][:, 0:4 * T].rearrange("p (a t) -> p a t", t=T)[:, :, pt0:pt0 + pn],
                                       ysb, [ybr_b[bi_]])
                            dma_in(yd[:, :, 0:pn], ybr_d[3][0:64, :].rearrange("p (a t) -> p a t", t=T)[:, :, pt0:pt0 + pn],
                                   yd, [ybr_b[3]])
                            def ld_m(oc):
                                k_ = wkc[0]
                                wkc[0] += 1
                                wg = wgs[k_ % 2]
                                wb = wbs[k_ % 2]
                                wd = wds[k_ % 2]
                                ocs = slice(oc * 128, (oc + 1) * 128)
                                for i in range(4):
                                    load_win_cols(wg, wg[:, :, i, :], C_MG + i * D + oc * 128, 128)
                                dma_in(wb[:, 0], wb_branch[l, 0][:, ocs].rearrange("(kc p) c -> p kc c", p=128), wb, cvb[("branch", l)], par=True)
                                for g_ in range(2):
                                    dma_in(wb[g_ * 64:(g_ + 1) * 64, 1],
                                           wb_branch[l, 1][:, ocs].rearrange("(g r d) c -> g d r c", g=2, r=4)[g_], wb,
                                           cvb[("branch", l)], par=True)
                                dma_in(wb[:, 2], wb_branch[l, 2][:, ocs].rearrange("(kc p) c -> p kc c", p=128), wb, cvb[("branch", l)], par=True)
                                dma_in(wd[:], wb_branch[l, 3][:, ocs].rearrange("(h p) c -> p h c", p=64), wd, cvb[("branch", l)], par=True)
                                return (wg, wb, wd)

                            def cp_m(oc, wts, part=part, pt0=pt0):
                                wg, wb, wd = wts
                                kk_ = [0]
                                for (t0, n, w) in part:
                                    acc = accs[kcnt[0] % 2]
                                    mb_ = mbl[kcnt[0] % 2]
                                    kcnt[0] += 1
                                    lt = t0 - pt0
                                    for i in range(4):
                                        sg = sgs[i % 2]
                                        pg = proj_fm(wg, lambda dc: wg[:, dc, i, :], t0, n)
                                        act(sg[:, 0:n], pg[:, 0:n], AF.Sigmoid, [pg], [sg])
                                        pp = ps()
                                        if i < 3:
                                            ysrc = (ya, yb, yc)[i]
                                            for kc in range(4):
                                                mm(pp, pp[:, 0:n], wb[:, i, kc, :], ysrc[:, kc, lt:lt + n], [wb, ysrc],
                                                   start=(kc == 0), stop=(kc == 3))
                                        else:
                                            for h in range(8):
                                                mm(pp, pp[:, 0:n], wd[:, h, :], yd[:, h, lt:lt + n], [wd, yd],
                                                   start=(h == 0), stop=(h == 7))
                                        if i == 0:
                                            tt("vector", acc[:, 0:n], sg[:, 0:n], pp[:, 0:n], ALU.mult, [sg, pp], [acc])
                                        else:
                                            tmp = tmps[i % 2]
                                            tt("vector", tmp[:, 0:n], sg[:, 0:n], pp[:, 0:n], ALU.mult, [sg, pp], [tmp])
                                            if i < 3:
                                                tt("gpsimd", acc[:, 0:n], acc[:, 0:n], tmp[:, 0:n], ALU.add, [acc, tmp], [acc])
                                            else:
                                                tt("gpsimd", mb_[:, 0:n], acc[:, 0:n], tmp[:, 0:n], ALU.add, [acc, tmp], [mb_])
                                    dma_out(mrg_d[:, oc * T + t0:oc * T + t0 + n], mb_[:, 0:n], mb_, [mrg_b], par=True)
                            pipe(list(range(8)), ld_m, cp_m)
                    with scope() as es:
                        wo = sb(es, "wo", [128, 8, D], BF16)
                        dma_in(wo[:], wrows(wb_out[l]), wo, cvb[("out", l)])
                        ms = [sb(es, "m", [128, 8, 512], F32) for _ in range(2)]
                        mgs = [sb(es, "mg", [128, 8, 512], BF16) for _ in range(2)]
                        xts = [sb(es, "xt", [128, 8, 512], F32) for _ in range(1)]
                        tmps = [sb(es, "tmp", [128, 512], F32) for _ in range(2)]
                        def ld_b(it):
                            bi, (t0, n, w) = it
                            mg = mgs[bi % 2]
                            dma_in(mg[:, :, 0:n], mrg_d.rearrange("p (a t) -> p a t", t=T)[:, :, t0:t0 + n], mg, [mrg_b])
                            return mg

                        def cp_b(it, mg):
                            bi, (t0, n, w) = it
                            m = ms[bi % 2]
                            xt = xts[0]
                            tmp = tmps[bi % 2]
                            for oc2 in range(8):
                                p = ps()
                                for oc in range(8):
                                    mm(p, p[:, 0:n], wo[:, oc, oc2 * 128:(oc2 + 1) * 128], mg[:, oc, 0:n],
                                       [wo, mg], start=(oc == 0), stop=(oc == 7))
                                act(m[:, oc2, 0:n], p[:, 0:n], AF.Copy, [p], [m])
                            rstd = norm_block(es, m, m[:, :, 0:n], n)
                            dma_in(xt[:, :, 0:n], wrows(src_res)[:, :, t0:t0 + n], xt, src_bufs(t0, n))
                            for dc in range(8):
                                stt(tmp[:, 0:n], m[:, dc, 0:n], GS[:, 2, dc, w:w + 1], rstd[:, 0:n], ALU.mult, ALU.mult,
                                    [m, GS, rstd], [tmp])
                                tt("gpsimd", xt[:, dc, 0:n], xt[:, dc, 0:n], tmp[:, 0:n], ALU.add, [xt, tmp], [xt])
                            dma_out(wrows(res_d[s])[:, :, t0:t0 + n], xt[:, :, 0:n], xt, gran(s, t0, n))
                            rstd2 = norm_block(es, xt, xt[:, :, 0:n], n)
                            modulate_to(hT, xt, n, t0, w, 3, 4, rstd2, tmp)
                        pipe(list(enumerate(TBS)), ld_b, cp_b)
                    with scope() as es:
                        aT = sb(es, "aT", [128, 32, 768], BF16)
                        mo = sb(es, "mo", [128, 8, 768], F32)
                        wus = [sb(es, "wu", [128, 8, 128], BF16) for _ in range(2)]
                        wdn = [sb(es, "wdn", [128, 32, 128], BF16) for _ in range(2)]
                        rl = [sb(es, "rl", [128, 512], F32) for _ in range(2)]
                        xts = [sb(es, "xt", [128, 8, 512], F32) for _ in range(1)]
                        tmps = [sb(es, "tmp", [128, 512], F32) for _ in range(2)]
                        kq = [0]
                        pre_u = [None]
                        for si_, sup in enumerate(MLP_SUP):
                            base = sup[0][0]
                            def ld_u(ht):
                                wu = wus[ht % 2]
                                dma_in(wu[:], wrows(wb_up[l])[:, :, ht * 128:(ht + 1) * 128], wu, cvb[("up", l)])
                                return wu

                            def cp_u(ht, wu, sup=sup, base=base):
                                for (t0, n, w) in sup:
                                    p = proj_fm(wu, lambda dc: wu[:, dc, :], t0, n)
                                    r_ = rl[kq[0] % 2]
                                    kq[0] += 1
                                    act(r_[:, 0:n], p[:, 0:n], AF.Relu, [p], [r_])
                                    tt("vector", aT[:, ht, t0 - base:t0 - base + n], r_[:, 0:n], r_[:, 0:n], ALU.mult, [r_], [aT])
                            pre_d = None
                            pipe(list(range(32)), ld_u, cp_u, first=pre_u[0])
                            pre_u[0] = None

                            def ld_d(oc):
                                wd_ = wdn[oc % 2]
                                dma_in(wd_[:], wb_down[l][:, oc * 128:(oc + 1) * 128].rearrange("(ht p) c -> p ht c", p=128), wd_, cvb[("down", l)])
                                return wd_

                            def cp_d(oc, wd_, sup=sup, base=base):
                                for (t0, n, w) in sup:
                                    p = ps()
                                    for ht in range(32):
                                        mm(p, p[:, 0:n], wd_[:, ht, :], aT[:, ht, t0 - base:t0 - base + n], [wd_, aT],
                                           start=(ht == 0), stop=(ht == 31))
                                    act(mo[:, oc, t0 - base:t0 - base + n], p[:, 0:n], AF.Copy, [p], [mo])
                            pipe(list(range(8)), ld_d, cp_d)
                            if si_ + 1 < len(MLP_SUP):
                                pre_u[0] = ld_u(0)
                            for bi, (t0, n, w) in enumerate(sup):
                                xt = xts[0]
                                tmp = tmps[bi % 2]
                                mos = mo[:, :, t0 - base:t0 - base + n]
                                rstd = norm_block(es, mo, mos, n)
                                dma_in(xt[:, :, 0:n], wrows(res_d[s])[:, :, t0:t0 + n], xt, gran(s, t0, n))
                                for dc in range(8):
                                    stt(tmp[:, 0:n], mo[:, dc, t0 - base:t0 - base + n], GS[:, 5, dc, w:w + 1], rstd[:, 0:n],
                                        ALU.mult, ALU.mult, [mo, GS, rstd], [tmp])
                                    tt("gpsimd", xt[:, dc, 0:n], xt[:, dc, 0:n], tmp[:, 0:n], ALU.add, [xt, tmp], [xt])
                                if not last:
                                    dma_out(wrows(res_d[s])[:, :, t0:t0 + n], xt[:, :, 0:n], xt, gran(s, t0, n))
                                elif t0 >= NCTX:
                                    ev = dma_out(wrows(out_d[s])[:, :, t0 - NCTX:t0 - NCTX + n], xt[:, :, 0:n], xt, [out_b], par=True)
                                    out_events.append(ev)

    S.barrier()
    S.wait_events("sync", out_events)
    top.close()
    S.close()
    return nc, S


def _host_inputs(inputs):
    f = np.float32
    x = np.asarray(inputs["x"], f)
    ctx = np.asarray(inputs["ctx"], f)
    c = np.asarray(inputs["c"], f)
    c_ctx = np.asarray(inputs["c_ctx"], f)
    B = x.shape[0]

    def pj(v, p=128):
        v = np.asarray(v, f)
        return np.ascontiguousarray(v.reshape(-1, p).T)

    shared = {}
    for k_ in ("w_ada", "w_in", "w_branch", "w_out", "w_mlp_up", "w_mlp_down"):
        shared[k_] = np.ascontiguousarray(np.asarray(inputs[k_], f))
    w_in = shared["w_in"]
    qk = np.concatenate([w_in[:, :, C_Q:C_Q + 512], w_in[:, :, C_K:C_K + 128]], axis=2)
    perm = np.arange(640).reshape(10, 2, 2, 16)[:, :, ::-1, :].reshape(640)
    shared["wqkp"] = np.ascontiguousarray(qk[:, :, perm])
    vecs = np.zeros((DEPTH, 128, NV), f)
    vecs64 = np.zeros((DEPTH, 64, NV64), f)
    rowv = np.zeros((DEPTH, NROW), f)
    lru_bd = np.zeros((DEPTH, 128, 16, 128), f)
    for l in range(DEPTH):
        def put(name, arr):
            o, w = VEC[name]
            assert arr.shape == (128, w), (name, arr.shape)
            vecs[l][:, o:o + w] = arr
        put("bada", pj(inputs["b_ada"][l]))
        put("gpre", pj(inputs["g_pre_mix"][l]))
        put("gpostmix", pj(inputs["g_post_mix"][l]))
        put("gpremlp", pj(inputs["g_pre_mlp"][l]))
        put("gpostmlp", pj(inputs["g_post_mlp"][l]))
        put("lcw", np.concatenate([pj(inputs["lru_conv_w"][l][j]) for j in range(4)], axis=1))
        put("lcb", pj(inputs["lru_conv_b"][l]))
        put("lrb", np.concatenate([pj(inputs["lru_rec_b"][l][d]) for d in range(2)], axis=1))
        put("lib", np.concatenate([pj(inputs["lru_inp_b"][l][d]) for d in range(2)], axis=1))
        put("llam", np.concatenate([pj(inputs["lru_lambda"][l][d]) for d in range(2)], axis=1))
        put("hl0", pj(inputs["hgrn_lb_logits"][0]))
        put("hl1", pj(inputs["hgrn_lb_logits"][l]))
        put("hng", pj(inputs["hgrn_norm_g"][l]))
        scw = np.asarray(inputs["ssd_conv_w"][l], f)
        scb = np.asarray(inputs["ssd_conv_b"][l], f)
        put("scw", np.concatenate([pj(scw[j, 512:1024]) for j in range(4)], axis=1))
        put("scb", pj(scb[512:1024]))

        def put64(name, arr):
            o, w = V64[name]
            assert arr.shape == (64, w), (name, arr.shape)
            vecs64[l][:, o:o + w] = arr
        put64("cwx", np.concatenate([pj(scw[j, 0:512], 64) for j in range(4)], axis=1))
        put64("cbx", pj(scb[0:512], 64))
        put64("ng", pj(inputs["ssd_norm_g"][l], 64))
        rowv[l, 0:8] = inputs["attn_sink"][l]
        rowv[l, 8:24] = np.asarray(inputs["ssd_dt_bias"][l], f).reshape(16)
        rowv[l, 24:40] = np.asarray(inputs["ssd_a_log"][l], f).reshape(16)
        rowv[l, 40:48] = inputs["ssd_skip"][l]
        for d in range(2):
            for gate, nm_ in enumerate(("lru_rec_w", "lru_inp_w")):
                wm = np.asarray(inputs[nm_][l][d], f)
                for c_ in range(4):
                    for hb in range(2):
                        lru_bd[l, hb * 64:(hb + 1) * 64, (d * 2 + gate) * 4 + c_, hb * 64:(hb + 1) * 64] = wm[2 * c_ + hb]
    shared.update(vecs=vecs, vecs64=vecs64, rowv=rowv, lru_bd=lru_bd)
    shared["ident"] = np.eye(128, dtype=f)
    quarter = 16
    inv_freq = (10000.0 ** (-np.arange(quarter, dtype=np.float64) / quarter))
    t = np.arange(NLAT)
    rows_, cols_ = t // 64, t % 64
    cos = np.ones((64, T), np.float64)
    sin = np.zeros((64, T), np.float64)
    for half, pos in ((0, rows_), (1, cols_)):
        ang = pos[None, :] * inv_freq[:, None]
        cc, ss = np.cos(ang.astype(np.float32)), np.sin(ang.astype(np.float32))
        b0 = half * 32
        cos[b0:b0 + 16, NCTX:] = cc
        cos[b0 + 16:b0 + 32, NCTX:] = cc
        sin[b0:b0 + 16, NCTX:] = -ss
        sin[b0 + 16:b0 + 32, NCTX:] = ss
    rope = np.stack([np.concatenate([cos, cos], 0), np.concatenate([sin, sin], 0)]).astype(f)
    shared["rope_cs"] = np.ascontiguousarray(rope)
    j = np.arange(128)[:, None]
    i = np.arange(128)[None, :]
    am = np.stack([np.tile((j >= i).astype(f), (1, 4)), np.tile((j <= i).astype(f), (1, 4))], axis=1)
    shared["amask"] = np.ascontiguousarray(am)
    a = np.arange(64)[:, None]
    b = np.arange(64)[None, :]
    shared["tri64"] = np.ascontiguousarray(np.stack([(a <= b).astype(f), (a >= b).astype(f)], axis=1))
    in_maps = []
    for core in range(NCORES):
        bs = [core * SPC + k_ for k_ in range(SPC)]
        xin = np.stack([np.concatenate([ctx[b_].T, x[b_].T], axis=1) for b_ in bs]).astype(f)
        cond = np.stack([pj(c[bs[0]]), pj(c[bs[1]]), pj(c_ctx)], axis=2)
        m = dict(shared)
        m["xin"] = np.ascontiguousarray(xin)
        m["cond"] = np.ascontiguousarray(cond)
        in_maps.append(m)
    return in_maps


_CACHE = {}


def kernel(**inputs):
    if "nc" not in _CACHE:
        _CACHE["nc"] = build()[0]
    nc = _CACHE["nc"]
    in_maps = _host_inputs(inputs)
    res = run_bass_kernel_spmd(nc, in_maps, core_ids=list(range(NCORES)))
    outs = []
    for core in range(NCORES):
        o = np.asarray(res.results[core]["out"])
        for k_ in range(SPC):
            outs.append(o[k_].T)
    return np.ascontiguousarray(np.stack(outs).astype(np.float32))
```

```python
import numpy as np
import concourse.bass as bass
import concourse.mybir as mybir
from concourse.bass_utils import run_bass_kernel_spmd
from contextlib import ExitStack, contextmanager

F32 = mybir.dt.float32
BF16 = mybir.dt.bfloat16
AF = mybir.ActivationFunctionType
ALU = mybir.AluOpType

D = 1024
NCTX = 256
NLAT = 2048
T = NCTX + NLAT
NCH = T // 64
NT = T // 128
DEPTH = 2
NCORES = 8
SPC = 2
EPS = 1e-6
TBS = [(0, 256, 1), (256, 512, 0), (768, 512, 0), (1280, 512, 0), (1792, 512, 0)]
MLP_SUP = [[(0, 256, 1), (256, 512, 0)], [(768, 512, 0), (1280, 256, 0)], [(1536, 512, 0), (2048, 256, 0)]]

C_LX, C_LG, C_Q, C_K, C_V = 0, 512, 1024, 1536, 1664
C_HQ, C_HI, C_HFF, C_HFB, C_HG = 1792, 2304, 2816, 3328, 3840
C_Z, C_XS, C_B, C_C, C_DT, C_MG = 4352, 4864, 5376, 5632, 5888, 5904

VEC = {}
_o = 0
for _n, _w in [("bada", 48), ("gpre", 8), ("gpostmix", 8), ("gpremlp", 8), ("gpostmlp", 8), ("lcw", 16), ("lcb", 4),
               ("lrb", 8), ("lib", 8), ("llam", 8), ("hl0", 4), ("hl1", 4), ("hng", 4), ("scw", 16), ("scb", 4)]:
    VEC[_n] = (_o, _w)
    _o += _w
NV = _o
V64 = {}
_o = 0
for _n, _w in [("cwx", 32), ("cbx", 8), ("ng", 8)]:
    V64[_n] = (_o, _w)
    _o += _w
NV64 = _o
ROWV = {"sink": (0, 8), "dtb": (8, 16), "alog": (24, 16), "skip": (40, 8)}
NROW = 48

ENGS = ("tensor", "vector", "scalar", "gpsimd", "sync")
import os as _os
SSD_STAGE = float(_os.environ.get("SSD_STAGE", "9"))


class Buf:
    __slots__ = ("name", "w", "r", "t")

    def __init__(self, name, t=None):
        self.name = name
        self.w = []
        self.r = []
        self.t = t

    def __getitem__(self, k):
        return self.t[k]


class Sched:
    def __init__(self, nc, n_dma_sems=32, n_sw_sems=8):
        self.nc = nc
        self.cnt = {e: 0 for e in ENGS}
        self.clock = {e: {} for e in ENGS}
        self.dsem = []
        self.dnext = 0
        self.n_dma_sems = n_dma_sems
        self.ssem = []
        self.snext = 0
        self.n_sw_sems = n_sw_sems
        self.evclock = {}
        self.ctx = []
        self.semh = {}
        self.ninstr = 0

    def open(self):
        nc = self.nc
        for e in ENGS:
            cm = nc.semaphore("es_" + e)
            self.semh["e_" + e] = cm.__enter__()
            self.ctx.append(cm)
        for i in range(self.n_dma_sems):
            cm = nc.semaphore("ds_%d" % i)
            self.semh["d_%d" % i] = cm.__enter__()
            self.ctx.append(cm)
            self.dsem.append([0, "d_%d" % i])
        for i in range(self.n_sw_sems):
            cm = nc.semaphore("ss_%d" % i)
            self.semh["s_%d" % i] = cm.__enter__()
            self.ctx.append(cm)
            self.ssem.append([0, "s_%d" % i])

    def close(self):
        for cm in reversed(self.ctx):
            cm.__exit__(None, None, None)

    def _emit1(self, eng, waits, fn, key, inc, fuse=False):
        e = getattr(self.nc, eng)
        fused = None
        if fuse and fn is not None and waits and _os.environ.get("FUSE", "1") == "1":
            fused = waits[-1]
            waits = waits[:-1]
        for (k, v) in waits:
            e.wait_ge(self.semh[k], v)
            self.ninstr += 1
        if fn is not None:
            ins = fn(e)
            if fused is not None:
                ins._wait_ge(self.semh[fused[0]], fused[1])
            ins.then_inc(self.semh[key], inc)
            self.ninstr += 1

    @staticmethod
    def _deps(reads, writes, par=False):
        deps = []
        for b in reads:
            deps.extend(b.w)
        for b in writes:
            if par and not b.r:
                continue
            deps.extend(b.w)
            deps.extend(b.r)
        return deps

    def _waits(self, eng, deps, skip_self=False):
        clk = self.clock[eng]
        own = "e_" + eng
        need = {}
        for (k, v) in deps:
            if skip_self and k == own:
                continue
            if clk.get(k, 0) >= v:
                continue
            if need.get(k, 0) < v:
                need[k] = v
        for k, v in need.items():
            ec = self.evclock.get((k, v))
            if ec is not None:
                for kk, vv in ec.items():
                    if clk.get(kk, 0) < vv:
                        clk[kk] = vv
            if clk.get(k, 0) < v:
                clk[k] = v
        return list(need.items())

    def _mark(self, ev, reads, writes, par=False):
        for b in writes:
            if par and not b.r:
                b.w.append(ev)
            else:
                b.w = [ev]
                b.r = []
        for b in reads:
            if b not in writes:
                b.r.append(ev)

    def op(self, eng, fn, reads=(), writes=(), skip_self=False):
        waits = self._waits(eng, self._deps(reads, writes), skip_self)
        self.cnt[eng] += 1
        ev = ("e_" + eng, self.cnt[eng])
        self.evclock[ev] = dict(self.clock[eng])
        self._emit1(eng, waits, fn, ev[0], 1, fuse=(eng != "tensor"))
        self._mark(ev, reads, writes)
        return ev

    def dma(self, eng, fn, reads=(), writes=(), par=False):
        deps = self._deps(reads, writes, par)
        if eng == "gpsimd":
            slot = self.ssem[self.snext]
            self.snext = (self.snext + 1) % len(self.ssem)
        else:
            slot = self.dsem[self.dnext]
            self.dnext = (self.dnext + 1) % len(self.dsem)
        cur, key = slot
        if cur > 0:
            deps.append((key, cur))
        waits = self._waits(eng, deps)
        slot[0] = cur + 16
        ev = (key, cur + 16)
        self.evclock[ev] = dict(self.clock[eng])
        self._emit1(eng, waits, fn, key, 16, fuse=True)
        self._mark(ev, reads, writes, par)
        return ev

    def wait_events(self, eng, events):
        waits = self._waits(eng, list(events))
        self._emit1(eng, waits, None, None, 0)

    def barrier(self):
        evs = [("e_" + e, self.cnt[e]) for e in ENGS if self.cnt[e] > 0]
        evs += [(key, cur) for (cur, key) in self.dsem + self.ssem if cur > 0]
        for e in ENGS:
            self.wait_events(e, evs)


def build(dbg=False, nlayers=DEPTH, nseq=SPC, stop_after=None):
    nc = bass.Bass("TRN2", target_bir_lowering=False)

    def din(name, shape, dt=F32):
        return nc.dram_tensor(name, list(shape), dt, kind="ExternalInput").ap()

    xin = din("xin", [SPC, D, T])
    cond_d = din("cond", [128, 8, 3])
    w_ada = din("w_ada", [DEPTH, D, 6 * D])
    w_in = din("w_in", [DEPTH, D, 10000])
    wqkp = din("wqkp", [DEPTH, D, 640])
    w_branch = din("w_branch", [DEPTH, 4, 512, D])
    w_out = din("w_out", [DEPTH, D, D])
    w_up = din("w_mlp_up", [DEPTH, D, 4 * D])
    w_down = din("w_mlp_down", [DEPTH, 4 * D, D])
    vecs_d = din("vecs", [DEPTH, 128, NV])
    vecs64_d = din("vecs64", [DEPTH, 64, NV64])
    rowv_d = din("rowv", [DEPTH, NROW])
    lru_bd_d = din("lru_bd", [DEPTH, 128, 16, 128])
    ident_d = din("ident", [128, 128])
    rope_d = din("rope_cs", [2, 128, T])
    amask_d = din("amask", [128, 2, 512])
    tri_d = din("tri64", [64, 2, 64])
    out_d = nc.dram_tensor("out", [SPC, D, NLAT], F32, kind="ExternalOutput").ap()
    skind = "ExternalOutput" if dbg else "Internal"
    res_d = nc.dram_tensor("res", [SPC, D, T], F32, kind=skind).ap()
    ybr_d = nc.dram_tensor("ybr", [4, 128, 8 * T], BF16, kind=skind).ap()
    mrg_d = nc.dram_tensor("mrg", [128, 8 * T], BF16, kind=skind).ap()
    h_dbg = nc.dram_tensor("h_dbg", [128, 8 * T], BF16, kind=skind).ap() if dbg else None

    def dscr(name, shape, dt=BF16):
        return nc.dram_tensor(name, list(shape), dt, kind="Internal").ap()

    wb_ada = dscr("wb_ada", [DEPTH, D, 6 * D])
    wb_in = dscr("wb_in", [DEPTH, D, 10000])
    wb_qkp = dscr("wb_qkp", [DEPTH, D, 640])
    wb_branch = dscr("wb_branch", [DEPTH, 4, 512, D])
    wb_out = dscr("wb_out", [DEPTH, D, D])
    wb_up = dscr("wb_up", [DEPTH, D, 4 * D])
    wb_down = dscr("wb_down", [DEPTH, 4 * D, D])
    wb_bd = dscr("wb_bd", [DEPTH, 128, 16 * 128])

    S = Sched(nc)
    S.open()
    cvb = {}

    conv_done = []
    conv_queue = {}

    def convert_plan(l):
        q = []

        def cv(key, dst2d, src2d, rows_per, seg):
            nrows = src2d.shape[0]
            cvb[(key, l)] = []
            for r0 in range(0, nrows, rows_per):
                b = Buf("cv_%s_%d_%d" % (key, l, r0))
                cvb[(key, l)].append(b)
                d_ = dst2d[r0:r0 + rows_per]
                s_ = src2d[r0:r0 + rows_per]
                if seg:
                    d_ = d_.rearrange("r (a c) -> r a c", c=seg)
                    s_ = s_.rearrange("r (a c) -> r a c", c=seg)
                q.append((b, d_, s_))
        cv("ada", wb_ada[l], w_ada[l], 512, 2048)
        cv("in", wb_in[l], w_in[l], 256, 2000)
        cv("qkp", wb_qkp[l], wqkp[l], 1024, 0)
        cv("bd", wb_bd[l], lru_bd_d[l].rearrange("p a c -> p (a c)"), 128, 0)
        cv("branch", wb_branch[l].rearrange("i r c -> (i r) c"), w_branch[l].rearrange("i r c -> (i r) c"), 1024, 0)
        cv("out", wb_out[l], w_out[l], 1024, 0)
        cv("up", wb_up[l], w_up[l], 512, 2048)
        cv("down", wb_down[l], w_down[l], 2048, 0)
        conv_queue[l] = q

    def convert_issue(l, n=100):
        q = conv_queue[l]
        while q and n > 0:
            b, d_, s_ = q.pop(0)
            rd = [conv_done[-3]] if len(conv_done) >= 3 else []
            S.dma("gpsimd", lambda e, d_=d_, s_=s_: e.dma_start(out=d_, in_=s_), reads=rd, writes=[b])
            conv_done.append(b)
            n -= 1

    uid = [0]

    def nm(p):
        uid[0] += 1
        return "%s_%d" % (p, uid[0])

    def sb(es, name, shape, dt=F32):
        return Buf(name, es.enter_context(nc.sbuf_tensor(nm(name), list(shape), dt)))

    @contextmanager
    def scope():
        with ExitStack() as es:
            yield es
            S.barrier()

    top = ExitStack()
    psb = [Buf("ps%d" % i, top.enter_context(nc.psum_tensor("psum%d" % i, [128, 512], F32))) for i in range(8)]
    psi = [0]

    psa = [0]
    psbi = [0]

    def ps(pool=None):
        if pool == "a":
            b = psb[psa[0]]
            psa[0] = (psa[0] + 1) % 4
            return b
        if pool == "b":
            b = psb[4 + psbi[0]]
            psbi[0] = (psbi[0] + 1) % 4
            return b
        b = psb[psi[0]]
        psi[0] = (psi[0] + 1) % 8
        return b

    def V(fn, reads=(), writes=()):
        return S.op("vector", fn, reads, writes)

    def A(fn, reads=(), writes=()):
        return S.op("scalar", fn, reads, writes)

    def G(fn, reads=(), writes=()):
        return S.op("gpsimd", fn, reads, writes)

    def PE(fn, reads=(), writes=()):
        return S.op("tensor", fn, reads, writes, skip_self=True)

    def mm(pbuf, out_ap, lhsT, rhs, reads, start=True, stop=True):
        return PE(lambda e: e.matmul(out_ap, lhsT=lhsT, rhs=rhs, start=start, stop=stop), reads=reads, writes=[pbuf])

    def dma_in(out_ap, in_ap, wbuf, rbufs=(), par=False):
        return S.dma("sync", lambda e: e.dma_start(out=out_ap, in_=in_ap), reads=list(rbufs), writes=[wbuf], par=par)

    def pipe(items, load, compute, first=None):
        nxt = load(items[0]) if first is None else first
        for i_, it in enumerate(items):
            cur = nxt
            if i_ + 1 < len(items):
                nxt = load(items[i_ + 1])
            compute(it, cur)

    class WQ:
        def __init__(self, specs, bufs, loader):
            self.specs, self.bufs, self.loader = list(specs), bufs, loader
            self.i = 0
            self._issue(0)

        def _issue(self, k):
            if k < len(self.specs):
                b = self.bufs[k % len(self.bufs)]
                self.loader(b, self.specs[k])

        def get(self):
            b = self.bufs[self.i % len(self.bufs)]
            self.i += 1
            self._issue(self.i)
            return b

    def dma_cast(out_ap, in_ap, wbuf, rbufs=()):
        return S.dma("gpsimd", lambda e: e.dma_start(out=out_ap, in_=in_ap), reads=list(rbufs), writes=[wbuf])

    def dma_out(out_ap, in_ap, rbuf, wbufs=(), par=False):
        return S.dma("sync", lambda e: e.dma_start(out=out_ap, in_=in_ap), reads=[rbuf], writes=list(wbufs), par=par)

    dumps = {}

    def dump(name, buf, ap, shape, dt=F32):
        if not dbg or name in dumps:
            return
        dumps[name] = 1
        dd = nc.dram_tensor("dump_" + name, list(shape), dt, kind="ExternalOutput").ap()
        S.dma("sync", lambda e: e.dma_start(out=dd, in_=ap), reads=[buf], writes=[Buf("dd_" + name)])

    def act(out_ap, in_ap, func, reads, writes, bias=None, scale=1.0):
        kw = {}
        if bias is not None:
            kw["bias"] = bias
        return A(lambda e: e.activation(out=out_ap, in_=in_ap, func=func, scale=scale, **kw), reads, writes)

    def tt(eng, out_ap, a, b, op, reads, writes):
        return S.op(eng, lambda e: e.tensor_tensor(out=out_ap, in0=a, in1=b, op=op), reads, writes)

    def stt(out_ap, in0, scalar, in1, op0, op1, reads, writes):
        return V(lambda e: e.scalar_tensor_tensor(out=out_ap, in0=in0, scalar=scalar, in1=in1, op0=op0, op1=op1),
                 reads, writes)

    def ts(eng, out_ap, in0, s1, op0, reads, writes, s2=None, op1=None):
        if op1 is None:
            return S.op(eng, lambda e: e.tensor_scalar(out=out_ap, in0=in0, scalar1=s1, scalar2=None, op0=op0),
                        reads, writes)
        return S.op(eng, lambda e: e.tensor_scalar(out=out_ap, in0=in0, scalar1=s1, scalar2=s2, op0=op0, op1=op1),
                    reads, writes)

    def wrows(ap2d):
        return ap2d.rearrange("(kc p) n -> p kc n", p=128)

    res_g = [[Buf("res%d_%d" % (s, i)) for i in range(T // 256)] for s in range(SPC)]
    ybr_b = [Buf("ybr%d" % i) for i in range(4)]
    out_b = Buf("out")
    hdbg_b = Buf("hdbg")
    mrg_b = Buf("mrg")
    out_events = []

    def gran(s, t0, n):
        return res_g[s][t0 // 256:(t0 + n + 255) // 256]

    ones32 = sb(top, "ones32", [128, 128], F32)
    onesb = sb(top, "onesb", [128, 128], BF16)
    identb = sb(top, "identb", [128, 128], BF16)
    cst = sb(top, "cst", [128, 4], F32)
    condb = sb(top, "condb", [128, 8, 3], BF16)
    tri = sb(top, "tri", [64, 2, 64], F32)
    V(lambda e: e.memset(ones32[:], 1.0), writes=[ones32])
    V(lambda e: e.memset(onesb[:], 1.0), writes=[onesb])
    V(lambda e: e.memset(cst[:, 0:1], EPS), writes=[cst])
    V(lambda e: e.memset(cst[:, 1:2], 1.0), writes=[cst])
    V(lambda e: e.memset(cst[:, 2:3], 0.0), writes=[cst])
    dma_cast(identb[:], ident_d, identb)
    dma_in(tri[:], tri_d, tri)
    modT3 = [sb(top, "modT3", [128, 48, 3], F32) for _ in range(DEPTH)]
    with ExitStack() as es0:
        c32 = sb(es0, "c32", [128, 8, 3], F32)
        dma_in(c32[:], cond_d, c32)
        act(condb[:], c32[:], AF.Silu, [c32], [condb])
        S.barrier()

    def norm_block(es_tmp, src, src_ap, n, nparts=128, nsub=8, denom=float(D)):
        sq = sqpool[0]
        sqi[0] += 1
        rs = rspool[rsi[0] % 2]
        rsi[0] += 1
        act(sq[0:nparts, 0:nsub, 0:n], src_ap, AF.Square, [src], [sq])
        p = ps()
        for k in range(nsub):
            mm(p, p[0:nparts, 0:n], ones32[0:nparts, 0:nparts], sq[0:nparts, k, 0:n], [ones32, sq],
               start=(k == 0), stop=(k == nsub - 1))
        act(rs[0:nparts, 0:n], p[0:nparts, 0:n], AF.Sqrt, [p, cst], [rs], bias=cst[0:nparts, 0:1], scale=1.0 / denom)
        V(lambda e: e.reciprocal(rs[0:nparts, 0:n], rs[0:nparts, 0:n]), [rs], [rs])
        return rs

    sqpool = []
    rspool = []
    sqi = [0]
    rsi = [0]

    convert_plan(0)
    convert_plan(1)
    convert_issue(0)
    for s in range(nseq):
        for l in range(nlayers):
            last = (l == DEPTH - 1)
            src_res = xin[s] if l == 0 else res_d[s]
            src_bufs = (lambda t0, n: []) if l == 0 else (lambda t0, n, s=s: gran(s, t0, n))
            with scope() as L:
                hT = sb(L, "hT", [128, 8, T], BF16)
                vec = sb(L, "vec", [128, NV], F32)
                GS = sb(L, "GS", [128, 6, 8, 2], F32)
                sqpool[:] = [sb(L, "sq", [128, 8, 512], F32) for _ in range(1)]
                rspool[:] = [sb(L, "rs", [128, 512], F32) for _ in range(2)]
                dma_in(vec[:], vecs_d[l], vec)

                def vcol(name, j=0, n=1, p0=0, p1=128):
                    o, w = VEC[name]
                    return vec[p0:p1, o + j:o + j + n]

                csl = slice(s, 3, 2 - s)
                if s == 0:
                  with scope() as es:
                    wa = [sb(es, "wada", [128, 8, 1536], BF16) for _ in range(2)]
                    pm = ps()
                    pmv = pm[:, 0:144].rearrange("p (j w) -> p j w", w=3)

                    def ld_ada(piece):
                        wb = wa[piece % 2]
                        dma_in(wb[:], wrows(wb_ada[l])[:, :, piece * 1536:(piece + 1) * 1536], wb, cvb[("ada", l)])
                        return wb

                    def cp_ada(piece, wb):
                        for jj in range(12):
                            j = piece * 12 + jj
                            for dc in range(8):
                                mm(pm, pmv[:, j, :], wb[:, dc, jj * 128:(jj + 1) * 128], condb[:, dc, :],
                                   [wb, condb], start=(dc == 0), stop=(dc == 7))
                    pipe(list(range(4)), ld_ada, cp_ada)
                    o, w = VEC["bada"]
                    tt("vector", modT3[l][:], pmv, vec[:, o:o + 48].unsqueeze(2).to_broadcast([128, 48, 3]), ALU.add,
                       [pm, vec], [modT3[l]])
                if True:
                    modT_b = modT3[l]

                    class _MT:
                        def __getitem__(self, k):
                            return modT_b[:, :, csl][k]
                    modT = _MT()
                    modTb = modT_b

                    def gbc(name):
                        o, w = VEC[name]
                        return vec[:, o:o + 8].unsqueeze(2).to_broadcast([128, 8, 2])
                    stt(GS[:, 0], modT[:, 8:16, :], 1.0, gbc("gpre"), ALU.add, ALU.mult, [modTb, vec], [GS])
                    V(lambda e: e.tensor_copy(GS[:, 1], modT[:, 0:8, :]), [modTb], [GS])
                    tt("vector", GS[:, 2], modT[:, 16:24, :], gbc("gpostmix"), ALU.mult, [modTb, vec], [GS])
                    stt(GS[:, 3], modT[:, 32:40, :], 1.0, gbc("gpremlp"), ALU.add, ALU.mult, [modTb, vec], [GS])
                    V(lambda e: e.tensor_copy(GS[:, 4], modT[:, 24:32, :]), [modTb], [GS])
                    tt("vector", GS[:, 5], modT[:, 40:48, :], gbc("gpostmlp"), ALU.mult, [modTb, vec], [GS])

                def modulate_to(dst, xt, n, t0, w, gi, si, rstd, tmp):
                    for dc in range(8):
                        stt(tmp[:, 0:n], xt[:, dc, 0:n], GS[:, gi, dc, w:w + 1], rstd[:, 0:n], ALU.mult, ALU.mult,
                            [xt, GS, rstd], [tmp])
                        act(dst[:, dc, t0:t0 + n], tmp[:, 0:n], AF.Identity, [tmp, GS], [dst],
                            bias=GS[:, si, dc, w:w + 1])

                with scope() as es:
                    xts = [sb(es, "xt", [128, 8, 512], F32) for _ in range(2)]
                    tmps = [sb(es, "tmp", [128, 512], F32) for _ in range(2)]
                    def ld_x(it):
                        bi, (t0, n, w) = it
                        xt = xts[bi % 2]
                        dma_in(xt[:, :, 0:n], wrows(src_res)[:, :, t0:t0 + n], xt, src_bufs(t0, n))
                        return xt

                    def cp_x(it, xt):
                        bi, (t0, n, w) = it
                        rstd = norm_block(es, xt, xt[:, :, 0:n], n)
                        modulate_to(hT, xt, n, t0, w, 0, 1, rstd, tmps[bi % 2])
                    pipe(list(enumerate(TBS)), ld_x, cp_x)
                if dbg and stop_after == "h":
                    dma_out(h_dbg, hT[:].rearrange("p a t -> p (a t)"), hT, [hdbg_b])

                def load_win_cols(wt, dst_ap, c0, ncols, src=None):
                    if src is None:
                        dma_in(dst_ap, wrows(wb_in[l])[:, :, c0:c0 + ncols], wt, cvb[("in", l)], par=True)
                    else:
                        dma_in(dst_ap, wrows(wb_qkp[l])[:, :, c0:c0 + ncols], wt, cvb[("qkp", l)], par=True)

                def proj_fm(wt, w_ap_fn, t0, n, M=128):
                    p = ps()
                    for dc in range(8):
                        mm(p, p[0:M, 0:n], w_ap_fn(dc), hT[:, dc, t0:t0 + n], [wt, hT], start=(dc == 0), stop=(dc == 7))
                    return p

                def mixer_lru():
                    with scope() as es:
                        bd = sb(es, "bd", [128, 16, 128], BF16)
                        dma_in(bd[:].rearrange("p a c -> p (a c)"), wb_bd[l], bd, cvb[("bd", l)])
                        cs = sb(es, "cs", [128, 8], F32)
                        o, w = VEC["llam"]
                        act(cs[:], vec[:, o:o + 8], AF.Exp, [vec], [cs], scale=-1.0)
                        act(cs[:], cs[:], AF.Ln, [cs, cst], [cs], bias=cst[:, 1:2])
                        ts("vector", cs[:], cs[:], -8.0, ALU.mult, [cs], [cs])
                        ya = sb(es, "ya", [128, 4, T], BF16)
                        wls = [sb(es, "wl", [128, 8, 2, 128], BF16) for _ in range(2)]
                        u = sb(es, "u", [128, T], F32)
                        g = sb(es, "g", [128, T], F32)
                        xc = sb(es, "xc", [128, T], F32)
                        xcb = sb(es, "xcb", [128, T], BF16)
                        r_ = sb(es, "r", [128, T], F32)
                        i_ = sb(es, "i", [128, T], F32)
                        t_ = sb(es, "t", [128, T], F32)
                        h0 = sb(es, "h0", [128, T], F32)
                        def ld_l(wl, c):
                            load_win_cols(wl, wl[:, :, 0, :], C_LX + c * 128, 128)
                            load_win_cols(wl, wl[:, :, 1, :], C_LG + c * 128, 128)
                        wq_l = WQ(range(4), wls, ld_l)
                        for c in range(4):
                            wl = wq_l.get()
                            for (t0, n, w) in TBS:
                                p = proj_fm(wl, lambda dc: wl[:, dc, 0, :], t0, n)
                                act(u[:, t0:t0 + n], p[:, 0:n], AF.Copy, [p], [u])
                                p = proj_fm(wl, lambda dc: wl[:, dc, 1, :], t0, n)
                                act(g[:, t0:t0 + n], p[:, 0:n], AF.Gelu_apprx_tanh, [p], [g])
                            act(xc[:], u[:], AF.Identity, [u, vec], [xc], bias=vcol("lcb", c), scale=vcol("lcw", 2 * 4 + c))
                            for (s0, s1) in ((0, NCTX), (NCTX, T)):
                                for j in (0, 1, 3):
                                    off = j - 2
                                    a_ = max(s0, s0 - off)
                                    b_ = min(s1, s1 - off)
                                    stt(xc[:, a_:b_], u[:, a_ + off:b_ + off], vcol("lcw", j * 4 + c), xc[:, a_:b_],
                                        ALU.mult, ALU.add, [u, vec, xc], [xc])
                            act(xcb[:], xc[:], AF.Copy, [xc], [xcb])
                            for d in range(2):
                                for (t0, n, w) in TBS:
                                    p = ps()
                                    mm(p, p[:, 0:n], bd[:, (d * 2 + 0) * 4 + c, :], xcb[:, t0:t0 + n], [bd, xcb])
                                    act(r_[:, t0:t0 + n], p[:, 0:n], AF.Sigmoid, [p, vec], [r_], bias=vcol("lrb", d * 4 + c))
                                    p = ps()
                                    mm(p, p[:, 0:n], bd[:, (d * 2 + 1) * 4 + c, :], xcb[:, t0:t0 + n], [bd, xcb])
                                    act(i_[:, t0:t0 + n], p[:, 0:n], AF.Sigmoid, [p, vec], [i_], bias=vcol("lib", d * 4 + c))
                                act(r_[:], r_[:], AF.Exp, [r_, cs], [r_], scale=cs[:, d * 4 + c:d * 4 + c + 1])
                                tt("vector", t_[:], r_[:], r_[:], ALU.mult, [r_], [t_])
                                act(t_[:], t_[:], AF.Sqrt, [t_, cst], [t_], bias=cst[:, 1:2], scale=-1.0)
                                tt("gpsimd", i_[:], i_[:], xc[:], ALU.mult, [i_, xc], [i_])
                                tt("vector", i_[:], i_[:], t_[:], ALU.mult, [i_, t_], [i_])
                                if d == 0:
                                    V(lambda e: e.tensor_tensor_scan(out=h0[:], data0=r_[:], data1=i_[:], initial=0.0,
                                                                     op0=ALU.mult, op1=ALU.add), [r_, i_], [h0])
                                else:
                                    V(lambda e: e.tensor_tensor_scan(out=u[:, 0:NCTX][:, ::-1], data0=r_[:, 0:NCTX][:, ::-1],
                                                                     data1=i_[:, 0:NCTX][:, ::-1], initial=0.0,
                                                                     op0=ALU.mult, op1=ALU.add), [r_, i_], [u])
                                    V(lambda e: e.tensor_tensor_scan(out=u[:, NCTX:T][:, ::-1], data0=r_[:, NCTX:T][:, ::-1],
                                                                     data1=i_[:, NCTX:T][:, ::-1], initial=u[:, 0:1],
                                                                     op0=ALU.mult, op1=ALU.add), [r_, i_, u], [u])
                            tt("gpsimd", h0[:], h0[:], u[:], ALU.add, [h0, u], [h0])
                            tt("vector", ya[:, c, :], h0[:], g[:], ALU.mult, [h0, g], [ya])
                        dma_out(ybr_d[0][:, 0:4 * T], ya[:].rearrange("p a t -> p (a t)"), ya, [ybr_b[0]])

                def mixer_attn():
                    with scope() as es:
                        cos = sb(es, "cos", [128, T], F32)
                        sin = sb(es, "sin", [128, T], F32)
                        dma_in(cos[:], rope_d[0], cos)
                        dma_in(sin[:], rope_d[1], sin)
                        am = sb(es, "am", [128, 2, 512], BF16)
                        dma_cast(am[:], amask_d, am)
                        es8 = sb(es, "es8", [128, 8], F32)
                        o, w = ROWV["sink"]
                        dma_in(es8[:], rowv_d[l, o:o + 8].partition_broadcast(128), es8)
                        act(es8[:], es8[:], AF.Exp, [es8], [es8])
                        QT = sb(es, "QT", [128, NT, 512], BF16)
                        KT = sb(es, "KT", [128, T], BF16)
                        Vt = sb(es, "Vt", [128, NT, 128], BF16)
                        yb = sb(es, "yb", [128, 4, T], BF16)
                        wq = [sb(es, "wq", [128, 8, 2, 128], BF16) for _ in range(2)]
                        t1s = [sb(es, "t1", [128, 512], F32) for _ in range(2)]
                        t2s = [sb(es, "t2", [128, 512], F32) for _ in range(2)]
                        cnt = 0
                        wqa = sb(es, "wqa", [128, 8, 2, 512], BF16)
                        load_win_cols(wqa, wqa[:, :, 0, :], C_Q, 512)
                        load_win_cols(wqa, wqa[:, :, 1, :], 0, 512, src=wqkp[l])

                        def ld_q(wt, r):
                            if r < 4:
                                for wh in range(2):
                                    for g2 in range(2):
                                        G(lambda e, wh=wh, g2=g2: e.tensor_copy(
                                            wt[:, :, wh, g2 * 64:(g2 + 1) * 64],
                                            wqa[:, :, wh, (g2 * 4 + r) * 64:(g2 * 4 + r + 1) * 64]), [wqa], [wt])
                            elif r == 4:
                                load_win_cols(wt, wt[:, :, 0, :], C_K, 128)
                                load_win_cols(wt, wt[:, :, 1, :], 512, 128, src=wqkp[l])
                            else:
                                load_win_cols(wt, wt[:, :, 0, :], C_V, 128)
                        wq_a = WQ(range(6), wq, ld_q)
                        for r in range(5):
                            wt = wq_a.get()
                            for (t0, n, w) in TBS:
                                t1 = t1s[cnt % 2]
                                t2 = t2s[cnt % 2]
                                cnt += 1
                                p = proj_fm(wt, lambda dc: wt[:, dc, 0, :], t0, n)
                                tt("vector", t1[:, 0:n], p[:, 0:n], cos[:, t0:t0 + n], ALU.mult, [p, cos], [t1])
                                p = proj_fm(wt, lambda dc: wt[:, dc, 1, :], t0, n)
                                tt("vector", t2[:, 0:n], p[:, 0:n], sin[:, t0:t0 + n], ALU.mult, [p, sin], [t2])
                                if r < 4:
                                    dst = QT[:, t0 // 128:(t0 + n) // 128, r * 128:(r + 1) * 128]
                                    tt("gpsimd", dst, t1[:, 0:n].rearrange("p (b q) -> p b q", q=128),
                                       t2[:, 0:n].rearrange("p (b q) -> p b q", q=128), ALU.add, [t1, t2], [QT])
                                else:
                                    tt("gpsimd", KT[:, t0:t0 + n], t1[:, 0:n], t2[:, 0:n], ALU.add, [t1, t2], [KT])
                        wv = wq_a.get()
                        for i in range(NT):
                            p = ps()
                            for dc in range(8):
                                mm(p, p[:, 0:128], hT[:, dc, i * 128:(i + 1) * 128], wv[:, dc, 0, :], [hT, wv],
                                   start=(dc == 0), stop=(dc == 7))
                            act(Vt[:, i, :], p[:, 0:128], AF.Copy, [p], [Vt])
                        Es = [sb(es, "E", [128, 512], BF16) for _ in range(3)]
                        den = sb(es, "den", [128, 512], F32)
                        ecnt = 0
                        items = []
                        for qb in range(NT):
                            keys = [(0, None), (1, None)]
                            if qb >= 2:
                                if qb - 1 >= 2:
                                    keys.append((qb - 1, 0))
                                keys.append((qb, None))
                                if qb + 1 < NT:
                                    keys.append((qb + 1, 1))
                            for g_ in range(2):
                                for ki, (kt, mk) in enumerate(keys):
                                    items.append((qb, g_, ki, kt, mk, len(keys)))
                        ecn = [0]

                        def stage1(it):
                            qb, g_, ki, kt, mk, nk = it
                            hs = slice(g_ * 64, (g_ + 1) * 64)
                            pS = ps("b")
                            mm(pS, pS[:, :], KT[hs, kt * 128:(kt + 1) * 128], QT[hs, qb, :], [KT, QT])
                            E = Es[ecn[0] % 3]
                            ecn[0] += 1
                            act(E[:], pS[:], AF.Exp, [pS], [E], scale=0.125)
                            if mk is not None:
                                tt("vector", E[:], E[:], am[:, mk, :], ALU.mult, [E, am], [E])
                            return E
                        v3 = lambda ap: ap.rearrange("p (r q) -> p r q", q=128)
                        Enext = stage1(items[0])
                        pv = p1 = None
                        for ii, it in enumerate(items):
                            E = Enext
                            if ii + 1 < len(items):
                                Enext = stage1(items[ii + 1])
                            qb, g_, ki, kt, mk, nk = it
                            hs = slice(g_ * 64, (g_ + 1) * 64)
                            if ki == 0:
                                pv = ps("a")
                                p1 = ps("a")
                            mm(pv, pv[:, :], Vt[:, kt, :], E[:], [Vt, E], start=(ki == 0), stop=(ki == nk - 1))
                            mm(p1, p1[:, :], onesb[:], E[:], [onesb, E], start=(ki == 0), stop=(ki == nk - 1))
                            if ki == nk - 1:
                                tt("vector", v3(den[hs, :]), v3(p1[hs, :]),
                                   es8[hs, g_ * 4:(g_ + 1) * 4].unsqueeze(2).to_broadcast([64, 4, 128]), ALU.add,
                                   [p1, es8], [den])
                                V(lambda e, hs=hs: e.reciprocal(den[hs, :], den[hs, :]), [den], [den])
                                tt("vector", yb[hs, :, qb * 128:(qb + 1) * 128], v3(pv[hs, :]), v3(den[hs, :]), ALU.mult,
                                   [pv, den], [yb])
                        dma_out(ybr_d[1][:, 0:4 * T], yb[:].rearrange("p a t -> p (a t)"), yb, [ybr_b[1]])

                def mixer_hgrn():
                    with scope() as es:
                        lbv = sb(es, "lbv", [128, 4], F32)
                        omlb = sb(es, "omlb", [128, 4], F32)
                        if l == 0:
                            V(lambda e: e.memset(lbv[:], 0.0), writes=[lbv])
                        else:
                            o0, _ = VEC["hl0"]
                            o1, _ = VEC["hl1"]
                            tt("vector", lbv[:], vec[:, o1:o1 + 4], vec[:, o0:o0 + 4], ALU.subtract, [vec], [lbv])
                            act(lbv[:], lbv[:], AF.Sigmoid, [lbv], [lbv])
                        ts("vector", omlb[:], lbv[:], -1.0, ALU.mult, [lbv], [omlb], s2=1.0, op1=ALU.add)
                        rm = sb(es, "rm", [128, T + 64], BF16)
                        V(lambda e: e.memset(rm[:], 1.0), writes=[rm])
                        V(lambda e: e.memset(rm[:].rearrange("p (c j) -> p c j", j=64)[:, :, 0:1], 0.0), [rm], [rm])
                        ws = [sb(es, "wh", [128, 8, 128], BF16) for _ in range(2)]
                        wi = [0]

                        hspecs = []
                        for h_ in range(4):
                            hspecs += [C_HQ + h_ * 128, C_HI + h_ * 128, C_HFF + h_ * 128, C_HFB + h_ * 128, C_HG + h_ * 128]
                        wq_h = WQ(hspecs, ws, lambda wt, c0: load_win_cols(wt, wt[:], c0, 128))

                        def wnext(c0):
                            assert wq_h.specs[wq_h.i] == c0
                            return wq_h.get()
                        qs = sb(es, "qs", [128, T], F32)
                        vtok = sb(es, "vtok", [64, NCH, 128], BF16)
                        oacc = sb(es, "oacc", [128, T], F32)
                        Fb = sb(es, "Fb", [128, T], F32)
                        Fb2 = sb(es, "Fb2", [128, T], F32)
                        trim = sb(es, "trim", [64, 2, 512], F32)
                        for d2 in range(2):
                            V(lambda e, d2=d2: e.tensor_copy(trim[:, d2, :].rearrange("p (c j) -> p c j", j=64),
                                                             tri[:, d2:d2 + 1, :].to_broadcast([64, 8, 64])), [tri], [trim])
                        trimu = trim[:].bitcast(mybir.dt.uint32)
                        KK = sb(es, "KK", [128, T], F32)
                        CUM = sb(es, "CUM", [128, T], F32)
                        DM = sb(es, "DM", [128, T], F32)
                        Ab = sb(es, "Ab", [128, T], BF16)
                        Bm = sb(es, "Bm", [128, T], BF16)
                        Aq = sb(es, "Aq", [128, T], BF16)
                        Bk = sb(es, "Bk", [128, T], BF16)
                        Bkt = sb(es, "Bkt", [64, NCH, 128], BF16)
                        ext = sb(es, "ext", [128, NCH], F32)
                        Sms = [sb(es, "Sm", [64, 512], BF16) for _ in range(2)]
                        st = sb(es, "st", [128, 128], F32)
                        stbh = [sb(es, "stbh", [128, 9, 128], BF16) for _ in range(2)]
                        hdel = sb(es, "hdel", [128, 128 * 8], F32)
                        hmul = sb(es, "hmul", [128, 128 * 8], F32)
                        hsc = sb(es, "hsc", [128, 128 * 8], F32)
                        hk = [0]
                        if dbg:
                            print("SBUF remaining in HGRN scope:", nc.sbuf_bytes_remaining, flush=True)
                        ych = sb(es, "ych", [128, T], BF16)
                        c3 = lambda b: b[:, 0:T].rearrange("p (c j) -> p c j", j=64)
                        smi = 0
                        for h in range(4):
                            wt = wnext(C_HQ + h * 128)
                            for (t0, n, w) in TBS:
                                p = proj_fm(wt, lambda dc: wt[:, dc, :], t0, n)
                                act(qs[:, t0:t0 + n], p[:, 0:n], AF.Silu, [p], [qs])
                            wt = wnext(C_HI + h * 128)
                            for c0 in range(0, NCH, 4):
                                p = ps()
                                for j in range(4):
                                    ch = c0 + j
                                    for dc in range(8):
                                        mm(p, p[0:64, j * 128:(j + 1) * 128], hT[:, dc, ch * 64:(ch + 1) * 64], wt[:, dc, :],
                                           [hT, wt], start=(dc == 0), stop=(dc == 7))
                                act(vtok[:, c0:c0 + 4, :], p[0:64, :].rearrange("p (c v) -> p c v", v=128), AF.Copy, [p], [vtok])
                            for d in range(2):
                                wt = wnext((C_HFF if d == 0 else C_HFB) + h * 128)
                                for (t0, n, w) in TBS:
                                    p = proj_fm(wt, lambda dc: wt[:, dc, :], t0, n)
                                    act(Fb[:, t0:t0 + n], p[:, 0:n], AF.Sigmoid, [p], [Fb])
                                ts("vector", Fb[:], Fb[:], omlb[:, h:h + 1], ALU.mult, [Fb, omlb, lbv], [Fb],
                                   s2=lbv[:, h:h + 1], op1=ALU.add)
                                ts("gpsimd", KK[:], Fb[:], -1.0, ALU.mult, [Fb], [KK], s2=1.0, op1=ALU.add)
                                act(Fb[:], Fb[:], AF.Ln, [Fb], [Fb])
                                if d == 0:
                                    V(lambda e: e.tensor_tensor_scan(out=CUM[:], data0=rm[:, 0:T], data1=Fb[:], initial=0.0,
                                                                     op0=ALU.mult, op1=ALU.add), [rm, Fb], [CUM])
                                    mid, lastj = 31, 63
                                else:
                                    V(lambda e: e.tensor_tensor_scan(out=CUM[:, ::-1], data0=rm[:, 1:T + 1][:, ::-1],
                                                                     data1=Fb[:, ::-1], initial=0.0,
                                                                     op0=ALU.mult, op1=ALU.add), [rm, Fb], [CUM])
                                    mid, lastj = 32, 0
                                cum3 = c3(CUM)
                                dump("lf%d" % d, Fb, Fb[:], [128, T])
                                dump("kk%d" % d, KK, KK[:], [128, T])
                                dump("cum%d" % d, CUM, CUM[:], [128, T])
                                tt("vector", c3(DM), cum3, cum3[:, :, mid:mid + 1].to_broadcast([128, NCH, 64]), ALU.subtract,
                                   [CUM], [DM])
                                act(Fb[:], DM[:], AF.Exp, [DM], [Fb])
                                tt("vector", Ab[:], qs[:], Fb[:], ALU.mult, [qs, Fb], [Ab])
                                act(Fb2[:], DM[:], AF.Exp, [DM], [Fb2], scale=-1.0)
                                tt("vector", Bm[:], KK[:], Fb2[:], ALU.mult, [KK, Fb2], [Bm])
                                act(Fb[:], CUM[:], AF.Exp, [CUM], [Fb])
                                tt("vector", Aq[:], qs[:], Fb[:], ALU.mult, [qs, Fb], [Aq])
                                tt("vector", c3(DM), cum3[:, :, lastj:lastj + 1].to_broadcast([128, NCH, 64]), cum3,
                                   ALU.subtract, [CUM], [DM])
                                act(Fb2[:], DM[:], AF.Exp, [DM], [Fb2])
                                tt("vector", Bk[:], KK[:], Fb2[:], ALU.mult, [KK, Fb2], [Bk])
                                act(ext[:], cum3[:, :, lastj], AF.Exp, [CUM], [ext])
                                dump("ext%d" % d, ext, ext[:], [128, NCH])
                                dump("Ab%d" % d, Ab, Ab[:], [128, T], BF16)
                                dump("Bm%d" % d, Bm, Bm[:], [128, T], BF16)
                                dump("Aq%d" % d, Aq, Aq[:], [128, T], BF16)
                                dump("Bk%d" % d, Bk, Bk[:], [128, T], BF16)
                                for c0 in range(0, NCH, 4):
                                    p = ps()
                                    pb = p[:].bitcast(BF16)
                                    for j in range(4):
                                        ch = c0 + j
                                        PE(lambda e, j=j, ch=ch: e.transpose(pb[0:64, j * 128:(j + 1) * 128],
                                                                            Bk[:, ch * 64:(ch + 1) * 64], identb[:]),
                                           [Bk, identb], [p])
                                    act(Bkt[:, c0:c0 + 4, :], pb[0:64, 0:512].rearrange("p (c k) -> p c k", k=128), AF.Copy,
                                        [p], [Bkt])
                                V(lambda e: e.memset(st[:], 0.0), writes=[st])
                                V(lambda e, b_=stbh[hk[0] % 2]: e.memset(b_[:, 0, :], 0.0), writes=[stbh[hk[0] % 2]])
                                for Sm_ in Sms:
                                    V(lambda e, Sm_=Sm_: e.memset(Sm_[:], 0.0), writes=[Sm_])
                                triu = tri[:].bitcast(mybir.dt.uint32)
                                if d == 0:
                                    groups = [list(range(0, 4))] + [list(range(c, c + 8)) for c in range(4, NCH, 8)]
                                else:
                                    groups = [[3, 2, 1, 0]] + [list(range(c + 7, c - 1, -1)) for c in range(NCH - 8, 3, -8)]
                                for grp in groups:
                                    nj = len(grp)
                                    lo, hi = min(grp), max(grp)
                                    rev = grp[0] > grp[-1]
                                    stb = stbh[hk[0] % 2]
                                    stb_n = stbh[(hk[0] + 1) % 2]
                                    hk[0] += 1
                                    dl_v = hdel[:, 0:128 * nj].rearrange("p (c j) -> p c j", j=nj)
                                    ml_v = hmul[:, 0:128 * nj].rearrange("p (c j) -> p c j", j=nj)
                                    sc_v = hsc[:, 0:128 * nj].rearrange("p (c j) -> p c j", j=nj)
                                    for j0 in range(0, nj, 4):
                                        pD = ps("b")
                                        for jj in range(4):
                                            ch = grp[j0 + jj]
                                            mm(pD, pD[:, jj * 128:(jj + 1) * 128], Bkt[:, ch, :], vtok[:, ch, :], [Bkt, vtok])
                                        act(dl_v[:, :, j0:j0 + 4].rearrange("p c j -> p j c"),
                                            pD[:, 0:512].rearrange("p (j c) -> p j c", j=4), AF.Copy, [pD], [hdel])
                                    ext_g = ext[:, lo:hi + 1]
                                    if rev:
                                        ext_g = ext_g[:, ::-1]
                                    V(lambda e, ml_v=ml_v, ext_g=ext_g, nj=nj: e.tensor_copy(
                                        ml_v, ext_g.unsqueeze(1).to_broadcast([128, 128, nj])), [ext], [hmul])
                                    V(lambda e, ml_v=ml_v: e.memset(ml_v[:, :, 0:1], 0.0), [hmul], [hmul])
                                    stt(dl_v[:, :, 0], st[:], ext[:, grp[0]:grp[0] + 1], dl_v[:, :, 0], ALU.mult, ALU.add,
                                        [st, ext, hdel], [hdel])
                                    V(lambda e, nj=nj: e.tensor_tensor_scan(out=hsc[:, 0:128 * nj], data0=hmul[:, 0:128 * nj],
                                                                           data1=hdel[:, 0:128 * nj], initial=0.0,
                                                                           op0=ALU.mult, op1=ALU.add), [hmul, hdel], [hsc])
                                    act(stb[:, 1:nj + 1, :], sc_v.rearrange("p c j -> p j c"), AF.Copy, [hsc], [stb])
                                    V(lambda e, sc_v=sc_v, nj=nj: e.tensor_copy(st[:], sc_v[:, :, nj - 1]), [hsc], [st])
                                    act(stb_n[:, 0, :], sc_v[:, :, nj - 1], AF.Copy, [hsc], [stb_n])
                                    pS = ps("a")
                                    for j, ch in enumerate(grp):
                                        mm(pS, pS[0:64, j * 64:(j + 1) * 64], Bm[:, ch * 64:(ch + 1) * 64],
                                           Ab[:, ch * 64:(ch + 1) * 64], [Bm, Ab])
                                    Sm = Sms[smi % 2]
                                    smi += 1
                                    V(lambda e, Sm=Sm, pS=pS, nj=nj: e.copy_predicated(
                                        Sm[:, 0:nj * 64], trimu[:, d, 0:nj * 64], pS[0:64, 0:nj * 64]), [pS, trim, Sm], [Sm])
                                    pO = ps("a")
                                    for j, ch in enumerate(grp):
                                        mm(pO, pO[:, j * 64:(j + 1) * 64], vtok[:, ch, :], Sm[:, j * 64:(j + 1) * 64],
                                           [vtok, Sm], start=True, stop=False)
                                        mm(pO, pO[:, j * 64:(j + 1) * 64], stb[:, j, :], Aq[:, ch * 64:(ch + 1) * 64],
                                           [stb, Aq], start=False, stop=True)
                                    oview = c3(oacc)[:, lo:hi + 1, :]
                                    if rev:
                                        oview = oview[:, ::-1, :]
                                    pov = pO[:, 0:nj * 64].rearrange("p (c j) -> p c j", j=64)
                                    if d == 0:
                                        act(oview, pov, AF.Copy, [pO], [oacc])
                                    else:
                                        tt("vector", oview, oview, pov, ALU.add, [oacc, pO], [oacc])
                            dump("oacc", oacc, oacc[:], [128, T])
                            dump("qs", qs, qs[:], [128, T])
                            dump("vtok", vtok, vtok[:], [64, NCH, 128], BF16)
                            dump("Bkt", Bkt, Bkt[:], [64, NCH, 128], BF16)
                            wt = wnext(C_HG + h * 128)
                            for (t0, n, w) in TBS:
                                p = proj_fm(wt, lambda dc: wt[:, dc, :], t0, n)
                                act(Fb[:, t0:t0 + n], p[:, 0:n], AF.Silu, [p], [Fb])
                                rstd = norm_block(es, oacc, oacc[:, t0:t0 + n].unsqueeze(1), n, nparts=128, nsub=1, denom=128.0)
                                stt(DM[:, t0:t0 + n], oacc[:, t0:t0 + n], vcol("hng", h), rstd[:, 0:n], ALU.mult, ALU.mult,
                                    [oacc, vec, rstd], [DM])
                                tt("vector", ych[:, t0:t0 + n], DM[:, t0:t0 + n], Fb[:, t0:t0 + n], ALU.mult, [DM, Fb], [ych])
                            dma_out(ybr_d[2][:, h * T:(h + 1) * T], ych[:], ych, [ybr_b[2]], par=True)

                def mixer_ssd():
                    with scope() as es:
                        v64 = sb(es, "v64", [64, NV64], F32)
                        dma_in(v64[:], vecs64_d[l], v64)
                        rowb = sb(es, "rowb", [128, NROW], F32)
                        dma_in(rowb[:], rowv_d[l].partition_broadcast(128), rowb)
                        aneg = sb(es, "aneg", [64, 16], F32)
                        o, w = ROWV["alog"]
                        act(aneg[:], rowb[0:64, o:o + 16], AF.Exp, [rowb], [aneg])
                        ts("vector", aneg[:], aneg[:], -1.0, ALU.mult, [aneg], [aneg])
                        odt, _ = ROWV["dtb"]
                        osk, _ = ROWV["skip"]
                        dtt = sb(es, "dtt", [64, 2, NCH, 8], F32)
                        dta = sb(es, "dta", [64, 2, NCH, 8], F32)
                        cumt = sb(es, "cumt", [64, 2, NCH, 8], F32)
                        wdt = sb(es, "wdt", [128, 8, 128], BF16)
                        load_win_cols(wdt, wdt[:], C_DT + 16 - 128, 128)
                        for c0 in range(0, NCH, 12):
                            p = ps()
                            for j in range(12):
                                ch = c0 + j
                                for dc in range(8):
                                    mm(p, p[0:64, j * 16:(j + 1) * 16], hT[:, dc, ch * 64:(ch + 1) * 64], wdt[:, dc, 112:128],
                                       [hT, wdt], start=(dc == 0), stop=(dc == 7))
                            for d in range(2):
                                tt("vector", dtt[:, d, c0:c0 + 12, :],
                                   p[0:64, 0:192].rearrange("p (c d h) -> p d c h", d=2, h=8)[:, d],
                                   rowb[0:64, odt + d * 8:odt + d * 8 + 8].unsqueeze(1).to_broadcast([64, 12, 8]), ALU.add,
                                   [p, rowb], [dtt])
                        ts("vector", dtt[:], dtt[:], 30.0, ALU.min, [dtt], [dtt])
                        act(dtt[:], dtt[:], AF.Exp, [dtt], [dtt])
                        act(dtt[:], dtt[:], AF.Ln, [dtt, cst], [dtt], bias=cst[0:64, 1:2])
                        for d in range(2):
                            tt("vector", dta[:, d], dtt[:, d], aneg[:, d * 8:(d + 1) * 8].unsqueeze(1).to_broadcast([64, NCH, 8]),
                               ALU.mult, [dtt, aneg], [dta])
                            p = ps()
                            mm(p, p[0:64, 0:NCH * 8], tri[:, d, :], dta[:, d].rearrange("p c h -> p (c h)"), [tri, dta])
                            act(cumt[:, d].rearrange("p c h -> p (c h)"), p[0:64, 0:NCH * 8], AF.Copy, [p], [cumt])
                        ws = [sb(es, "wsd", [128, 8, 128], BF16) for _ in range(2)]
                        wi = [0]

                        sspecs = []
                        for g2 in range(2):
                            sspecs += [(C_XS + (g2 * 4 + r2) * 64, 128) for r2 in (0, 2)]
                            sspecs += [(C_B + g2 * 128, 128), (C_C + g2 * 128, 128)]
                            sspecs += [(C_Z + (g2 * 4 + r2) * 64, 128) for r2 in (0, 2)]
                        wq_s = WQ(sspecs, ws, lambda wt, sp: load_win_cols(wt, wt[:, :, 0:sp[1]], sp[0], sp[1]))

                        def wnext(c0, ncols):
                            assert wq_s.specs[wq_s.i] == (c0, ncols)
                            return wq_s.get()
                        yacc = sb(es, "yacc", [64, 4, T], F32)
                        xtok = sb(es, "xtok", [64, NCH, 256], BF16)
                        BT = sb(es, "BT", [128, T], BF16)
                        CTb = sb(es, "CTb", [128, T], BF16)
                        Btok = sb(es, "Btok", [64, NCH, 128], BF16)
                        CBs = sb(es, "CBs", [64, NCH, 64], BF16)
                        ocw, _ = V64["cwx"]
                        ocb, _ = V64["cbx"]
                        ong, _ = V64["ng"]
                        for g_ in range(2):
                            with scope() as e2:
                                u = sb(e2, "u", [128, T], F32)
                                xc = sb(e2, "xc", [128, T], F32)
                                xsb = sb(e2, "xsb", [64, T], BF16)

                                def conv_silu(np_, cw_fn, cb_ap, dst_ap, dst_buf, rd):
                                    act(xc[0:np_, :], u[0:np_, :], AF.Identity, [u] + rd, [xc], bias=cb_ap, scale=cw_fn(2))
                                    for (s0, s1) in ((0, NCTX), (NCTX, T)):
                                        for j in (0, 1, 3):
                                            off = j - 2
                                            a_ = max(s0, s0 - off)
                                            b_ = min(s1, s1 - off)
                                            stt(xc[0:np_, a_:b_], u[0:np_, a_ + off:b_ + off], cw_fn(j), xc[0:np_, a_:b_],
                                                ALU.mult, ALU.add, [u, xc] + rd, [xc])
                                    act(dst_ap, xc[0:np_, :], AF.Silu, [xc], [dst_buf])
                                for r in range(4):
                                    h = g_ * 4 + r
                                    if r % 2 == 0:
                                        wtx = wnext(C_XS + h * 64, 128)
                                    wt = wtx
                                    cof = (r % 2) * 64
                                    for (t0, n, w) in TBS:
                                        p = proj_fm(wt, lambda dc: wt[:, dc, cof:cof + 64], t0, n, M=64)
                                        act(u[0:64, t0:t0 + n], p[0:64, 0:n], AF.Copy, [p], [u])
                                    conv_silu(64, lambda j: v64[:, ocw + j * 8 + h:ocw + j * 8 + h + 1],
                                              v64[:, ocb + h:ocb + h + 1], xc[0:64, :], xc, [v64])
                                    G(lambda e: e.tensor_copy(xsb[:], xc[0:64, :]), [xc], [xsb])
                                    ts("vector", yacc[:, r, :], xc[0:64, :], rowb[0:64, osk + h:osk + h + 1], ALU.mult,
                                       [xc, rowb], [yacc])
                                    for c0 in range(0, NCH, 12):
                                        p = ps()
                                        pb = p[:].bitcast(BF16)
                                        for j in range(12):
                                            ch = c0 + j
                                            PE(lambda e, j=j, ch=ch, pb=pb: e.transpose(pb[0:64, j * 64:(j + 1) * 64],
                                                                                       xsb[:, ch * 64:(ch + 1) * 64], identb[0:64, 0:64]),
                                               [xsb, identb], [p])
                                        act(xtok[:, c0:c0 + 12, r * 64:(r + 1) * 64],
                                            pb[0:64, 0:768].rearrange("p (c k) -> p c k", k=64), AF.Copy, [p], [xtok])
                                for which, dstb in ((0, BT), (1, CTb)):
                                    wt = wnext((C_B if which == 0 else C_C) + g_ * 128, 128)
                                    for (t0, n, w) in TBS:
                                        p = proj_fm(wt, lambda dc: wt[:, dc, :], t0, n)
                                        act(u[:, t0:t0 + n], p[:, 0:n], AF.Copy, [p], [u])
                                    cidx = which * 2 + g_
                                    conv_silu(128, lambda j: vcol("scw", j * 4 + cidx), vcol("scb", cidx), dstb[:], dstb, [vec])
                                for c0 in range(0, NCH, 4):
                                    p = ps()
                                    pb = p[:].bitcast(BF16)
                                    for j in range(4):
                                        ch = c0 + j
                                        PE(lambda e, j=j, ch=ch, pb=pb: e.transpose(pb[0:64, j * 128:(j + 1) * 128],
                                                                                   BT[:, ch * 64:(ch + 1) * 64], identb[:]),
                                           [BT, identb], [p])
                                    act(Btok[:, c0:c0 + 4, :], pb[0:64, 0:512].rearrange("p (c k) -> p c k", k=128), AF.Copy,
                                        [p], [Btok])
                                for c0 in range(0, NCH, 8):
                                    nj = min(8, NCH - c0)
                                    p = ps()
                                    for j in range(nj):
                                        ch = c0 + j
                                        mm(p, p[0:64, j * 64:(j + 1) * 64], BT[:, ch * 64:(ch + 1) * 64],
                                           CTb[:, ch * 64:(ch + 1) * 64], [BT, CTb])
                                    act(CBs[:, c0:c0 + nj, :], p[0:64, 0:nj * 64].rearrange("p (c t) -> p c t", t=64), AF.Copy,
                                        [p], [CBs])
                            with scope() as e2:
                                BS = []
                                TS = []
                                for d_ in range(2):
                                    BS.append(dict(
                                        st4=sb(e2, "st4", [128, 4, 64], F32), stb4=sb(e2, "stb4", [128, 4, 64], BF16)))
                                    TS.append([dict(
                                        prep=sb(e2, "prep", [64, 4, 64], F32), Dd=sb(e2, "Dd", [64, 4, 64], F32),
                                        mdt=sb(e2, "mdt", [64, 4, 64], F32), Mb=sb(e2, "Mb", [64, 4, 64], BF16),
                                        Ec=sb(e2, "Ec", [128, 4, 64], F32), Cs=sb(e2, "Cs", [128, 4, 64], BF16),
                                        wv=sb(e2, "wv", [64, 4], F32), xw=sb(e2, "xw", [64, 4, 64], BF16),
                                        pcs=sb(e2, "pcs", [128, 256], F32)) for _ in range(2)])
                                    V(lambda e, b_=BS[d_]["st4"]: e.memset(b_[:], 0.0), writes=[BS[d_]["st4"]])
                                    V(lambda e, b_=BS[d_]["stb4"]: e.memset(b_[:], 0.0), writes=[BS[d_]["stb4"]])
                                hsl = slice(g_ * 4, g_ * 4 + 4)
                                orders = [list(range(NCH)), [3, 2, 1, 0] + list(range(NCH - 1, 3, -1))]
                                pdk = {}

                                def stepA(d, ch, par_):
                                    T_ = TS[d][par_]
                                    prep, Dd, mdt = T_["prep"], T_["Dd"], T_["mdt"]
                                    Mb, Ec, Cs, wv, xw = T_["Mb"], T_["Ec"], T_["Cs"], T_["wv"], T_["xw"]
                                    lastj = 63 if d == 0 else 0
                                    trib = tri[:, d:d + 1, :].to_broadcast([64, 4, 64])
                                    tsl = slice(ch * 64, (ch + 1) * 64)
                                    tt("vector", prep[:], dta[:, d, ch, hsl].unsqueeze(2).to_broadcast([64, 4, 64]), trib,
                                       ALU.mult, [dta, tri], [prep])
                                    pc = psb[d]
                                    mm(pc, pc[:, 0:256], ones32[0:64, :], prep[:].rearrange("p r t -> p (r t)"), [ones32, prep])
                                    pc_ = T_["pcs"]
                                    V(lambda e, pc_=pc_, pc=pc: e.tensor_copy(pc_[:], pc[:, 0:256]), [pc], [pc_])
                                    pc3 = pc_[:].rearrange("p (r t) -> p r t", t=64)
                                    pc = pc_
                                    tt("vector", Dd[:], pc3[0:64], cumt[:, d, ch, hsl].unsqueeze(2).to_broadcast([64, 4, 64]),
                                       ALU.subtract, [pc, cumt], [Dd])
                                    tt("vector", wv[:], pc3[0:64, :, lastj], cumt[:, d, ch, hsl], ALU.subtract, [pc, cumt], [wv])
                                    act(Ec[:], pc3, AF.Exp, [pc], [Ec])
                                    ts("vector", Dd[:], Dd[:], 0.0, ALU.min, [Dd], [Dd])
                                    act(Dd[:], Dd[:], AF.Exp, [Dd], [Dd])
                                    act(wv[:], wv[:], AF.Exp, [wv], [wv])
                                    tt("gpsimd", mdt[:], dtt[:, d, ch, hsl].unsqueeze(2).to_broadcast([64, 4, 64]), trib,
                                       ALU.mult, [dtt, tri], [mdt])
                                    tt("gpsimd", Cs[:], Ec[:], CTb[:, tsl].unsqueeze(1).to_broadcast([128, 4, 64]), ALU.mult,
                                       [Ec, CTb], [Cs])
                                    tt("vector", Dd[:], Dd[:], mdt[:], ALU.mult, [Dd, mdt], [Dd])
                                    tt("vector", Mb[:], Dd[:], CBs[:, ch:ch + 1, :].to_broadcast([64, 4, 64]), ALU.mult,
                                       [Dd, CBs], [Mb])
                                    tt("vector", wv[:], wv[:], dtt[:, d, ch, hsl], ALU.mult, [wv, dtt], [wv])
                                    tt("gpsimd", xw[:], xtok[:, ch, :].rearrange("p (r k) -> p r k", k=64),
                                       wv[:].unsqueeze(2).to_broadcast([64, 4, 64]), ALU.mult, [xtok, wv], [xw])
                                    pd = psb[2 + d * 2 + par_]
                                    mm(pd, pd[:, 0:256], Btok[:, ch, :], xw[:].rearrange("p r k -> p (r k)"), [Btok, xw])
                                    pdk[(d, par_)] = pd

                                def stepB(d, ch, par_):
                                    B_ = BS[d]
                                    T_ = TS[d][par_]
                                    st4, stb4 = B_["st4"], B_["stb4"]
                                    Mb, Ec, Cs = T_["Mb"], T_["Ec"], T_["Cs"]
                                    lastj = 63 if d == 0 else 0
                                    tsl = slice(ch * 64, (ch + 1) * 64)
                                    pd = pdk[(d, par_)]
                                    po = psb[6 + d]
                                    for r in range(4):
                                        mm(po, po[0:64, r * 64:(r + 1) * 64], xtok[:, ch, r * 64:(r + 1) * 64], Mb[:, r, :],
                                           [xtok, Mb], start=True, stop=False)
                                        mm(po, po[0:64, r * 64:(r + 1) * 64], stb4[:, r, :], Cs[:, r, :], [stb4, Cs],
                                           start=False, stop=True)
                                    tt("vector", st4[:], st4[:], Ec[:, :, lastj:lastj + 1].to_broadcast([128, 4, 64]), ALU.mult,
                                       [st4, Ec], [st4])
                                    tt("vector", st4[:], st4[:], pd[:, 0:256].rearrange("p (r k) -> p r k", k=64), ALU.add,
                                       [st4, pd], [st4])
                                    act(stb4[:], st4[:], AF.Copy, [st4], [stb4])
                                    tt("vector", yacc[:, :, tsl], yacc[:, :, tsl],
                                       po[0:64, 0:256].rearrange("p (r t) -> p r t", t=64), ALU.add, [yacc, po], [yacc])
                                if _os.environ.get("SWP", "1") == "1":
                                    for d_ in range(2):
                                        stepA(d_, orders[d_][0], 0)
                                    for k_ in range(NCH):
                                        if k_ + 1 < NCH:
                                            for d_ in range(2):
                                                stepA(d_, orders[d_][k_ + 1], (k_ + 1) % 2)
                                        for d_ in range(2):
                                            stepB(d_, orders[d_][k_], k_ % 2)
                                else:
                                    for k_ in range(NCH):
                                        for d_ in range(2):
                                            stepA(d_, orders[d_][k_], k_ % 2)
                                            stepB(d_, orders[d_][k_], k_ % 2)
                            with scope() as e2:
                                zs = sb(e2, "zs", [64, 512], F32)
                                ydb = [sb(e2, "ydb", [64, 4, 512], BF16) for _ in range(2)]
                                for r in range(4):
                                    h = g_ * 4 + r
                                    if r % 2 == 0:
                                        wtz = wnext(C_Z + h * 64, 128)
                                    wt = wtz
                                    cof = (r % 2) * 64
                                    for (t0, n, w) in TBS:
                                        p = proj_fm(wt, lambda dc: wt[:, dc, cof:cof + 64], t0, n, M=64)
                                        act(zs[:, 0:n], p[0:64, 0:n], AF.Silu, [p], [zs])
                                        tt("vector", yacc[:, r, t0:t0 + n], yacc[:, r, t0:t0 + n], zs[:, 0:n], ALU.mult,
                                           [yacc, zs], [yacc])
                                for bi, (t0, n, w) in enumerate(TBS):
                                    rstd = norm_block(e2, yacc, yacc[:, :, t0:t0 + n], n, nparts=64, nsub=4, denom=256.0)
                                    yb_ = ydb[bi % 2]
                                    for r in range(4):
                                        h = g_ * 4 + r
                                        stt(yb_[:, r, 0:n], yacc[:, r, t0:t0 + n], v64[:, ong + h:ong + h + 1], rstd[0:64, 0:n],
                                            ALU.mult, ALU.mult, [yacc, v64, rstd], [yb_])
                                    dma_out(ybr_d[3][0:64, g_ * 4 * T:(g_ + 1) * 4 * T].rearrange("p (r t) -> p r t", t=T)[:, :, t0:t0 + n],
                                            yb_[:, :, 0:n], yb_, [ybr_b[3]], par=True)

                def run_mixers():
                    mixer_lru()
                    if stop_after == "lru":
                        return
                    mixer_attn()
                    if stop_after == "attn":
                        return
                    mixer_hgrn()
                    if stop_after == "hgrn":
                        return
                    mixer_ssd()

                if stop_after != "h":
                    run_mixers()
                if s == 0 and l == 0 and nlayers > 1:
                    convert_issue(1)

                if stop_after is None:
                    PARTS = [TBS[0:2], TBS[2:4], TBS[4:5]]
                    with scope() as es:
                        ya = sb(es, "ya", [128, 4, 1024], BF16)
                        yb = sb(es, "yb", [128, 4, 1024], BF16)
                        yc = sb(es, "yc", [128, 4, 1024], BF16)
                        yd = sb(es, "yd", [64, 8, 1024], BF16)
                        wgs = [sb(es, "wg", [128, 8, 4, 128], BF16) for _ in range(2)]
                        wbs = [sb(es, "wb", [128, 3, 4, 128], BF16) for _ in range(2)]
                        wds = [sb(es, "wd", [64, 8, 128], BF16) for _ in range(2)]
                        sgs = [sb(es, "sg", [128, 512], F32) for _ in range(2)]
                        accs = [sb(es, "acc", [128, 512], F32) for _ in range(2)]
                        tmps = [sb(es, "tmpm", [128, 512], F32) for _ in range(2)]
                        mbl = [sb(es, "mbl", [128, 512], BF16) for _ in range(2)]
                        kcnt = [0]
                        wkc = [0]
                        for part in PARTS:
                            pt0 = part[0][0]
                            pn = sum(b_[1] for b_ in part)
                            for bi_, ysb in enumerate((ya, yb, yc)):
                                dma_in(ysb[:, :, 0:pn], ybr_d[bi_][:, 0:4 * T].rearrange("p (a t) -> p a t", t=T)[:, :, pt0:pt0 + pn],
                                       ysb, [ybr_b[bi_]])
                            dma_in(yd[:, :, 0:pn], ybr_d[3][0:64, :].rearrange("p (a t) -> p a t", t=T)[:, :, pt0:pt0 + pn],
                                   yd, [ybr_b[3]])
                            def ld_m(oc):
                                k_ = wkc[0]
                                wkc[0] += 1
                                wg = wgs[k_ % 2]
                                wb = wbs[k_ % 2]
                                wd = wds[k_ % 2]
                                ocs = slice(oc * 128, (oc + 1) * 128)
                                for i in range(4):
                                    load_win_cols(wg, wg[:, :, i, :], C_MG + i * D + oc * 128, 128)
                                dma_in(wb[:, 0], wb_branch[l, 0][:, ocs].rearrange("(kc p) c -> p kc c", p=128), wb, cvb[("branch", l)], par=True)
                                for g_ in range(2):
                                    dma_in(wb[g_ * 64:(g_ + 1) * 64, 1],
                                           wb_branch[l, 1][:, ocs].rearrange("(g r d) c -> g d r c", g=2, r=4)[g_], wb,
                                           cvb[("branch", l)], par=True)
                                dma_in(wb[:, 2], wb_branch[l, 2][:, ocs].rearrange("(kc p) c -> p kc c", p=128), wb, cvb[("branch", l)], par=True)
                                dma_in(wd[:], wb_branch[l, 3][:, ocs].rearrange("(h p) c -> p h c", p=64), wd, cvb[("branch", l)], par=True)
                                return (wg, wb, wd)

                            def cp_m(oc, wts, part=part, pt0=pt0):
                                wg, wb, wd = wts
                                kk_ = [0]
                                for (t0, n, w) in part:
                                    acc = accs[kcnt[0] % 2]
                                    mb_ = mbl[kcnt[0] % 2]
                                    kcnt[0] += 1
                                    lt = t0 - pt0
                                    for i in range(4):
                                        sg = sgs[i % 2]
                                        pg = proj_fm(wg, lambda dc: wg[:, dc, i, :], t0, n)
                                        act(sg[:, 0:n], pg[:, 0:n], AF.Sigmoid, [pg], [sg])
                                        pp = ps()
                                        if i < 3:
                                            ysrc = (ya, yb, yc)[i]
                                            for kc in range(4):
                                                mm(pp, pp[:, 0:n], wb[:, i, kc, :], ysrc[:, kc, lt:lt + n], [wb, ysrc],
                                                   start=(kc == 0), stop=(kc == 3))
                                        else:
                                            for h in range(8):
                                                mm(pp, pp[:, 0:n], wd[:, h, :], yd[:, h, lt:lt + n], [wd, yd],
                                                   start=(h == 0), stop=(h == 7))
                                        if i == 0:
                                            tt("vector", acc[:, 0:n], sg[:, 0:n], pp[:, 0:n], ALU.mult, [sg, pp], [acc])
                                        else:
                                            tmp = tmps[i % 2]
                                            tt("vector", tmp[:, 0:n], sg[:, 0:n], pp[:, 0:n], ALU.mult, [sg, pp], [tmp])
                                            if i < 3:
                                                tt("gpsimd", acc[:, 0:n], acc[:, 0:n], tmp[:, 0:n], ALU.add, [acc, tmp], [acc])
                                            else:
                                                tt("gpsimd", mb_[:, 0:n], acc[:, 0:n], tmp[:, 0:n], ALU.add, [acc, tmp], [mb_])
                                    dma_out(mrg_d[:, oc * T + t0:oc * T + t0 + n], mb_[:, 0:n], mb_, [mrg_b], par=True)
                            pipe(list(range(8)), ld_m, cp_m)
                    with scope() as es:
                        wo = sb(es, "wo", [128, 8, D], BF16)
                        dma_in(wo[:], wrows(wb_out[l]), wo, cvb[("out", l)])
                        ms = [sb(es, "m", [128, 8, 512], F32) for _ in range(2)]
                        mgs = [sb(es, "mg", [128, 8, 512], BF16) for _ in range(2)]
                        xts = [sb(es, "xt", [128, 8, 512], F32) for _ in range(1)]
                        tmps = [sb(es, "tmp", [128, 512], F32) for _ in range(2)]
                        def ld_b(it):
                            bi, (t0, n, w) = it
                            mg = mgs[bi % 2]
                            dma_in(mg[:, :, 0:n], mrg_d.rearrange("p (a t) -> p a t", t=T)[:, :, t0:t0 + n], mg, [mrg_b])
                            return mg

                        def cp_b(it, mg):
                            bi, (t0, n, w) = it
                            m = ms[bi % 2]
                            xt = xts[0]
                            tmp = tmps[bi % 2]
                            for oc2 in range(8):
                                p = ps()
                                for oc in range(8):
                                    mm(p, p[:, 0:n], wo[:, oc, oc2 * 128:(oc2 + 1) * 128], mg[:, oc, 0:n],
                                       [wo, mg], start=(oc == 0), stop=(oc == 7))
                                act(m[:, oc2, 0:n], p[:, 0:n], AF.Copy, [p], [m])
                            rstd = norm_block(es, m, m[:, :, 0:n], n)
                            dma_in(xt[:, :, 0:n], wrows(src_res)[:, :, t0:t0 + n], xt, src_bufs(t0, n))
                            for dc in range(8):
                                stt(tmp[:, 0:n], m[:, dc, 0:n], GS[:, 2, dc, w:w + 1], rstd[:, 0:n], ALU.mult, ALU.mult,
                                    [m, GS, rstd], [tmp])
                                tt("gpsimd", xt[:, dc, 0:n], xt[:, dc, 0:n], tmp[:, 0:n], ALU.add, [xt, tmp], [xt])
                            dma_out(wrows(res_d[s])[:, :, t0:t0 + n], xt[:, :, 0:n], xt, gran(s, t0, n))
                            rstd2 = norm_block(es, xt, xt[:, :, 0:n], n)
                            modulate_to(hT, xt, n, t0, w, 3, 4, rstd2, tmp)
                        pipe(list(enumerate(TBS)), ld_b, cp_b)
                    with scope() as es:
                        aT = sb(es, "aT", [128, 32, 768], BF16)
                        mo = sb(es, "mo", [128, 8, 768], F32)
                        wus = [sb(es, "wu", [128, 8, 256], BF16) for _ in range(2)]
                        wdn = [sb(es, "wdn", [128, 32, 128], BF16) for _ in range(2)]
                        rl = [sb(es, "rl", [128, 512], F32) for _ in range(2)]
                        xts = [sb(es, "xt", [128, 8, 512], F32) for _ in range(1)]
                        tmps = [sb(es, "tmp", [128, 512], F32) for _ in range(2)]
                        kq = [0]
                        pre_u = [None]
                        for si_, sup in enumerate(MLP_SUP):
                            base = sup[0][0]
                            def ld_u(hp):
                                wu = wus[hp % 2]
                                dma_in(wu[:], wrows(wb_up[l])[:, :, hp * 256:(hp + 1) * 256], wu, cvb[("up", l)])
                                return wu

                            def cp_u(hp, wu, sup=sup, base=base):
                                for j2 in range(2):
                                    ht = hp * 2 + j2
                                    for (t0, n, w) in sup:
                                        p = proj_fm(wu, lambda dc: wu[:, dc, j2 * 128:(j2 + 1) * 128], t0, n)
                                        r_ = rl[kq[0] % 2]
                                        kq[0] += 1
                                        act(r_[:, 0:n], p[:, 0:n], AF.Relu, [p], [r_])
                                        tt("vector", aT[:, ht, t0 - base:t0 - base + n], r_[:, 0:n], r_[:, 0:n], ALU.mult, [r_], [aT])
                            pre_d = None
                            pipe(list(range(16)), ld_u, cp_u, first=pre_u[0])
                            pre_u[0] = None

                            def ld_d(oc):
                                wd_ = wdn[oc % 2]
                                dma_in(wd_[:], wb_down[l][:, oc * 128:(oc + 1) * 128].rearrange("(ht p) c -> p ht c", p=128), wd_, cvb[("down", l)])
                                return wd_

                            def cp_d(oc, wd_, sup=sup, base=base):
                                for (t0, n, w) in sup:
                                    p = ps()
                                    for ht in range(32):
                                        mm(p, p[:, 0:n], wd_[:, ht, :], aT[:, ht, t0 - base:t0 - base + n], [wd_, aT],
                                           start=(ht == 0), stop=(ht == 31))
                                    act(mo[:, oc, t0 - base:t0 - base + n], p[:, 0:n], AF.Copy, [p], [mo])
                            pipe(list(range(8)), ld_d, cp_d)
                            if si_ + 1 < len(MLP_SUP):
                                pre_u[0] = ld_u(0)
                            for bi, (t0, n, w) in enumerate(sup):
                                xt = xts[0]
                                tmp = tmps[bi % 2]
                                mos = mo[:, :, t0 - base:t0 - base + n]
                                rstd = norm_block(es, mo, mos, n)
                                dma_in(xt[:, :, 0:n], wrows(res_d[s])[:, :, t0:t0 + n], xt, gran(s, t0, n))
                                for dc in range(8):
                                    stt(tmp[:, 0:n], mo[:, dc, t0 - base:t0 - base + n], GS[:, 5, dc, w:w + 1], rstd[:, 0:n],
                                        ALU.mult, ALU.mult, [mo, GS, rstd], [tmp])
                                    tt("gpsimd", xt[:, dc, 0:n], xt[:, dc, 0:n], tmp[:, 0:n], ALU.add, [xt, tmp], [xt])
                                if not last:
                                    dma_out(wrows(res_d[s])[:, :, t0:t0 + n], xt[:, :, 0:n], xt, gran(s, t0, n))
                                elif t0 >= NCTX:
                                    ev = dma_out(wrows(out_d[s])[:, :, t0 - NCTX:t0 - NCTX + n], xt[:, :, 0:n], xt, [out_b], par=True)
                                    out_events.append(ev)

    S.barrier()
    S.wait_events("sync", out_events)
    top.close()
    S.close()
    return nc, S


def _host_inputs(inputs):
    f = np.float32
    x = np.asarray(inputs["x"], f)
    ctx = np.asarray(inputs["ctx"], f)
    c = np.asarray(inputs["c"], f)
    c_ctx = np.asarray(inputs["c_ctx"], f)
    B = x.shape[0]

    def pj(v, p=128):
        v = np.asarray(v, f)
        return np.ascontiguousarray(v.reshape(-1, p).T)

    shared = {}
    for k_ in ("w_ada", "w_in", "w_branch", "w_out", "w_mlp_up", "w_mlp_down"):
        shared[k_] = np.ascontiguousarray(np.asarray(inputs[k_], f))
    w_in = shared["w_in"]
    qk = np.concatenate([w_in[:, :, C_Q:C_Q + 512], w_in[:, :, C_K:C_K + 128]], axis=2)
    perm = np.arange(640).reshape(10, 2, 2, 16)[:, :, ::-1, :].reshape(640)
    shared["wqkp"] = np.ascontiguousarray(qk[:, :, perm])
    vecs = np.zeros((DEPTH, 128, NV), f)
    vecs64 = np.zeros((DEPTH, 64, NV64), f)
    rowv = np.zeros((DEPTH, NROW), f)
    lru_bd = np.zeros((DEPTH, 128, 16, 128), f)
    for l in range(DEPTH):
        def put(name, arr):
            o, w = VEC[name]
            assert arr.shape == (128, w), (name, arr.shape)
            vecs[l][:, o:o + w] = arr
        put("bada", pj(inputs["b_ada"][l]))
        put("gpre", pj(inputs["g_pre_mix"][l]))
        put("gpostmix", pj(inputs["g_post_mix"][l]))
        put("gpremlp", pj(inputs["g_pre_mlp"][l]))
        put("gpostmlp", pj(inputs["g_post_mlp"][l]))
        put("lcw", np.concatenate([pj(inputs["lru_conv_w"][l][j]) for j in range(4)], axis=1))
        put("lcb", pj(inputs["lru_conv_b"][l]))
        put("lrb", np.concatenate([pj(inputs["lru_rec_b"][l][d]) for d in range(2)], axis=1))
        put("lib", np.concatenate([pj(inputs["lru_inp_b"][l][d]) for d in range(2)], axis=1))
        put("llam", np.concatenate([pj(inputs["lru_lambda"][l][d]) for d in range(2)], axis=1))
        put("hl0", pj(inputs["hgrn_lb_logits"][0]))
        put("hl1", pj(inputs["hgrn_lb_logits"][l]))
        put("hng", pj(inputs["hgrn_norm_g"][l]))
        scw = np.asarray(inputs["ssd_conv_w"][l], f)
        scb = np.asarray(inputs["ssd_conv_b"][l], f)
        put("scw", np.concatenate([pj(scw[j, 512:1024]) for j in range(4)], axis=1))
        put("scb", pj(scb[512:1024]))

        def put64(name, arr):
            o, w = V64[name]
            assert arr.shape == (64, w), (name, arr.shape)
            vecs64[l][:, o:o + w] = arr
        put64("cwx", np.concatenate([pj(scw[j, 0:512], 64) for j in range(4)], axis=1))
        put64("cbx", pj(scb[0:512], 64))
        put64("ng", pj(inputs["ssd_norm_g"][l], 64))
        rowv[l, 0:8] = inputs["attn_sink"][l]
        rowv[l, 8:24] = np.asarray(inputs["ssd_dt_bias"][l], f).reshape(16)
        rowv[l, 24:40] = np.asarray(inputs["ssd_a_log"][l], f).reshape(16)
        rowv[l, 40:48] = inputs["ssd_skip"][l]
        for d in range(2):
            for gate, nm_ in enumerate(("lru_rec_w", "lru_inp_w")):
                wm = np.asarray(inputs[nm_][l][d], f)
                for c_ in range(4):
                    for hb in range(2):
                        lru_bd[l, hb * 64:(hb + 1) * 64, (d * 2 + gate) * 4 + c_, hb * 64:(hb + 1) * 64] = wm[2 * c_ + hb]
    shared.update(vecs=vecs, vecs64=vecs64, rowv=rowv, lru_bd=lru_bd)
    shared["ident"] = np.eye(128, dtype=f)
    quarter = 16
    inv_freq = (10000.0 ** (-np.arange(quarter, dtype=np.float64) / quarter))
    t = np.arange(NLAT)
    rows_, cols_ = t // 64, t % 64
    cos = np.ones((64, T), np.float64)
    sin = np.zeros((64, T), np.float64)
    for half, pos in ((0, rows_), (1, cols_)):
        ang = pos[None, :] * inv_freq[:, None]
        cc, ss = np.cos(ang.astype(np.float32)), np.sin(ang.astype(np.float32))
        b0 = half * 32
        cos[b0:b0 + 16, NCTX:] = cc
        cos[b0 + 16:b0 + 32, NCTX:] = cc
        sin[b0:b0 + 16, NCTX:] = -ss
        sin[b0 + 16:b0 + 32, NCTX:] = ss
    rope = np.stack([np.concatenate([cos, cos], 0), np.concatenate([sin, sin], 0)]).astype(f)
    shared["rope_cs"] = np.ascontiguousarray(rope)
    j = np.arange(128)[:, None]
    i = np.arange(128)[None, :]
    am = np.stack([np.tile((j >= i).astype(f), (1, 4)), np.tile((j <= i).astype(f), (1, 4))], axis=1)
    shared["amask"] = np.ascontiguousarray(am)
    a = np.arange(64)[:, None]
    b = np.arange(64)[None, :]
    shared["tri64"] = np.ascontiguousarray(np.stack([(a <= b).astype(f), (a >= b).astype(f)], axis=1))
    in_maps = []
    for core in range(NCORES):
        bs = [core * SPC + k_ for k_ in range(SPC)]
        xin = np.stack([np.concatenate([ctx[b_].T, x[b_].T], axis=1) for b_ in bs]).astype(f)
        cond = np.stack([pj(c[bs[0]]), pj(c[bs[1]]), pj(c_ctx)], axis=2)
        m = dict(shared)
        m["xin"] = np.ascontiguousarray(xin)
        m["cond"] = np.ascontiguousarray(cond)
        in_maps.append(m)
    return in_maps


_CACHE = {}


def kernel(**inputs):
    if "nc" not in _CACHE:
        _CACHE["nc"] = build()[0]
    nc = _CACHE["nc"]
    in_maps = _host_inputs(inputs)
    res = run_bass_kernel_spmd(nc, in_maps, core_ids=list(range(NCORES)))
    outs = []
    for core in range(NCORES):
        o = np.asarray(res.results[core]["out"])
        for k_ in range(SPC):
            outs.append(o[k_].T)
    return np.ascontiguousarray(np.stack(outs).astype(np.float32))
```
